# Optimizing a Trainium2 kernel written in Bass

```python
import math
import jax, jax.numpy as jnp
from jax import lax
import numpy as np

D_MODEL = 1024
BATCH = 8
SEQ = 2048
DEPTH = 2
DEC_BATCH = 128
DEC_SEQ = 1
PAST_LEN = 16384
PAGE_SIZE = 128

A_HEADS = 4
A_DH = 64
A_WIDTH = A_HEADS * A_DH
S5_P = 16
S5_N = 64
S5_WIDTH = D_MODEL // 2
S5_GROUPS = S5_WIDTH // S5_P
C_HEADS = 4
C_DH = 64
C_WIDTH = C_HEADS * C_DH
CONV_K = 4
MIX_WIDTH = A_WIDTH + S5_WIDTH + C_WIDTH
MEM_TOKENS = 256
X_HEADS = 4
X_DH = D_MODEL // X_HEADS
D_FF = (((8 * D_MODEL + 2) // 3 + 255) // 256) * 256
CHUNK = 64
EPS = 1e-6
IN_SIZES = [A_WIDTH, A_WIDTH, A_WIDTH, A_WIDTH, A_HEADS, A_HEADS,
            S5_WIDTH,
            3 * C_WIDTH, C_WIDTH, C_HEADS, C_HEADS]
IN_WIDTH = sum(IN_SIZES)
IN_SPLITS = [int(s) for s in np.cumsum(IN_SIZES)[:-1]]

kernel_name = 'hymba_mlstm_s5_gdn_step'


def rmsnorm(x, g):
    x32 = x.astype(jnp.float32)
    y = x32 * lax.rsqrt(jnp.mean(x32 * x32, axis=-1, keepdims=True) + EPS)
    return (y * g.astype(jnp.float32)).astype(x.dtype)


def l2norm(x):
    return x * lax.rsqrt(jnp.sum(x * x, axis=-1, keepdims=True) + EPS)


def to_chunks(x, L):
    B, S, H = x.shape[:3]
    x = x.reshape((B, S // L, L, H) + x.shape[3:])
    return jnp.moveaxis(x, (1, 2), (0, 3))


def from_chunks(y):
    y = jnp.moveaxis(y, (0, 3), (1, 2))
    return y.reshape((y.shape[0], y.shape[1] * y.shape[2]) + y.shape[3:])


def mlstm_chunked(q, k, v, i_pre, f_pre, c0, n0, m0):
    S = q.shape[1]
    L = math.gcd(S, CHUNK)
    tril = jnp.tril(jnp.ones((L, L), dtype=bool))
    logf = jax.nn.log_sigmoid(f_pre)
    xs = (to_chunks(q, L), to_chunks(k, L), to_chunks(v, L), to_chunks(i_pre, L), to_chunks(logf, L))

    def step(carry, inp):
        c, n, m = carry
        qc, kc, vc, ic, lfc = inp
        b = jnp.cumsum(lfc, axis=-1)
        d_intra = jnp.where(tril, b[..., :, None] - b[..., None, :] + ic[..., None, :], -jnp.inf)
        d_inter = b + m[..., None]
        m_t = jnp.maximum(jnp.max(d_intra, axis=-1), d_inter)
        w_intra = jnp.exp(d_intra - m_t[..., None])
        w_inter = jnp.exp(d_inter - m_t)
        s = jnp.einsum('bhtd,bhjd->bhtj', qc, kc) * w_intra
        num = w_inter[..., None] * jnp.einsum('bhtd,bhde->bhte', qc, c) + jnp.einsum('bhtj,bhje->bhte', s, vc)
        den = w_inter * jnp.einsum('bhtd,bhd->bht', qc, n) + jnp.sum(s, axis=-1)
        h = num / jnp.maximum(jnp.abs(den), jnp.exp(-m_t))[..., None]
        b_last = b[..., -1]
        g = b_last[..., None] - b + ic
        m_new = jnp.maximum(b_last + m, jnp.max(g, axis=-1))
        w_k = jnp.exp(g - m_new[..., None])
        decay = jnp.exp(b_last + m - m_new)
        c_new = decay[..., None, None] * c + jnp.einsum('bhj,bhjd,bhje->bhde', w_k, kc, vc)
        n_new = decay[..., None] * n + jnp.einsum('bhj,bhjd->bhd', w_k, kc)
        return (c_new, n_new, m_new), h

    (c1, n1, m1), h = lax.scan(step, (c0, n0, m0), xs)
    return from_chunks(h), c1, n1, m1


def s5_scan(u, h0_re, h0_im, a_re, a_im, log_dt, b_re, b_im, c_re, c_im, d):
    f32 = jnp.float32
    a_re, a_im, b_re, b_im = a_re.astype(f32), a_im.astype(f32), b_re.astype(f32), b_im.astype(f32)
    dt = jnp.exp(log_dt.astype(f32))[:, None]
    mag = jnp.exp(a_re * dt)
    lam_re, lam_im = mag * jnp.cos(a_im * dt), mag * jnp.sin(a_im * dt)
    nr, ni = lam_re - 1.0, lam_im
    den = a_re * a_re + a_im * a_im
    coef_re = (nr * a_re + ni * a_im) / den
    coef_im = (ni * a_re - nr * a_im) / den
    bb_re = coef_re[..., None] * b_re - coef_im[..., None] * b_im
    bb_im = coef_re[..., None] * b_im + coef_im[..., None] * b_re
    bu_re = jnp.einsum('bsgp,gnp->bsgn', u, bb_re)
    bu_im = jnp.einsum('bsgp,gnp->bsgn', u, bb_im)
    ar = jnp.broadcast_to(lam_re, bu_re.shape)
    ai = jnp.broadcast_to(lam_im, bu_re.shape)

    def combine(e1, e2):
        a1r, a1i, b1r, b1i = e1
        a2r, a2i, b2r, b2i = e2
        return (a2r * a1r - a2i * a1i, a2r * a1i + a2i * a1r,
                a2r * b1r - a2i * b1i + b2r, a2r * b1i + a2i * b1r + b2i)

    Ar, Ai, Hr, Hi = lax.associative_scan(combine, (ar, ai, bu_re, bu_im), axis=1)
    h_re = Ar * h0_re[:, None] - Ai * h0_im[:, None] + Hr
    h_im = Ar * h0_im[:, None] + Ai * h0_re[:, None] + Hi
    y = (jnp.einsum('bsgn,gpn->bsgp', h_re, c_re.astype(f32))
         - jnp.einsum('bsgn,gpn->bsgp', h_im, c_im.astype(f32))
         + d.astype(f32) * u)
    return y, h_re[:, -1], h_im[:, -1]


def causal_conv(x, buf, w):
    S = x.shape[1]
    xp = jnp.concatenate([buf, x], axis=1)
    y = sum(w[j] * xp[:, j:j + S] for j in range(CONV_K))
    return y, xp[:, -(CONV_K - 1):]


def gdn_chunked(q, k, v, beta, log_alpha, S0):
    S = q.shape[1]
    L = math.gcd(S, CHUNK)
    tril = jnp.tril(jnp.ones((L, L), dtype=bool))
    strict = jnp.tril(jnp.ones((L, L), dtype=bool), -1)
    eye = jnp.eye(L, dtype=jnp.float32)
    xs = (to_chunks(q, L), to_chunks(k, L), to_chunks(v, L), to_chunks(beta, L), to_chunks(log_alpha, L))

    def step(st, inp):
        qc, kc, vc, bc, lac = inp
        g = jnp.cumsum(lac, axis=-1)
        diff = g[..., :, None] - g[..., None, :]
        dec_strict = jnp.where(strict, jnp.exp(jnp.where(strict, diff, 0.0)), 0.0)
        dec_incl = jnp.where(tril, jnp.exp(jnp.where(tril, diff, 0.0)), 0.0)
        eg = jnp.exp(g)
        A = dec_strict * jnp.einsum('bhtd,bhjd->bhtj', kc, kc) * bc[..., None, :]
        rhs = vc - eg[..., None] * jnp.einsum('bhtd,bhde->bhte', kc, st)
        U = lax.linalg.triangular_solve(eye + A, rhs, left_side=True, lower=True, unit_diagonal=True)
        qk = jnp.einsum('bhtd,bhjd->bhtj', qc, kc) * dec_incl * bc[..., None, :]
        o = eg[..., None] * jnp.einsum('bhtd,bhde->bhte', qc, st) + jnp.einsum('bhtj,bhje->bhte', qk, U)
        g_last = g[..., -1]
        w_j = jnp.exp(g_last[..., None] - g) * bc
        st_new = jnp.exp(g_last)[..., None, None] * st + jnp.einsum('bhj,bhjd,bhje->bhde', w_j, kc, U)
        return st_new, o

    S1, o = lax.scan(step, S0, xs)
    return from_chunks(o), S1


def zero_state(B):
    f = jnp.float32
    return (jnp.zeros((B, A_HEADS, A_DH, A_DH), f), jnp.zeros((B, A_HEADS, A_DH), f), jnp.zeros((B, A_HEADS), f),
            jnp.zeros((B, S5_GROUPS, S5_N), f), jnp.zeros((B, S5_GROUPS, S5_N), f),
            jnp.zeros((B, C_HEADS, C_DH, C_DH), f), jnp.zeros((B, CONV_K - 1, 3 * C_WIDTH), f))


def mixer_block(x, state, lp):
    f32 = jnp.float32
    c0, n0, m0, s5r0, s5i0, S0, buf0 = [s.astype(f32) for s in state]
    B, S, _ = x.shape
    z = (rmsnorm(x, lp['norm_mix']) @ lp['w_in']).astype(f32)
    a_q, a_k, a_v, a_o, a_i, a_f, s_u, c_qkv, c_g, c_b, c_a = jnp.split(z, IN_SPLITS, axis=-1)
    q = a_q.reshape(B, S, A_HEADS, A_DH) * A_DH ** -0.5
    k = a_k.reshape(B, S, A_HEADS, A_DH)
    v = a_v.reshape(B, S, A_HEADS, A_DH)
    ha, c1, n1, m1 = mlstm_chunked(q, k, v, a_i + lp['mlstm_b_i'], a_f + lp['mlstm_b_f'], c0, n0, m0)
    ha = rmsnorm(ha, lp['mlstm_norm'].reshape(A_HEADS, A_DH)).reshape(B, S, A_WIDTH) * jax.nn.sigmoid(a_o)
    ys, s5r1, s5i1 = s5_scan(s_u.reshape(B, S, S5_GROUPS, S5_P), s5r0, s5i0, lp['s5_a_re'], lp['s5_a_im'],
                             lp['s5_log_dt'], lp['s5_b_re'], lp['s5_b_im'], lp['s5_c_re'], lp['s5_c_im'], lp['s5_d'])
    gy = jax.nn.gelu(ys.reshape(B, S, S5_WIDTH))
    ys = gy * jax.nn.sigmoid(gy @ lp['s5_w_glu'].astype(f32))
    qkv, buf1 = causal_conv(c_qkv, buf0, lp['gdn_conv_w'].astype(f32))
    gq, gk, gv = jnp.split(jax.nn.silu(qkv), 3, axis=-1)
    gq = l2norm(gq.reshape(B, S, C_HEADS, C_DH)) * C_DH ** -0.5
    gk = l2norm(gk.reshape(B, S, C_HEADS, C_DH))
    gv = gv.reshape(B, S, C_HEADS, C_DH)
    beta = jax.nn.sigmoid(c_b)
    log_alpha = -jnp.exp(lp['gdn_a_log'].astype(f32)) * jax.nn.softplus(c_a + lp['gdn_dt_bias'])
    hc, S1 = gdn_chunked(gq, gk, gv, beta, log_alpha, S0)
    hc = (rmsnorm(hc, lp['gdn_norm']) * jax.nn.silu(c_g.reshape(B, S, C_HEADS, C_DH))).reshape(B, S, C_WIDTH)
    mix = jnp.concatenate([ha, ys, hc], axis=-1).astype(x.dtype)
    return x + mix @ lp['w_out'], (c1, n1, m1, s5r1, s5i1, S1, buf1)


def mem_kv(mem, lp):
    B, M, _ = mem.shape
    mn = rmsnorm(mem, lp['norm_mem'])
    return ((mn @ lp['w_mk']).reshape(B, M, X_HEADS, X_DH), (mn @ lp['w_mv']).reshape(B, M, X_HEADS, X_DH))


def cross_attn(x, mk, mv, lp):
    B, S, _ = x.shape
    q = (rmsnorm(x, lp['norm_xattn']) @ lp['w_mq']).reshape(B, S, X_HEADS, X_DH)
    s = jnp.einsum('bshd,bmhd->bhsm', q, mk).astype(jnp.float32) * X_DH ** -0.5
    p = jax.nn.softmax(s, axis=-1).astype(mv.dtype)
    o = jnp.einsum('bhsm,bmhd->bshd', p, mv).reshape(B, S, D_MODEL)
    return x + o @ lp['w_mo']


def ffn(x, lp):
    h = rmsnorm(x, lp['norm_ffn'])
    return x + (jax.nn.silu(h @ lp['w_gate']) * (h @ lp['w_up'])) @ lp['w_down']


def setup_inputs(seed: int = 0) -> dict:
    key = jax.random.key(seed)
    ks = jax.random.split(key, 64)
    it = iter(range(64))
    f32 = jnp.float32

    def nrm(shape, scale=1.0):
        return scale * jax.random.normal(ks[next(it)], shape, f32)

    def unif(shape, lo, hi):
        return jax.random.uniform(ks[next(it)], shape, f32, lo, hi)

    L = DEPTH
    inp = {}
    inp['x_prompt'] = nrm((BATCH, SEQ, D_MODEL))
    inp['x_sample'] = nrm((DEC_BATCH, DEC_SEQ, D_MODEL))
    inp['mem_prompt'] = nrm((BATCH, MEM_TOKENS, D_MODEL))
    inp['cache_mem_k'] = nrm((L, DEC_BATCH, MEM_TOKENS, X_HEADS, X_DH))
    inp['cache_mem_v'] = nrm((L, DEC_BATCH, MEM_TOKENS, X_HEADS, X_DH))
    inp['state_mlstm_c'] = nrm((L, DEC_BATCH, A_HEADS, A_DH, A_DH), 0.5)
    inp['state_mlstm_n'] = nrm((L, DEC_BATCH, A_HEADS, A_DH))
    inp['state_mlstm_m'] = nrm((L, DEC_BATCH, A_HEADS))
    inp['state_s5_re'] = nrm((L, DEC_BATCH, S5_GROUPS, S5_N), 0.5)
    inp['state_s5_im'] = nrm((L, DEC_BATCH, S5_GROUPS, S5_N), 0.5)
    inp['state_gdn'] = nrm((L, DEC_BATCH, C_HEADS, C_DH, C_DH), 0.3)
    inp['state_gdn_conv'] = nrm((L, DEC_BATCH, CONV_K - 1, 3 * C_WIDTH))
    inp['norm_mix'] = 1.0 + nrm((L, D_MODEL), 0.02)
    inp['w_in'] = nrm((L, D_MODEL, IN_WIDTH), D_MODEL ** -0.5)
    inp['w_out'] = nrm((L, MIX_WIDTH, D_MODEL), MIX_WIDTH ** -0.5)
    inp['mlstm_b_i'] = nrm((L, A_HEADS), 0.1)
    inp['mlstm_b_f'] = jnp.linspace(3.0, 6.0, A_HEADS, dtype=f32)[None, :] + nrm((L, A_HEADS), 0.1)
    inp['mlstm_norm'] = 1.0 + nrm((L, A_WIDTH), 0.02)
    n_idx = jnp.arange(S5_N, dtype=f32)
    inp['s5_a_re'] = -0.5 + nrm((L, S5_GROUPS, S5_N), 0.01)
    inp['s5_a_im'] = math.pi * n_idx + nrm((L, S5_GROUPS, S5_N), 0.01)
    inp['s5_log_dt'] = unif((L, S5_GROUPS), math.log(0.001), math.log(0.1))
    inp['s5_b_re'] = nrm((L, S5_GROUPS, S5_N, S5_P), S5_P ** -0.5)
    inp['s5_b_im'] = nrm((L, S5_GROUPS, S5_N, S5_P), S5_P ** -0.5)
    inp['s5_c_re'] = nrm((L, S5_GROUPS, S5_P, S5_N), S5_N ** -0.5)
    inp['s5_c_im'] = nrm((L, S5_GROUPS, S5_P, S5_N), S5_N ** -0.5)
    inp['s5_d'] = nrm((L, S5_GROUPS, S5_P))
    inp['s5_w_glu'] = nrm((L, S5_WIDTH, S5_WIDTH), S5_WIDTH ** -0.5)
    inp['gdn_conv_w'] = nrm((L, CONV_K, 3 * C_WIDTH), CONV_K ** -0.5)
    inp['gdn_a_log'] = jnp.log(unif((L, C_HEADS), 1.0, 16.0))
    dt = jnp.exp(unif((L, C_HEADS), math.log(0.001), math.log(0.1)))
    inp['gdn_dt_bias'] = dt + jnp.log(-jnp.expm1(-dt))
    inp['gdn_norm'] = 1.0 + nrm((L, C_DH), 0.02)
    inp['norm_xattn'] = 1.0 + nrm((L, D_MODEL), 0.02)
    inp['norm_mem'] = 1.0 + nrm((L, D_MODEL), 0.02)
    inp['w_mq'] = nrm((L, D_MODEL, D_MODEL), D_MODEL ** -0.5)
    inp['w_mk'] = nrm((L, D_MODEL, D_MODEL), D_MODEL ** -0.5)
    inp['w_mv'] = nrm((L, D_MODEL, D_MODEL), D_MODEL ** -0.5)
    inp['w_mo'] = nrm((L, D_MODEL, D_MODEL), D_MODEL ** -0.5)
    inp['norm_ffn'] = 1.0 + nrm((L, D_MODEL), 0.02)
    inp['w_gate'] = nrm((L, D_MODEL, D_FF), D_MODEL ** -0.5)
    inp['w_up'] = nrm((L, D_MODEL, D_FF), D_MODEL ** -0.5)
    inp['w_down'] = nrm((L, D_FF, D_MODEL), D_FF ** -0.5)
    inp['norm_final'] = 1.0 + nrm((D_MODEL,), 0.02)
    return inp


def reference(x_prompt, x_sample, mem_prompt, cache_mem_k, cache_mem_v, state_mlstm_c, state_mlstm_n,
              state_mlstm_m, state_s5_re, state_s5_im, state_gdn, state_gdn_conv,
              norm_mix, w_in, w_out, mlstm_b_i, mlstm_b_f, mlstm_norm,
              s5_a_re, s5_a_im, s5_log_dt, s5_b_re, s5_b_im, s5_c_re, s5_c_im, s5_d, s5_w_glu,
              gdn_conv_w, gdn_a_log, gdn_dt_bias, gdn_norm,
              norm_xattn, norm_mem, w_mq, w_mk, w_mv, w_mo,
              norm_ffn, w_gate, w_up, w_down, norm_final):
    xp, xs = x_prompt, x_sample
    zero_p = zero_state(x_prompt.shape[0])
    mem_k_list, mem_v_list, st_p_list, st_s_list = [], [], [], []
    for l in range(DEPTH):
        lp = {'norm_mix': norm_mix[l], 'w_in': w_in[l], 'w_out': w_out[l],
              'mlstm_b_i': mlstm_b_i[l], 'mlstm_b_f': mlstm_b_f[l], 'mlstm_norm': mlstm_norm[l],
              's5_a_re': s5_a_re[l], 's5_a_im': s5_a_im[l], 's5_log_dt': s5_log_dt[l],
              's5_b_re': s5_b_re[l], 's5_b_im': s5_b_im[l], 's5_c_re': s5_c_re[l], 's5_c_im': s5_c_im[l],
              's5_d': s5_d[l], 's5_w_glu': s5_w_glu[l],
              'gdn_conv_w': gdn_conv_w[l], 'gdn_a_log': gdn_a_log[l], 'gdn_dt_bias': gdn_dt_bias[l],
              'gdn_norm': gdn_norm[l],
              'norm_xattn': norm_xattn[l], 'norm_mem': norm_mem[l], 'w_mq': w_mq[l], 'w_mk': w_mk[l],
              'w_mv': w_mv[l], 'w_mo': w_mo[l],
              'norm_ffn': norm_ffn[l], 'w_gate': w_gate[l], 'w_up': w_up[l], 'w_down': w_down[l]}
        xp, st_p = mixer_block(xp, zero_p, lp)
        mk_p, mv_p = mem_kv(mem_prompt, lp)
        xp = cross_attn(xp, mk_p, mv_p, lp)
        xp = ffn(xp, lp)
        st_in = (state_mlstm_c[l], state_mlstm_n[l], state_mlstm_m[l], state_s5_re[l], state_s5_im[l],
                 state_gdn[l], state_gdn_conv[l])
        xs, st_s = mixer_block(xs, st_in, lp)
        xs = cross_attn(xs, cache_mem_k[l], cache_mem_v[l], lp)
        xs = ffn(xs, lp)
        mem_k_list.append(mk_p)
        mem_v_list.append(mv_p)
        st_p_list.append(st_p)
        st_s_list.append(st_s)
    y_prompt = rmsnorm(xp, norm_final)
    y_sample = rmsnorm(xs, norm_final)
    mem_k_prompt = jnp.stack(mem_k_list)
    mem_v_prompt = jnp.stack(mem_v_list)
    mlstm_c_p, mlstm_n_p, mlstm_m_p, s5_re_p, s5_im_p, gdn_p, gdn_conv_p = [
        jnp.stack([st[i] for st in st_p_list]) for i in range(7)]
    mlstm_c_s, mlstm_n_s, mlstm_m_s, s5_re_s, s5_im_s, gdn_s, gdn_conv_s = [
        jnp.stack([st[i] for st in st_s_list]) for i in range(7)]
    return (y_prompt, y_sample, mem_k_prompt, mem_v_prompt,
            mlstm_c_p, mlstm_n_p, mlstm_m_p, s5_re_p, s5_im_p, gdn_p, gdn_conv_p,
            mlstm_c_s, mlstm_n_s, mlstm_m_s, s5_re_s, s5_im_s, gdn_s, gdn_conv_s)
```

```python
import contextlib
import numpy as np
import concourse.bass as bass
import concourse.mybir as mybir
from concourse.bass_utils import run_bass_kernel_spmd
import math

F32 = mybir.dt.float32
BF16 = mybir.dt.bfloat16
I32 = mybir.dt.int32
ALU = mybir.AluOpType
AF = mybir.ActivationFunctionType
AX = mybir.AxisListType


class Tl:
    def __init__(self, K, t, name):
        self.K = K
        self.t = t
        self.name = name
        self.w = None
        self.r = []
        self.dsem = None
        self.dcnt = 0

    def __getitem__(self, k):
        return V(self, self.t[k])

    def ap(self):
        return self.t.ap() if hasattr(self.t, "ap") else self.t[:]


class V:
    def __init__(self, tl, ap):
        self.tl = tl
        self.ap = ap

    def __getitem__(self, k):
        return V(self.tl, self.ap[k])

    def re(self, s, **kw):
        return V(self.tl, self.ap.rearrange(s, **kw))

    def bc(self, shape):
        return V(self.tl, self.ap.to_broadcast(shape))

    def bct(self, shape):
        return V(self.tl, self.ap.broadcast_to(shape))

    def un(self, axis):
        return V(self.tl, self.ap.unsqueeze(axis))

    def bitcast(self, dt):
        return V(self.tl, self.ap.bitcast(dt))


class Eng:
    def __init__(self, K, name, e):
        self.K = K
        self.name = name
        self.e = e
        self.sem = K.es.enter_context(K.nc.semaphore("s_" + name))
        self.n = 0
        self.known = {}
        self.prog = []


class KB:
    def __init__(self, nc):
        self.nc = nc
        self.es = contextlib.ExitStack()
        self.uid = 0
        self.nwaits = 0
        self.sem_cnt = {}
        self.lazy_sems = set()

    def start(self):
        self.pe = Eng(self, "pe", self.nc.tensor)
        self.dve = Eng(self, "dve", self.nc.vector)
        self.act = Eng(self, "act", self.nc.scalar)
        self.pool = Eng(self, "pool", self.nc.gpsimd)
        self.sp = Eng(self, "sp", self.nc.sync)
        self.engs = [self.pe, self.dve, self.act, self.pool, self.sp]

    def sb(self, shape, dt=F32, name=None):
        self.uid += 1
        name = (name or "t") + "_%d" % self.uid
        t = self.es.enter_context(self.nc.sbuf_tensor(name, list(shape), dt))
        return Tl(self, t, name)

    def ps(self, shape, dt=F32, name=None):
        self.uid += 1
        name = (name or "p") + "_%d" % self.uid
        t = self.es.enter_context(self.nc.psum_tensor(name, list(shape), dt))
        tl = Tl(self, t, name)
        tl.is_psum = True
        return tl

    def _wait(self, eng, tok):
        sem, val, src = tok
        k = id(sem)
        if val == -1:
            if eng.known.get(k, -1) == 'all':
                return
            eng.prog.append(('w', sem, None))
            self.nwaits += 1
            eng.known[k] = 'all'
            return
        if eng.known.get(k, -1) == 'all' or eng.known.get(k, -1) >= val:
            return
        if src is eng.name:
            if eng.name == "pe" or val < eng.n - 2:
                return
        eng.prog.append(('w', sem, val))
        self.nwaits += 1
        eng.known[k] = val

    def _tok_of(self, tl, tok):
        sem, val, src = tok
        if src == "dma":
            if id(sem) in self.lazy_sems:
                return (sem, -1, src)
            return (sem, self.sem_cnt[id(sem)] * 16, src)
        return tok

    def deps(self, eng, reads, writes, raw_only_same=True):
        for v in reads:
            tl = v.tl if isinstance(v, V) else v
            if tl.w is not None:
                self._wait(eng, self._tok_of(tl, tl.w))
            if getattr(tl, "is_psum", False):
                for tok in tl.r:
                    if not (tok[2] is eng.name):
                        self._wait(eng, tok)
        for v in writes:
            tl = v.tl if isinstance(v, V) else v
            if tl.w is not None:
                tok = self._tok_of(tl, tl.w)
                if not (tok[2] is eng.name):
                    self._wait(eng, tok)
            for tok in tl.r:
                tok = self._tok_of(tl, tok)
                if not (tok[2] is eng.name):
                    self._wait(eng, tok)

    def done(self, tok, reads, writes):
        for v in reads:
            tl = v.tl if isinstance(v, V) else v
            tl.r.append(tok)
            if len(tl.r) > 24:
                best = {}
                for t_ in tl.r:
                    kk = (id(t_[0]))
                    if kk not in best or best[kk][1] < t_[1]:
                        best[kk] = t_
                tl.r = list(best.values())
        for v in writes:
            tl = v.tl if isinstance(v, V) else v
            tl.w = tok
            tl.r = []

    def op(self, eng, fn, reads, writes):
        reads = [r for r in reads if isinstance(r, (V, Tl))]
        self.deps(eng, reads, writes)
        eng.n += 1
        eng.prog.append(('i', fn, eng.sem, 1))
        tok = (eng.sem, eng.n, eng.name)
        self.done(tok, reads, writes)
        if getattr(self, "switch_hook", None) is not None:
            self.switch_hook()
        return tok

    def dma(self, eng, out, in_, **kw):
        reads = [in_] if isinstance(in_, V) else []
        writes = [out] if isinstance(out, V) else []
        self.deps(eng, reads, writes)
        tl = (writes + reads)[0].tl if (writes + reads) else None
        o = out.ap if isinstance(out, V) else out
        i = in_.ap if isinstance(in_, V) else in_
        fn = (lambda e: e.dma_start(out=o, in_=i, **kw))
        return fn, reads, writes

    def dma_tile(self, eng, out, in_, **kw):
        fn, reads, writes = self.dma(eng, out, in_, **kw)
        tl = (writes + reads)[0].tl
        if tl.dsem is None:
            if getattr(self, "cur_group", None) is not None:
                tl.dsem = self.cur_group
            else:
                self.nsem = getattr(self, 'nsem', 0) + 1
                try:
                    tl.dsem = self.es.enter_context(self.nc.semaphore("d_" + tl.name))
                except KeyError:
                    print('OUT OF SEMAPHORES at', self.nsem, tl.name); raise
        eng.prog.append(('i', fn, tl.dsem, 16))
        c_ = self.sem_cnt.get(id(tl.dsem), 0) + 1
        self.sem_cnt[id(tl.dsem)] = c_
        tl.dcnt = c_
        tok = (tl.dsem, c_ * 16, "dma")
        self.done(tok, reads, writes)
        return tok

    @staticmethod
    def _a(x):
        return x.ap if isinstance(x, V) else x

    def mm(self, out, lhsT, rhs, start=True, stop=True):
        a = self._a
        lb = a(lhsT).base_partition(); ob = a(out).base_partition()
        kw = {}
        if lb != 0 or ob != 0:
            kw["tile_position"] = (lb, ob)
        return self.op(self.pe, lambda e: e.matmul(a(out), a(lhsT), a(rhs), start=start, stop=stop, **kw),
                       [lhsT, rhs] + ([] if start else [out]), [out])

    def tr(self, out, in_, ident):
        a = self._a
        return self.op(self.pe, lambda e: e.transpose(a(out), a(in_), a(ident)),
                       [in_, ident], [out])

    def activation(self, out, in_, func, bias=None, scale=None, accum=None, eng=None):
        a = self._a
        eng = eng or self.act
        kw = {}
        if bias is not None:
            kw["bias"] = a(bias)
        if scale is not None:
            kw["scale"] = a(scale)
        if accum is not None:
            kw["accum_out"] = a(accum)
        w = [out] + ([accum] if accum is not None else [])
        return self.op(eng, lambda e: e.activation(a(out), a(in_), func, **kw),
                       [in_, bias, scale], w)

    def tt(self, eng, out, in0, in1, op):
        a = self._a
        return self.op(eng, lambda e: e.tensor_tensor(a(out), a(in0), a(in1), op), [in0, in1], [out])

    def ts(self, eng, out, in0, s1, op0, s2=None, op1=None, accum=None):
        a = self._a
        kw = {}
        if op1 is not None:
            kw["op1"] = op1
        if accum is not None:
            kw["accum_out"] = a(accum)
        w = [out] + ([accum] if accum is not None else [])
        return self.op(eng, lambda e: e.tensor_scalar(a(out), a(in0), a(s1), a(s2) if s2 is not None else None, op0, **kw),
                       [in0, s1, s2], w)

    def stt(self, eng, out, in0, scalar, in1, op0, op1):
        a = self._a
        return self.op(eng, lambda e: e.scalar_tensor_tensor(a(out), a(in0), a(scalar), a(in1), op0, op1),
                       [in0, scalar, in1], [out])

    def copy(self, eng, out, in_):
        a = self._a
        if eng is self.act:
            return self.op(eng, lambda e: e.copy(a(out), a(in_)), [in_], [out])
        return self.op(eng, lambda e: e.tensor_copy(a(out), a(in_)), [in_], [out])

    def scan(self, out, d0, d1, init, op0, op1, eng=None):
        a = self._a
        eng = eng or self.dve
        return self.op(eng, lambda e: e.tensor_tensor_scan(a(out), a(d0), a(d1), a(init), op0, op1),
                       [d0, d1, init], [out])

    def reduce(self, eng, out, in_, op, axis=None):
        a = self._a
        axis = axis if axis is not None else AX.X
        return self.op(eng, lambda e: e.tensor_reduce(a(out), a(in_), axis, op), [in_], [out])

    def memset(self, eng, out, val):
        a = self._a
        return self.op(eng, lambda e: e.memset(a(out), val), [], [out])

    def recip(self, out, in_):
        a = self._a
        return self.op(self.dve, lambda e: e.reciprocal(a(out), a(in_)), [in_], [out])

    def finish(self, tiles):
        for tl in tiles:
            if tl.dsem is not None:
                self.sp.prog.append(('w', tl.dsem, tl.dcnt * 16))

    def replay(self):
        def run(eng):
            def f(e):
                for a in eng.prog:
                    if a[0] == 'w':
                        e.wait_ge(a[1], a[2] if a[2] is not None else self.sem_cnt[id(a[1])] * 16)
                    else:
                        ins = a[1](e)
                        ins.then_inc(a[2], a[3])
            return f
        with self.nc.Block() as block:
            block.tensor(run(self.pe))
            block.vector(run(self.dve))
            block.scalar(run(self.act))
            block.gpsimd(run(self.pool))
            block.sync(run(self.sp))


import threading


class Coop:
    def __init__(self, K, fns):
        self.K = K; self.fns = fns; self.n = len(fns)
        self.sems = [threading.Semaphore(0) for _ in fns]
        self.finished = [False] * self.n
        self.err = None
        self.tl = threading.local()

    def _next(self, i):
        for d in range(1, self.n + 1):
            j = (i + d) % self.n
            if not self.finished[j]:
                return j
        return None

    def switch(self):
        i = self.tl.idx
        j = self._next(i)
        if j is None or j == i:
            return
        self.sems[j].release()
        self.sems[i].acquire()

    def _wrap(self, i):
        self.tl.idx = i
        self.sems[i].acquire()
        try:
            r = self.fns[i]()
            if r is not None:
                for _ in r:
                    pass
        except BaseException as e:
            self.err = e
        finally:
            self.finished[i] = True
            j = self._next(i)
            if j is not None:
                self.sems[j].release()
            else:
                self.main.release()

    def run(self):
        self.main = threading.Semaphore(0)
        ths = [threading.Thread(target=self._wrap, args=(i,)) for i in range(self.n)]
        for t in ths:
            t.start()
        self.K.switch_hook = self.switch
        self.sems[0].release()
        self.main.acquire()
        self.K.switch_hook = None
        for t in ths:
            t.join()
        if self.err is not None:
            raise self.err

D = 1024; T = 2048; N = 256; NB = T // N; NTT = N // 128; LYR = 2; DFF = 2816; NS = 16
EPS = 1e-6
MUL = ALU.mult; ADD = ALU.add; SUB = ALU.subtract; MAX = ALU.max

WSHAPES = dict(norm_mix=(2, 1024), w_in=(2, 1024, 2576), w_out=(2, 1024, 1024), mlstm_b_i=(2, 4), mlstm_b_f=(2, 4),
               mlstm_norm=(2, 256), s5_a_re=(2, 32, 64), s5_a_im=(2, 32, 64), s5_log_dt=(2, 32),
               s5_b_re=(2, 32, 64, 16), s5_b_im=(2, 32, 64, 16), s5_c_re=(2, 32, 16, 64), s5_c_im=(2, 32, 16, 64),
               s5_d=(2, 32, 16), s5_w_glu=(2, 512, 512), gdn_conv_w=(2, 4, 768), gdn_a_log=(2, 4), gdn_dt_bias=(2, 4),
               gdn_norm=(2, 64), norm_xattn=(2, 1024), norm_mem=(2, 1024), w_mq=(2, 1024, 1024), w_mk=(2, 1024, 1024),
               w_mv=(2, 1024, 1024), w_mo=(2, 1024, 1024), norm_ffn=(2, 1024), w_gate=(2, 1024, 2816),
               w_up=(2, 1024, 2816), w_down=(2, 2816, 1024), norm_final=(1024,))
INSHAPES = dict(xp=(T, D), xs=(NS, D), mem=(256, D), ck=(2, NS, 256, 1024), cv=(2, NS, 256, 1024),
                st_c=(2, NS, 4, 64, 64), st_n=(2, NS, 4, 64), st_m=(2, NS, 4), st_s5r=(2, NS, 32, 64),
                st_s5i=(2, NS, 32, 64), st_g=(2, NS, 4, 64, 64), st_gc=(2, NS, 3, 768))
OUTSHAPES = dict(y_p=(T, D), y_s=(NS, D), mk_p=(2, 256, 1024), mv_p=(2, 256, 1024),
                 c_p=(2, 4, 64, 64), n_p=(2, 4, 64), m_p=(2, 4), s5r_p=(2, 32, 64), s5i_p=(2, 32, 64),
                 g_p=(2, 4, 64, 64), gc_p=(2, 3, 768),
                 c_s=(2, NS, 4, 64, 64), n_s=(2, NS, 4, 64), m_s=(2, NS, 4), s5r_s=(2, NS, 32, 64),
                 s5i_s=(2, NS, 32, 64), g_s=(2, NS, 4, 64, 64), gc_s=(2, NS, 3, 768))


def make_consts():
    c = {}
    c["c_ident"] = np.eye(128, dtype=np.float32)
    j = np.arange(128)[:, None]; t = np.arange(128)[None, :]
    m = np.zeros((128, 5, 128), np.float32)
    m[:, 0] = (j <= t); m[:, 1] = np.where(j <= t, 0.0, -30000.0); m[:, 2] = (j < t)
    m[:, 3] = (j // 64 == t // 64); m[:, 4] = (j >= 64) & (t < 64)
    c["c_mask"] = m
    c["c_blk64"] = (j // 64 == t // 64).astype(np.float32)
    sel = np.zeros((128, 12, 128), np.float32)
    k = np.arange(128)[:, None]; mm_ = np.arange(128)[None, :]
    for p in range(2):
        sel[:, p] = 8.0 * (k == 2 * p + mm_ // 64)
        sel[:, 2 + p] = 1.0 * (k == 32 + 2 * p + mm_ // 64)
    for h in range(4):
        sel[:, 4 + h] = -1.0 * (k == 32 + h)
        sel[:, 8 + h] = 1.0 * (k == 32 + h)
    c["c_sel"] = sel
    c["c_tau"] = np.tile(np.arange(N, dtype=np.float32)[None, :], (128, 1))
    return c


CONSTSHAPES = dict(c_ident=(128, 128), c_mask=(128, 5, 128), c_blk64=(128, 128), c_sel=(128, 12, 128), c_tau=(128, N))


def build():
    nc = bass.Bass("TRN2", target_bir_lowering=False)
    I = {k: nc.dram_tensor(k, list(s), F32, kind="ExternalInput").ap()
         for k, s in {**INSHAPES, **WSHAPES, **CONSTSHAPES}.items()}
    O = {k: nc.dram_tensor(k, list(s), F32, kind="ExternalOutput").ap() for k, s in OUTSHAPES.items()}
    K = KB(nc); K.start()
    dve, act, pool, sp, pe = K.dve, K.act, K.pool, K.sp, K.pe
    cnt = [0]

    def dscr(shape, dt=BF16):
        cnt[0] += 1
        return nc.dram_tensor("scr%d" % cnt[0], list(shape), dt, kind="Internal").ap()

    tcache = {}

    def tmp(shape, dt, name):
        k = (name, tuple(shape), str(dt))
        if k not in tcache:
            tcache[k] = K.sb(shape, dt, name)
        return tcache[k]

    def newres(nm="res"):
        cnt[0] += 1
        return Tl(K, None, "%s%d" % (nm, cnt[0]))

    outs_tiles = []

    def store(dst_ap, src_v, eng=None, **kw):
        K.dma_tile(eng or pool, dst_ap, src_v, **kw)
        if src_v.tl not in outs_tiles:
            outs_tiles.append(src_v.tl)

    def ld(dst_v, src_ap, **kw):
        K.dma_tile(sp, dst_v, src_ap, **kw)

    ident = K.sb([128, 128], F32, "ident"); ld(ident[:, :], I["c_ident"])
    identb = K.sb([128, 128], BF16, "identb"); K.copy(dve, identb[:, :], ident[:, :])
    mask = K.sb([128, 5, 128], F32, "mask"); ld(mask[:, :, :], I["c_mask"])
    blkf = K.sb([128, 128], F32, "blkf"); ld(blkf[:, :], I["c_blk64"])
    blk64 = K.sb([128, 128], BF16, "blk64"); K.copy(dve, blk64[:, :], blkf[:, :])
    onesb = K.sb([128, 128], BF16, "onesb"); K.memset(dve, onesb[:, :], 1.0)
    sel = K.sb([128, 12, 128], F32, "sel"); ld(sel[:, :, :], I["c_sel"])
    tau = K.sb([128, N], F32, "tau"); ld(tau[:, :], I["c_tau"])
    zrow = K.sb([128, N], F32, "zrow"); K.memset(dve, zrow[:, :], 0.0)
    onerow = K.sb([128, N], F32, "onerow"); K.memset(dve, onerow[:, :], 1.0)

    P = [K.ps([128, 512], F32, "P%d" % i) if i != 5 else K.ps([128, 1024], BF16, "P5b") for i in range(8)]
    pjc = [0]

    def pj():
        pjc[0] += 1
        return P[pjc[0] % 2]

    PW = [P[0], P[1], P[2], P[3], P[4], P[6], P[7]]
    pwc = [0]

    def pjw():
        pwc[0] += 1
        return PW[pwc[0] % len(PW)]

    import os as _os2
    WS = [dict() for _ in range(LYR)]

    def castw(l, name, groups, nk):
        tot = sum(g.shape[2] for g in groups)
        dst = dscr([128, nk, tot]); res = newres("w")
        if l >= 1 and int(_os2.environ.get('KL1', 1)) and name not in ('mk', 'mv'):
            if getattr(K, "l1sem", None) is None:
                K.l1sem = K.es.enter_context(nc.semaphore("w_l1"))
                if int(_os2.environ.get('KL1', 1)) == 1: K.lazy_sems.add(id(K.l1sem))
            res.dsem = K.l1sem
        o = 0
        for gi_, g in enumerate(groups):
            w = g.shape[2]
            if gi_ > 0 and id(res.dsem) in K.lazy_sems:
                r2 = newres("w"); r2.dsem = res.dsem; res = r2
            K.dma_tile(pool, V(res, dst[:, :, o:o + w]), g)
            o += w
        WS[l][name] = (res, dst, nk, tot)

    def cast_group(l, which):
        win = I["w_in"][l].rearrange("(k p) c -> p k c", p=128)
        if which == 0:
            castw(l, "in_tm", [win[:, :, 0:768]], 8)
            castw(l, "in_fm", [win[:, :, 768:1024], win[:, :, 1032:1544], win[:, :, 1544:2312], win[:, :, 2312:2568]], 8)
        elif which == 3:
            castw(l, "mk", [I["w_mk"][l].rearrange("(k p) c -> p k c", p=128)], 8)
            castw(l, "mv", [I["w_mv"][l].rearrange("(k p) c -> p k c", p=128)], 8)
        elif which == 1:
            castw(l, "glu", [I["s5_w_glu"][l].rearrange("(k p) c -> p k c", p=128)], 4)
            for nm, src in (("out", "w_out"), ("mq", "w_mq"), ("mo", "w_mo")):
                castw(l, nm, [I[src][l].rearrange("(k p) c -> p k c", p=128)], 8)
        else:
            for nm, src in (("gate", "w_gate"), ("up", "w_up")):
                castw(l, nm, [I[src][l].rearrange("(k p) c -> p k c", p=128)], 8)
            castw(l, "down", [I["w_down"][l].rearrange("(k p) c -> p k c", p=128)], 22)


    ring = [K.sb([128, 2048], BF16, "wr%d" % i) for i in range(4)]
    rpos = [0]

    def wload(res, ap3, nk, w):
        s = ring[rpos[0] % 4]; rpos[0] += 1
        assert nk * w <= 2048, (nk, w)
        v = s[:, 0:nk * w].re("p (k w) -> p k w", k=nk)
        K.dma_tile(sp, v, V(res, ap3))
        return v

    def proj_fm(l, name, c0, nct_tot, rhs_fn, evac, n, wide=True):
        res, dst, nk, tot = WS[l][name]
        per = max(1, (2048 // nk) // 128)
        for u0 in range(0, nct_tot, per):
            nct = min(per, nct_tot - u0)
            wv = wload(res, dst[:, :, c0 + u0 * 128: c0 + (u0 + nct) * 128], nk, nct * 128)
            for c in range(nct):
                ps = pjw() if wide else pj()
                for kt in range(nk):
                    K.mm(ps[:, 0:n], wv[:, kt, c * 128:(c + 1) * 128], rhs_fn(kt), start=(kt == 0), stop=(kt == nk - 1))
                evac(u0 + c, ps[:, 0:n])

    PAR = K.sb([128, 160], F32, "PAR"); parpos = [0]

    def paralloc(w):
        v = PAR[:, parpos[0]:parpos[0] + w]; parpos[0] += w
        return v

    def colload(src1d, name):
        t = paralloc(8)
        ld(t, src1d.rearrange("(k p) -> p k", p=128), allow_slow_non_contiguous=True)
        return t

    gmix = [colload(I["norm_mix"][l], "gmix") for l in range(LYR)]
    gxat = [colload(I["norm_xattn"][l], "gxat") for l in range(LYR)]
    gffn = [colload(I["norm_ffn"][l], "gffn") for l in range(LYR)]
    gmem = [colload(I["norm_mem"][l], "gmem") for l in range(LYR)]
    gfin = colload(I["norm_final"], "gfin")
    WG, bcol, GN, GDNN, CW, Dcol = [], [], [], [], [], []
    for l in range(LYR):
        win = I["w_in"][l].rearrange("(k p) c -> p k c", p=128)
        gwf = K.sb([128, 8, 16], F32, "gwf")
        ld(gwf[:, :, 0:8], win[:, :, 1024:1032]); ld(gwf[:, :, 8:16], win[:, :, 2568:2576])
        wg = K.sb([128, 8, 2, 128], BF16, "WG"); K.memset(dve, wg[:, :, :, :], 0.0)
        K.copy(dve, wg[:, :, 0, 0:4], gwf[:, :, 0:4]); K.copy(dve, wg[:, :, 1, 0:4], gwf[:, :, 4:8])
        K.copy(dve, wg[:, :, 0, 32:36], gwf[:, :, 8:12]); K.copy(dve, wg[:, :, 1, 32:36], gwf[:, :, 12:16])
        WG.append(wg)
        bc_ = K.sb([128, 4], F32, "bcol"); K.memset(dve, bc_[:, :], 0.0)
        ld(bc_[0:4, 0:1], I["mlstm_b_i"][l].rearrange("(h o) -> h o", o=1))
        ld(bc_[0:4, 1:2], I["mlstm_b_f"][l].rearrange("(h o) -> h o", o=1))
        ld(bc_[32:36, 2:3], I["gdn_dt_bias"][l].rearrange("(h o) -> h o", o=1))
        ld(bc_[32:36, 3:4], I["gdn_a_log"][l].rearrange("(h o) -> h o", o=1))
        K.ts(dve, bc_[0:4, 1:2], bc_[0:4, 1:2], -1.0, MUL)
        K.activation(bc_[32:36, 3:4], bc_[32:36, 3:4], AF.Exp)
        K.ts(dve, bc_[32:36, 3:4], bc_[32:36, 3:4], -1.0, MUL)
        bcol.append(bc_)
        gn = K.sb([128, 256], F32, "GN"); ld(gn[:, :], I["mlstm_norm"][l].partition_broadcast(128)); GN.append(gn)
        g2 = K.sb([128, 64], F32, "GDNN"); ld(g2[:, :], I["gdn_norm"][l].partition_broadcast(128)); GDNN.append(g2)
        cw = paralloc(24).re("p (i j) -> p i j", i=6)
        for i6 in range(6):
            ld(cw[:, i6, :], I["gdn_conv_w"][l][:, i6 * 128:(i6 + 1) * 128].rearrange("j p -> p j"), allow_slow_non_contiguous=True)
        CW.append(cw)
        dc = paralloc(4)
        ld(dc, I["s5_d"][l].rearrange("g p -> (g p)").rearrange("(o q) -> q o", q=128), allow_slow_non_contiguous=True)
        Dcol.append(dc)

    K.cur_group = None
    TWO_PI = 2.0 * math.pi

    def sincos(sin_o, cos_o, ang, shp, nm):
        for out, add in ((sin_o, 0.0), (cos_o, math.pi / 2)):
            r = tmp(shp, F32, nm + "r"); ki = tmp(shp, I32, nm + "ki"); kf = tmp(shp, F32, nm + "kf")
            K.ts(dve, r[:, :], ang, add, ADD, 1.0 / TWO_PI, MUL)
            K.copy(dve, ki[:, :], r[:, :]); K.copy(dve, kf[:, :], ki[:, :])
            K.tt(dve, r[:, :], r[:, :], kf[:, :], SUB)
            K.stt(dve, kf[:, :], r[:, :], 0.5, r[:, :], ALU.is_gt, SUB)
            K.activation(out, kf[:, :], AF.Sin, scale=-TWO_PI)

    tabring_pre = K.sb([128, 2, N], F32, "tabr0")
    import os as _os
    S5 = []

    def s5_setup(l):
        s = {}
        are = tmp([128, 16], F32, "are"); aim = tmp([128, 16], F32, "aim"); ldt = tmp([128, 16], F32, "ldt")
        ld(are[:, :], I["s5_a_re"][l].rearrange("(i g) n -> (g n) i", g=2), allow_slow_non_contiguous=True)
        ld(aim[:, :], I["s5_a_im"][l].rearrange("(i g) n -> (g n) i", g=2), allow_slow_non_contiguous=True)
        lv = I["s5_log_dt"][l].rearrange("(i g) -> g i", g=2)
        for gg in range(2):
            ld(ldt[gg * 64:(gg + 1) * 64, :], lv[gg].partition_broadcast(64), allow_slow_non_contiguous=True)
        dt_ = tmp([128, 16], F32, "dt"); K.activation(dt_[:, :], ldt[:, :], AF.Exp)
        mag = K.sb([128, 16], F32, "mag"); K.tt(dve, mag[:, :], are[:, :], dt_[:, :], MUL)
        K.activation(mag[:, :], mag[:, :], AF.Exp)
        th = K.sb([128, 16], F32, "th"); K.tt(dve, th[:, :], aim[:, :], dt_[:, :], MUL)
        ur = K.sb([128, 16], F32, "ur"); ui = K.sb([128, 16], F32, "ui")
        sincos(ui[:, :], ur[:, :], th[:, :], [128, 16], "sc0")
        lre = K.sb([128, 16], F32, "lre"); lim = K.sb([128, 16], F32, "lim")
        K.tt(dve, lre[:, :], mag[:, :], ur[:, :], MUL); K.tt(dve, lim[:, :], mag[:, :], ui[:, :], MUL)
        nr = tmp([128, 16], F32, "nr"); K.ts(dve, nr[:, :], lre[:, :], -1.0, ADD)
        den = tmp([128, 16], F32, "den"); t0 = tmp([128, 16], F32, "t0"); t1 = tmp([128, 16], F32, "t1")
        K.tt(dve, den[:, :], are[:, :], are[:, :], MUL); K.tt(dve, t0[:, :], aim[:, :], aim[:, :], MUL)
        K.tt(dve, den[:, :], den[:, :], t0[:, :], ADD); K.recip(den[:, :], den[:, :])
        cre = tmp([128, 16], F32, "cre"); cim = tmp([128, 16], F32, "cim")
        K.tt(dve, t0[:, :], nr[:, :], are[:, :], MUL); K.tt(dve, t1[:, :], lim[:, :], aim[:, :], MUL)
        K.tt(dve, t0[:, :], t0[:, :], t1[:, :], ADD); K.tt(dve, cre[:, :], t0[:, :], den[:, :], MUL)
        K.tt(dve, t0[:, :], lim[:, :], are[:, :], MUL); K.tt(dve, t1[:, :], nr[:, :], aim[:, :], MUL)
        K.tt(dve, t0[:, :], t0[:, :], t1[:, :], SUB); K.tt(dve, cim[:, :], t0[:, :], den[:, :], MUL)
        bre = V(tmp([128, N], F32, "s5t2"), tmp([128, N], F32, "s5t2").t[:, :].rearrange("p (a b) -> p a b", a=16)); bim = V(tmp([128, N], F32, "s5t3"), tmp([128, N], F32, "s5t3").t[:, :].rearrange("p (a b) -> p a b", a=16))
        ld(bre[:, :, :], I["s5_b_re"][l].rearrange("(i g) n p -> (g n) i p", g=2))
        ld(bim[:, :, :], I["s5_b_im"][l].rearrange("(i g) n p -> (g n) i p", g=2))
        bb = [V(tmp([128, N], F32, "s5t4"), tmp([128, N], F32, "s5t4").t[:, :].rearrange("p (a b) -> p a b", a=16)), V(tmp([128, N], F32, "s5gr"), tmp([128, N], F32, "s5gr").t[:, :].rearrange("p (a b) -> p a b", a=16))]
        tb = V(tmp([128, N], F32, "s5gi"), tmp([128, N], F32, "s5gi").t[:, :].rearrange("p (a b) -> p a b", a=16))
        crb = cre[:, :].un(2).bct([128, 16, 16]); cib = cim[:, :].un(2).bct([128, 16, 16])
        K.tt(dve, bb[0][:, :, :], bre[:, :, :], crb, MUL); K.tt(dve, tb[:, :, :], bim[:, :, :], cib, MUL)
        K.tt(dve, bb[0][:, :, :], bb[0][:, :, :], tb[:, :, :], SUB)
        K.tt(dve, bb[1][:, :, :], bim[:, :, :], crb, MUL); K.tt(dve, tb[:, :, :], bre[:, :, :], cib, MUL)
        K.tt(dve, bb[1][:, :, :], bb[1][:, :, :], tb[:, :, :], ADD)
        BTd = dscr([128, 2, 16, 128]); BTres = newres("bt")
        for ri in range(2):
            bbz = V(tmp([128, 2048], F32, "big8k"), tmp([128, 2048], F32, "big8k").t[:, :].rearrange("p (i c) -> p i c", i=16)); K.memset(dve, bbz[:, :, :], 0.0)
            for gg in range(2):
                for q in range(4):
                    dst = bbz[gg * 64:(gg + 1) * 64, :, :].re("p (a q) c -> p q a c", q=4)[:, q, :, (2 * q + gg) * 16:(2 * q + gg) * 16 + 16]
                    src = bb[ri][gg * 64:(gg + 1) * 64, :, :].re("p (a q) c -> p q a c", q=4)[:, q, :, :]
                    K.copy(dve, dst, src)
            bts = tmp([128, 16, 128], BF16, "big4kb")
            for i4 in range(4):
                for ii in range(4):
                    K.tr(P[3][:, ii * 128:(ii + 1) * 128], bbz[:, i4 * 4 + ii, :], ident[:, :])
                K.copy(act, bts[:, i4 * 4:(i4 + 1) * 4, :], P[3][:, :].re("p (a c) -> p a c", a=4))
            K.dma_tile(sp, V(BTres, BTd[:, ri]), bts[:, :, :])
        s["BT"] = (BTres, BTd)
        CTd = dscr([128, 2, 16, 128]); CTres = newres("ct")
        for ri, nm in enumerate(("s5_c_re", "s5_c_im")):
            cz = V(tmp([128, 2048], F32, "big8k"), tmp([128, 2048], F32, "big8k").t[0:32, :].rearrange("p (i c) -> p i c", i=16)); K.memset(dve, cz[:, :, :], 0.0)
            cv_ = I[nm][l].rearrange("(i g) p n -> g p i n", g=2)
            ld(cz[0:16, :, 0:64], cv_[0]); ld(cz[16:32, :, 64:128], cv_[1])
            for i in range(16):
                K.tr(P[4][:, i * 32:(i + 1) * 32], cz[:, i, :], ident[0:32, 0:32])
            cts = tmp([128, 16, 128], BF16, "big4kb"); K.memset(dve, cts[:, :, :], 0.0)
            for q in range(4):
                dst = cts[:, :, :].re("p (a q) c -> p q a c", q=4)[:, q, :, q * 32:(q + 1) * 32]
                src = P[4][:, :].re("p (a q c) -> p q a c", q=4, c=32)[:, q, :, :]
                K.ts(dve, dst, src, 1.0 if ri == 0 else -1.0, MUL)
            K.dma_tile(sp, V(CTres, CTd[:, ri]), cts[:, :, :])
        s["CT"] = (CTres, CTd)
        TABd = dscr([16, 128, 2, N], F32); TABres = newres("tab")
        CL = K.sb([128, 16], F32, "CL"); SL = K.sb([128, 16], F32, "SL")
        for i in range(16):
            ang = tmp([128, N], F32, "s5t1"); K.ts(dve, ang[:, :], tau[:, :], th[:, i:i + 1], MUL)
            tabs = tabring_pre
            sincos(tabs[:, 1, :], tabs[:, 0, :], ang[:, :], [128, N], "sct")
            K.copy(act, CL[:, i:i + 1], tabs[:, 0, N - 1:N]); K.copy(act, SL[:, i:i + 1], tabs[:, 1, N - 1:N])
            K.dma_tile(sp, V(TABres, TABd[i]), tabs[:, :, :])
        s.update(TAB=(TABres, TABd), CL=CL, SL=SL, ur=ur, ui=ui, mag=mag, lre=lre, lim=lim)
        s["Hre"] = K.sb([128, 16], F32, "Hre"); s["Him"] = K.sb([128, 16], F32, "Him")
        K.memset(dve, s["Hre"][:, :], 0.0); K.memset(dve, s["Him"][:, :], 0.0)
        S5.append(s)

    def rmsnorm_fm(xT, gcol, xn, n):
        ps = pj()
        for kt in range(8):
            sq = tmp([128, N], BF16, "sq%d" % (kt % 2))
            K.activation(sq[:, 0:n], xT[kt][:, 0:n], AF.Square)
            K.mm(ps[:, 0:n], onesb[:, :], sq[:, 0:n], start=(kt == 0), stop=(kt == 7))
        rstd = tmp([128, N], F32, "rstd")
        K.activation(rstd[:, 0:n], ps[:, 0:n], AF.Sqrt, scale=1.0 / D, bias=EPS)
        K.recip(rstd[:, 0:n], rstd[:, 0:n])
        for kt in range(8):
            K.stt(dve, xn[:, kt, 0:n], xT[kt][:, 0:n], gcol[:, kt:kt + 1], rstd[:, 0:n], MUL, MUL)

    cast_group(0, 0); cast_group(0, 3); cast_group(1, 3)
    s5_setup(0)
    MKT, MVB = [], []
    memx = [tmp([128, 1024], F32, "big4k%d" % i) for i in range(2)]
    xhT = V(tmp([128, 2048], F32, "big8k"), tmp([128, 2048], F32, "big8k").t[:, :].rearrange("p (k m) -> p k m", k=8))
    for mt in range(2):
        ld(memx[mt][:, :], I["mem"][mt * 128:(mt + 1) * 128, :])
        ss = tmp([128, 1], F32, "mss"); junk = tmp([128, 1024], BF16, "mjunk")
        K.activation(junk[:, :], memx[mt][:, :], AF.Square, accum=ss[:, :])
        K.activation(ss[:, :], ss[:, :], AF.Sqrt, scale=1.0 / D, bias=EPS)
        K.recip(ss[:, :], ss[:, :])
        K.ts(dve, memx[mt][:, :], memx[mt][:, :], ss[:, 0:1], MUL)
        for kt in range(8):
            K.tr(P[3][:, (kt % 4) * 128:(kt % 4 + 1) * 128], memx[mt][:, kt * 128:(kt + 1) * 128], ident[:, :])
            if kt % 4 == 3:
                K.copy(act, xhT[:, kt - 3:kt + 1, mt * 128:(mt + 1) * 128], P[3][:, :].re("p (a c) -> p a c", a=4))
    def memkv(l):
        mnT = tmp([128, 8, 256], BF16, "mnT")
        for kt in range(8):
            K.ts(dve, mnT[:, kt, :], xhT[:, kt, :], gmem[l][:, kt:kt + 1], MUL)
        mkT = K.sb([128, 8, 256], BF16, "mkT"); mvb = K.sb([128, 2, 1024], BF16, "mvb")
        proj_fm(l, "mk", 0, 8, lambda kt: mnT[:, kt, :], lambda ot, ps: K.copy(act, mkT[:, ot, :], ps), 256, wide=False)
        for nm, oname in (("mk", "mk_p"), ("mv", "mv_p")):
            res, dst, nk, tot = WS[l][nm]
            for ch in range(4):
                wv = wload(res, dst[:, :, ch * 256:(ch + 1) * 256], 8, 256)
                for mt in range(2):
                    ps = pj()
                    for kt in range(8):
                        K.mm(ps[:, 0:256], mnT[:, kt, mt * 128:(mt + 1) * 128], wv[:, kt, :], start=(kt == 0), stop=(kt == 7))
                    st_ = tmp([128, 512], F32, "kvst%d" % ((ch * 2 + mt) % 2))
                    K.copy(act, st_[:, 0:256], ps[:, 0:256])
                    if nm == "mv":
                        K.copy(dve, mvb[:, mt, ch * 256:(ch + 1) * 256], ps[:, 0:256])
                    store(O[oname][l, mt * 128:(mt + 1) * 128, ch * 256:(ch + 1) * 256], st_[:, 0:256])
        MKT.append(mkT); MVB.append(mvb)

    memkv(0); memkv(1)
    cast_group(0, 1); cast_group(0, 2)

    ST = []
    for l in range(LYR):
        s = {}
        s["CA"] = [K.sb([128, 65], F32, "CA") for _ in range(2)]
        s["CAb"] = [K.sb([128, 65], BF16, "CAb") for _ in range(2)]
        s["GS"] = [K.sb([128, 64], F32, "GS") for _ in range(2)]
        s["GSb"] = [K.sb([128, 64], BF16, "GSb") for _ in range(2)]
        for p in range(2):
            K.memset(dve, s["CA"][p][:, :], 0.0); K.memset(dve, s["CAb"][p][:, :], 0.0)
            K.memset(dve, s["GS"][p][:, :], 0.0); K.memset(dve, s["GSb"][p][:, :], 0.0)
        s["car"] = K.sb([128, 2], F32, "car"); K.memset(dve, s["car"][:, :], 0.0)
        s["XP"] = [K.sb([128, 3 + N], F32, "XP") for _ in range(6)]
        for i in range(6):
            K.memset(dve, s["XP"][i][:, 0:3], 0.0)
        ST.append(s)

    xT = [K.sb([128, N], F32, "xT%d" % i) for i in range(8)]
    xn = K.sb([128, 8, N], BF16, "xn")
    mixT = K.sb([128, 8, N], BF16, "mixT")
    ztm = [K.sb([128, 768], BF16, "ztm%d" % i) for i in range(NTT)]
    G = [K.sb([128, N], F32, "G%d" % i) for i in range(2)]
    for g_ in G:
        K.memset(dve, g_[:, :], 0.0)
    uT = [K.sb([128, N], BF16, "uT%d" % i) for i in range(4)]
    aoT = [K.sb([128, N], BF16, "aoT%d" % i) for i in range(2)]
    cgT = [K.sb([128, N], BF16, "cgT%d" % i) for i in range(2)]
    gsc = [K.sb([128, N], F32, "gsc%d" % i) for i in range(6)]
    for g_ in gsc:
        K.memset(dve, g_[:, :], 0.0)

    def inproj(l, n, prompt):
        st = ST[l]
        if prompt:
            res, dst, nk, tot = WS[l]["in_tm"]
            for (c0, w) in ((0, 256), (256, 256), (512, 256)):
                wv = wload(res, dst[:, :, c0:c0 + w], 8, w)
                for ts_ in range(NTT):
                    ps = pjw()
                    for kt in range(8):
                        K.mm(ps[:, 0:w], xn[:, kt, ts_ * 128:(ts_ + 1) * 128], wv[:, kt, :], start=(kt == 0), stop=(kt == 7))
                    K.copy(act, ztm[ts_][:, c0:c0 + w], ps[:, 0:w])
        for gi in range(2):
            ps = pjw()
            for kt in range(8):
                K.mm(ps[:, 0:n], WG[l][:, kt, gi, :], xn[:, kt, 0:n], start=(kt == 0), stop=(kt == 7))
            K.copy(act, G[gi][0:36, 0:n], ps[0:36, 0:n])

        def evac(ot, ps):
            if ot < 2:
                K.activation(aoT[ot][:, 0:n], ps, AF.Sigmoid)
            elif ot < 6:
                K.copy(act, uT[ot - 2][:, 0:n], ps)
            elif ot < 12:
                K.copy(act, st["XP"][ot - 6][:, 3:3 + n], ps)
            else:
                K.activation(cgT[ot - 12][:, 0:n], ps, AF.Silu)
        proj_fm(l, "in_fm", 0, 14, lambda kt: xn[:, kt, 0:n], evac, n)

    def mlstm_prompt(l, bi):
        st = ST[l]; car = st["car"]; bc_ = bcol[l]
        ic, l1, Fn, M_, Mp = gsc[0], gsc[1], gsc[2], gsc[3], gsc[4]
        r4 = slice(0, 4)
        K.activation(ic[r4, :], G[0][r4, :], AF.Identity, bias=bc_[r4, 0:1])
        K.activation(l1[r4, :], G[1][r4, :], AF.Exp, bias=bc_[r4, 1:2], scale=-1.0)
        K.activation(l1[r4, :], l1[r4, :], AF.Ln, bias=1.0)
        K.scan(Fn[r4, :], onerow[r4, :], l1[r4, :], car[r4, 0:1], MUL, ADD)
        K.tt(dve, ic[r4, :], ic[r4, :], Fn[r4, :], ADD)
        K.scan(M_[r4, :], ic[r4, :], zrow[r4, :], car[r4, 1:2], MAX, ADD)
        K.tt(dve, l1[r4, :], Fn[r4, :], M_[r4, :], SUB)
        K.activation(l1[r4, :], l1[r4, :], AF.Exp)
        for c in range(NTT):
            src = car[r4, 1:2] if c == 0 else M_[r4, c * 128 - 1:c * 128]
            K.ts(dve, Mp[r4, c * 128:(c + 1) * 128], zrow[r4, 0:128], src, ADD)
        K.tt(dve, ic[r4, :], ic[r4, :], Mp[r4, :], SUB)
        K.activation(ic[r4, :], ic[r4, :], AF.Exp)
        K.tt(dve, Mp[r4, :], Mp[r4, :], M_[r4, :], SUB)
        K.activation(Mp[r4, :], Mp[r4, :], AF.Exp, bias=-math.log(8.0))
        SUBS = float(_os.environ.get('KSUB', 9))
        if SUBS < 1: return
        dec = tmp([128, 2, NTT], F32, "mdec")
        for p in range(2):
            K.mm(P[3][:, p * 8:p * 8 + NTT], sel[0:4, p, :], Mp[r4, 127::128])
        K.copy(act, dec[:, :, :], P[3][:, 0:16].re("p (a c) -> p a c", a=2)[:, :, 0:NTT])
        K.copy(act, car[r4, 0:1], Fn[r4, N - 1:N]); K.copy(act, car[r4, 1:2], M_[r4, N - 1:N])
        if SUBS < 2: return
        for c in range(NTT):
            cs = slice(c * 128, (c + 1) * 128)
            for qi, srct in enumerate((ic, Mp, l1)):
                K.tr(P[4][:, qi * 4:qi * 4 + 4], srct[r4, cs], ident[0:4, 0:4])
            TT = tmp([128, 12], F32, "mTT"); K.copy(act, TT[:, :], P[4][:, 0:12])
            ka = tmp([128, 4, 64], BF16, "mka"); qa = tmp([128, 4, 64], BF16, "mqa")
            va = tmp([128, 4, 65], BF16, "mva%d" % c)
            if bi == 0 and l == 0:
                K.memset(dve, va[:, :, 64:65], 1.0)
            z = ztm[c]
            K.tt(dve, qa[:, :, :], z[:, 0:256].re("p (h d) -> p h d", h=4), TT[:, 4:8].un(2).bct([128, 4, 64]), MUL)
            K.tt(dve, ka[:, :, :], z[:, 256:512].re("p (h d) -> p h d", h=4), TT[:, 0:4].un(2).bct([128, 4, 64]), MUL)
            K.copy(act, va[:, :, 0:64], z[:, 512:768].re("p (h d) -> p h d", h=4))
            if SUBS < 3: continue
            kT = tmp([128, 2, 128], BF16, "mkT"); qT = tmp([128, 2, 128], BF16, "mqT")
            PB = P[5][:, :]
            for p in range(2):
                K.tr(PB[:, p * 128:(p + 1) * 128], ka[:, 2 * p:2 * p + 2, :].re("p h d -> p (h d)"), identb[:, :])
                K.tr(PB[:, 256 + p * 128:256 + (p + 1) * 128], qa[:, 2 * p:2 * p + 2, :].re("p h d -> p (h d)"), identb[:, :])
            K.copy(act, kT[:, :, :], PB[:, 0:256].re("p (a c) -> p a c", a=2))
            K.copy(act, qT[:, :, :], PB[:, 256:512].re("p (a c) -> p a c", a=2))
            if SUBS < 3.5: continue
            for h in range(4):
                rs = slice((h % 2) * 64, (h % 2) * 64 + 64)
                K.mm((P[6] if h % 2 == 0 else P[4])[:, (h // 2) * 128:(h // 2 + 1) * 128], kT[rs, h // 2, :], qT[rs, h // 2, :])
            if SUBS < 4: continue
            sT = tmp([128, 4, 128], BF16, "msT")
            for par, pt_ in ((0, P[6]), (1, P[4])):
                K.tt(dve, sT[:, :, :].re("p (a b) t -> p b a t", b=2)[:, par], pt_[:, 0:256].re("p (h t) -> p h t", h=2), mask[:, 0:1, :].bct([128, 2, 128]), MUL)
            for h in range(4):
                rs = slice((h % 2) * 64, (h % 2) * 64 + 64)
                K.mm(P[7][:, h * 128:h * 128 + 65], sT[:, h, :], va[:, h, :], start=True, stop=False)
                K.mm(P[7][:, h * 128:h * 128 + 65], qT[rs, h // 2, :], st["CAb"][h // 2][rs, :], start=False, stop=True)
            if SUBS < 5: continue
            na = P[7][:, :].re("p (h e) -> p h e", h=4)
            dn = tmp([128, 4], F32, "mdn")
            K.activation(dn[:, :], na[:, :, 64], AF.Abs)
            K.tt(dve, dn[:, :], dn[:, :], TT[:, 8:12], MAX)
            K.recip(dn[:, :], dn[:, :])
            hh = tmp([128, 4, 64], F32, "mhh"); sq = tmp([128, 4, 64], F32, "mhsq"); ss = tmp([128, 4], F32, "mss4")
            K.tt(dve, hh[:, :, :], na[:, :, 0:64], dn[:, :].un(2).bct([128, 4, 64]), MUL)
            K.tt(dve, sq[:, :, :], hh[:, :, :], hh[:, :, :], MUL)
            K.reduce(dve, ss[:, :], sq[:, :, :], ADD)
            K.activation(ss[:, :], ss[:, :], AF.Sqrt, scale=1.0 / 64, bias=EPS)
            K.recip(ss[:, :], ss[:, :])
            K.tt(dve, hh[:, :, :], hh[:, :, :], ss[:, :].un(2).bct([128, 4, 64]), MUL)
            if SUBS < 6: continue
            hb = tmp([128, 256], BF16, "mhb")
            K.tt(dve, hb[:, :], hh[:, :, :].re("p h d -> p (h d)"), GN[l][:, :], MUL)
            for p in range(2):
                K.tr(PB[:, 512 + p * 128:512 + (p + 1) * 128], hb[:, p * 128:(p + 1) * 128], identb[:, :])
            for p in range(2):
                K.tt(dve, mixT[:, p, cs], PB[:, 512 + p * 128:512 + (p + 1) * 128], aoT[p][:, cs], MUL)
            if SUBS < 7: continue
            for p in range(2):
                for hh_ in range(2):
                    h = 2 * p + hh_
                    K.mm(P[3][:, 64 + h * 65:64 + (h + 1) * 65], ka[:, 2 * p:2 * p + 2, :].re("p h d -> p (h d)"), va[:, h, :])
                for hh_ in range(2):
                    h = 2 * p + hh_; rs = slice(hh_ * 64, hh_ * 64 + 64)
                    K.tt(dve, st["CA"][p][rs, :], st["CA"][p][rs, :], P[3][rs, 64 + h * 65:64 + (h + 1) * 65], ADD)
                K.ts(dve, st["CA"][p][:, :], st["CA"][p][:, :], dec[:, p, c:c + 1], MUL)
                K.copy(act, st["CAb"][p][:, :], st["CA"][p][:, :])
            yield

    MFLAG = {"m": True, "g": False}

    def gdn_prompt(l, bi):
        st = ST[l]; bc_ = bcol[l]; XP = st["XP"]
        r4 = slice(32, 36)
        bet, la, gg, eg, wr = gsc[0], gsc[1], gsc[2], gsc[3], gsc[4]
        cvs = [tmp([128, N], F32, "gcv%d" % i) for i in range(6)]
        for i in range(6):
            K.ts(dve, cvs[i][:, :], XP[i][:, 0:N], CW[l][:, i, 0:1], MUL)
            for j in range(1, 4):
                K.stt(dve, cvs[i][:, :], XP[i][:, j:j + N], CW[l][:, i, j:j + 1], cvs[i][:, :], MUL, ADD)
            K.activation(cvs[i][:, :], cvs[i][:, :], AF.Silu)
            if bi == NB - 1:
                gcst = tmp([128, 6, 3], F32, "gcst")
                K.copy(act, gcst[:, i, :], XP[i][:, N:N + 3])
                store(O["gc_p"][l].rearrange("j (i p) -> p i j", p=128)[:, i, :], gcst[:, i, :], allow_slow_non_contiguous=True)
            else:
                K.copy(act, XP[i][:, 0:3], XP[i][:, N:N + 3])
        qkT = [tmp([128, N], BF16, "gqk%d" % i) for i in range(4)]
        for i in range(4):
            sq = tmp([128, N], BF16, "gsq")
            K.activation(sq[:, :], cvs[i][:, :], AF.Square)
            K.mm(P[2][:, 0:N], blk64[:, :], sq[:, :])
            rs_ = tmp([128, N], F32, "grs")
            K.activation(rs_[:, :], P[2][:, 0:N], AF.Sqrt, bias=EPS)
            K.recip(rs_[:, :], rs_[:, :])
            K.stt(dve, qkT[i][:, :], cvs[i][:, :], 0.125 if i < 2 else 1.0, rs_[:, :], MUL, MUL)
        K.activation(bet[r4, :], G[0][r4, :], AF.Sigmoid)
        K.activation(la[r4, :], G[1][r4, :], AF.Exp, bias=bc_[r4, 2:3])
        K.activation(la[r4, :], la[r4, :], AF.Ln, bias=1.0)
        K.ts(dve, la[r4, :], la[r4, :], bc_[r4, 3:4], MUL)
        for c in range(NTT):
            cs = slice(c * 128, (c + 1) * 128)
            K.scan(gg[r4, cs], onerow[r4, 0:128], la[r4, cs], 0.0, MUL, ADD)
            K.ts(dve, wr[r4, cs], gg[r4, cs], -1.0, MUL, gg[r4, c * 128 + 127:c * 128 + 128], ADD)
        K.activation(eg[r4, :], gg[r4, :], AF.Exp)
        K.activation(wr[r4, :], wr[r4, :], AF.Exp)
        K.tt(dve, wr[r4, :], wr[r4, :], bet[r4, :], MUL)
        egl = tmp([128, 2, NTT], F32, "gegl")
        for p in range(2):
            K.mm(P[2][:, 256 + p * 8:256 + p * 8 + NTT], sel[:, 2 + p, :], eg[:, 127::128])
        K.copy(act, egl[:, :, :], P[2][:, 256:272].re("p (a c) -> p a c", a=2)[:, :, 0:NTT])
        MFLAG["g"] = True
        while not MFLAG["m"]:
            if getattr(K, "switch_hook", None) is None:
                break
            K.switch_hook()
        PB = P[5][:, :]
        for c in range(NTT):
            cs = slice(c * 128, (c + 1) * 128)
            for qi, srct in enumerate((bet, eg, wr)):
                K.tr(P[4][:, 32 + qi * 4:32 + qi * 4 + 4], srct[r4, cs], ident[32:36, 32:36])
            TT = tmp([128, 16], F32, "gTT"); K.copy(act, TT[:, 0:12], P[4][:, 32:44])
            K.ts(dve, TT[:, 12:16], TT[:, 4:8], -1.0, MUL)
            ktm = tmp([128, 256], BF16, "gktm"); vtm = tmp([128, 256], F32, "gvtm")
            for p in range(2):
                K.tr(PB[:, p * 128:(p + 1) * 128], qkT[2 + p][:, cs], identb[:, :])
                K.tr(P[6][:, p * 128:(p + 1) * 128], cvs[4 + p][:, cs], ident[:, :])
            K.copy(act, ktm[:, :], PB[:, 0:256]); K.copy(act, vtm[:, :], P[6][:, 0:256])
            Tb = tmp([128, 4, 128], BF16, "gTb"); QKb = tmp([128, 4, 128], BF16, "gQKb")
            def par(h, T_, PP):
                rs = slice((h % 2) * 64, (h % 2) * 64 + 64)
                kT_h = qkT[2 + h // 2][rs, cs]; qT_h = qkT[h // 2][rs, cs]
                yield K.mm(PP[:, 0:128], gg[:, cs], sel[:, 4 + h, :], start=True, stop=False)
                yield K.mm(PP[:, 0:128], sel[:, 8 + h, :], gg[:, cs], start=False, stop=False)
                yield K.mm(PP[:, 0:128], ident[:, :], mask[:, 1, :], start=False, stop=True)
                ET = T_("gET"); ETs = T_("gETs")
                yield K.activation(ET, PP[:, 0:128], AF.Exp)
                yield K.tt(dve, ETs, ET, mask[:, 2, :], MUL)
                yield K.mm(PP[:, 128:256], kT_h, kT_h)
                yield K.mm(PP[:, 256:384], kT_h, qT_h)
                At = T_("gAt")
                yield K.stt(dve, At, PP[:, 128:256], TT[:, h:h + 1], ETs, MUL, MUL)
                yield K.stt(dve, QKb[:, h, :], PP[:, 256:384], TT[:, h:h + 1], ET, MUL, MUL)
                yield K.tr(PP[:, 384:512], At, ident[:, :])
                Cm = [T_("gC0"), T_("gC1")]; Bm = [T_("gB0"), T_("gB1")]
                Aoff = T_("gAoff")
                yield K.tt(dve, Cm[0], PP[:, 384:512], mask[:, 3, :], MUL)
                yield K.tt(dve, Aoff, PP[:, 384:512], mask[:, 4, :], MUL)
                yield K.tt(dve, Bm[0], At, mask[:, 3, :], MUL)
                Pt = T_("gPt")
                yield K.tt(dve, Pt, ident[:, :], Bm[0], SUB)
                cur = 0
                for k_ in range(5):
                    nxt = 1 - cur
                    yield K.mm(PP[:, 0:128], Bm[cur], Cm[cur])
                    if k_ < 4:
                        yield K.mm(PP[:, 128:256], Cm[cur], Bm[cur])
                    yield K.copy(act, Cm[nxt], PP[:, 0:128])
                    if k_ < 4:
                        yield K.copy(act, Bm[nxt], PP[:, 128:256])
                    yield K.mm(PP[:, 256:384], Cm[nxt], Pt)
                    yield K.tt(dve, Pt, Pt, PP[:, 256:384], ADD)
                    cur = nxt
                X = T_("gET"); Pn = T_("gETs")
                yield K.mm(PP[:, 0:128], Aoff, Pt)
                yield K.copy(act, X, PP[:, 0:128])
                yield K.tr(PP[:, 128:256], Pt, ident[:, :])
                yield K.copy(act, Pn, PP[:, 128:256])
                yield K.mm(PP[:, 256:384], Pn, X)
                yield K.tt(dve, Tb[:, h, :], Pt, PP[:, 256:384], SUB)

            b8 = tmp([128, 2048], F32, "big8k")
            b4 = [tmp([128, 1024], F32, "big4k%d" % i_) for i_ in range(2)]
            names1 = ["gET", "gETs", "gAt", "gC0", "gC1", "gB0", "gB1", "gAoff", "gPt"]
            names8 = ["gET", "gETs", "gC0", "gC1", "gB0", "gB1", "gAoff", "gPt"]

            def set0(nm):
                return tmp([128, 128], F32, nm)[:, :]

            def set1(nm):
                k_ = names1.index(nm)
                return V(b8, b8.t[:, k_ * 128:(k_ + 1) * 128])

            def mkset(tl_):
                def f(nm):
                    k_ = names8.index("gC1" if nm == "gAt" else nm)
                    return V(tl_, tl_.t[:, k_ * 128:(k_ + 1) * 128])
                return f
            sets = [set0, set1, mkset(b4[0]), mkset(b4[1])]
            PPs = [P[7], P[2], P[6], P[3]]

            def rr(gens_):
                alive = list(gens_)
                while alive:
                    for g_ in list(alive):
                        try:
                            next(g_)
                        except StopIteration:
                            alive.remove(g_)
            rr([par(h_, sets[h_], PPs[h_]) for h_ in range(4)])
            O4 = tmp([128, 4, 64], F32, "gO4")

            def seq(h, T_):
                p = h // 2; rs = slice((h % 2) * 64, (h % 2) * 64 + 64)
                kT_h = qkT[2 + p][rs, cs]; qT_h = qkT[p][rs, cs]
                Sb = st["GSb"][p]; Sf = st["GS"][p]
                PQ = P[6] if h % 2 == 0 else P[3]
                o_ = (h // 2) * 256
                QSe = T_("gET")[:, 0:64]
                bfv = T_("gETs").bitcast(BF16)
                R = bfv[:, 0:64]; Ub = bfv[:, 64:128]; Uw = bfv[:, 128:192]
                yield K.mm(PQ[:, o_:o_ + 64], kT_h, Sb[rs, :])
                yield K.mm(PQ[:, o_ + 64:o_ + 128], qT_h, Sb[rs, :])
                yield K.stt(dve, R, PQ[:, o_:o_ + 64], TT[:, 12 + h:13 + h], vtm[:, h * 64:(h + 1) * 64], MUL, ADD)
                yield K.ts(dve, QSe, PQ[:, o_ + 64:o_ + 128], TT[:, 4 + h:5 + h], MUL)
                yield K.mm(PQ[:, o_ + 128:o_ + 192], Tb[:, h, :], R)
                yield K.copy(act, Ub, PQ[:, o_ + 128:o_ + 192])
                yield K.ts(dve, Uw, PQ[:, o_ + 128:o_ + 192], TT[:, 8 + h:9 + h], MUL)
                yield K.mm(PQ[:, o_ + 192:o_ + 256], QKb[:, h, :], Ub)
                yield K.tt(dve, O4[:, h, :], PQ[:, o_ + 192:o_ + 256], QSe, ADD)
                yield K.mm(P[4][:, 64 + h * 64:64 + (h + 1) * 64], ktm[:, p * 128:(p + 1) * 128], Uw)
                yield K.stt(dve, Sf[rs, :], Sf[rs, :], egl[rs, p, c:c + 1], P[4][rs, 64 + h * 64:64 + (h + 1) * 64], MUL, ADD)
                yield K.copy(act, Sb[rs, :], Sf[rs, :])
            rr([seq(h_, sets[h_]) for h_ in range(4)])
            yield
            sq = tmp([128, 4, 64], F32, "gosq"); ss = tmp([128, 4], F32, "goss")
            K.tt(dve, sq[:, :, :], O4[:, :, :], O4[:, :, :], MUL)
            K.reduce(dve, ss[:, :], sq[:, :, :], ADD)
            K.activation(ss[:, :], ss[:, :], AF.Sqrt, scale=1.0 / 64, bias=EPS)
            K.recip(ss[:, :], ss[:, :])
            K.tt(dve, O4[:, :, :], O4[:, :, :], ss[:, :].un(2).bct([128, 4, 64]), MUL)
            ob = tmp([128, 4, 64], BF16, "gob")
            K.tt(dve, ob[:, :, :], O4[:, :, :], GDNN[l][:, :].un(1).bct([128, 4, 64]), MUL)
            for p in range(2):
                K.tr(PB[:, 512 + p * 128:512 + (p + 1) * 128], ob[:, 2 * p:2 * p + 2, :].re("p h d -> p (h d)"), identb[:, :])
            for p in range(2):
                K.tt(dve, mixT[:, 6 + p, cs], PB[:, 512 + p * 128:512 + (p + 1) * 128], cgT[p][:, cs], MUL)

    tabring = [tabring_pre, K.sb([128, 2, N], F32, "tabr1")]
    tabc = [0]

    def s5_prompt(l, bi):
        s = S5[l]
        BTr = wload(s["BT"][0], s["BT"][1][:, 0], 16, 128); BTi = wload(s["BT"][0], s["BT"][1][:, 1], 16, 128)
        CTr = wload(s["CT"][0], s["CT"][1][:, 0], 16, 128); CTi = wload(s["CT"][0], s["CT"][1][:, 1], 16, 128)
        g0r = tmp([128, 16], F32, "g0r"); g0i = tmp([128, 16], F32, "g0i"); t_ = tmp([128, 16], F32, "g0t")
        K.tt(dve, g0r[:, :], s["ur"][:, :], s["Hre"][:, :], MUL); K.tt(dve, t_[:, :], s["ui"][:, :], s["Him"][:, :], MUL)
        K.tt(dve, g0r[:, :], g0r[:, :], t_[:, :], SUB)
        K.tt(dve, g0i[:, :], s["ui"][:, :], s["Hre"][:, :], MUL); K.tt(dve, t_[:, :], s["ur"][:, :], s["Him"][:, :], MUL)
        K.tt(dve, g0i[:, :], g0i[:, :], t_[:, :], ADD)
        ger = tmp([128, 16], F32, "ger"); gei = tmp([128, 16], F32, "gei")
        gyT = [tmp([128, N], BF16, "gyT%d" % i) for i in range(4)]
        for ot in range(4):
            for ii in range(4):
                i = ot * 4 + ii
                tab = tabring[tabc[0] % 2]; tabc[0] += 1
                K.dma_tile(sp, tab[:, :, :], V(s["TAB"][0], s["TAB"][1][i]))
                co = tab[:, 0, :]; si = tab[:, 1, :]
                K.mm(P[1][:, 0:N], BTr[:, i, :], uT[ot][:, :])
                K.mm(P[1][:, N:2 * N], BTi[:, i, :], uT[ot][:, :])
                a_ = P[1][:, 0:N]; b_ = P[1][:, N:2 * N]
                t1 = tmp([128, N], F32, "s5t1"); t2 = tmp([128, N], F32, "s5t2")
                t3 = tmp([128, N], F32, "s5t3"); t4 = tmp([128, N], F32, "s5t4")
                K.tt(dve, t1[:, :], a_, co, MUL); K.tt(dve, t2[:, :], b_, si, MUL)
                K.tt(pool, t1[:, :], t1[:, :], t2[:, :], ADD)
                K.tt(dve, t3[:, :], b_, co, MUL); K.tt(dve, t4[:, :], a_, si, MUL)
                K.tt(pool, t3[:, :], t3[:, :], t4[:, :], SUB)
                gr = tmp([128, N], F32, "s5gr"); gi_ = tmp([128, N], F32, "s5gi")
                K.scan(gr[:, :], s["mag"][:, i:i + 1].bc([128, N]), t1[:, :], g0r[:, i:i + 1], MUL, ADD)
                K.scan(gi_[:, :], s["mag"][:, i:i + 1].bc([128, N]), t3[:, :], g0i[:, i:i + 1], MUL, ADD)
                K.copy(act, ger[:, i:i + 1], gr[:, N - 1:N]); K.copy(act, gei[:, i:i + 1], gi_[:, N - 1:N])
                hr = tmp([128, N], BF16, "s5hr"); hi = tmp([128, N], BF16, "s5hi")
                K.tt(dve, t1[:, :], gr[:, :], co, MUL); K.tt(dve, t2[:, :], gi_[:, :], si, MUL)
                K.tt(dve, hr[:, :], t1[:, :], t2[:, :], SUB)
                K.tt(pool, t3[:, :], gr[:, :], si, MUL); K.tt(pool, t4[:, :], gi_[:, :], co, MUL)
                K.tt(pool, hi[:, :], t3[:, :], t4[:, :], ADD)
                K.mm(P[0][:, 0:N], CTr[:, i, :], hr[:, :], start=(ii == 0), stop=False)
                K.mm(P[0][:, 0:N], CTi[:, i, :], hi[:, :], start=False, stop=(ii == 3))
                yield
            yv = tmp([128, N], F32, "s5yv")
            K.stt(dve, yv[:, :], uT[ot][:, :], Dcol[l][:, ot:ot + 1], P[0][:, 0:N], MUL, ADD)
            K.activation(gyT[ot][:, :], yv[:, :], AF.Gelu)
        t_ = tmp([128, 16], F32, "g0t")
        K.tt(dve, s["Hre"][:, :], ger[:, :], s["CL"][:, :], MUL); K.tt(dve, t_[:, :], gei[:, :], s["SL"][:, :], MUL)
        K.tt(dve, s["Hre"][:, :], s["Hre"][:, :], t_[:, :], SUB)
        K.tt(dve, s["Him"][:, :], ger[:, :], s["SL"][:, :], MUL); K.tt(dve, t_[:, :], gei[:, :], s["CL"][:, :], MUL)
        K.tt(dve, s["Him"][:, :], s["Him"][:, :], t_[:, :], ADD)
        if bi == NRUN - 1:
            store(O["s5r_p"][l].rearrange("(i g) n -> (g n) i", g=2), s["Hre"][:, :], allow_slow_non_contiguous=True)
            store(O["s5i_p"][l].rearrange("(i g) n -> (g n) i", g=2), s["Him"][:, :], allow_slow_non_contiguous=True)
        glu(l, gyT, N)

    def glu(l, gyT, n):
        def evac(ot, ps):
            sg = tmp([128, N], F32, "glusg")
            K.activation(sg[:, 0:n], ps, AF.Sigmoid)
            K.tt(dve, mixT[:, 2 + ot, 0:n], gyT[ot][:, 0:n], sg[:, 0:n], MUL)
        proj_fm(l, "glu", 0, 4, lambda kt: gyT[kt][:, 0:n], evac, n, wide=False)

    def resid_proj(l, name, src, nk, n):
        def evac(ot, ps):
            K.tt(dve, xT[ot][:, 0:n], xT[ot][:, 0:n], ps, ADD)
        proj_fm(l, name, 0, 8, lambda kt: src[:, kt, 0:n], evac, n)

    actT = K.sb([128, 22, N], BF16, "actT")
    qT_x = actT[:, 0:8, :]; oT_x = actT[:, 8:16, :]

    def xattn_prompt(l):
        rmsnorm_fm(xT, gxat[l], xn, N)
        proj_fm(l, "mq", 0, 8, lambda kt: xn[:, kt, :], lambda ot, ps: K.copy(act, qT_x[:, ot, :], ps), N)
        b4_ = [tmp([128, 1024], F32, "big4k%d" % i_) for i_ in range(2)]
        b8_ = tmp([128, 2048], F32, "big8k")

        def bfview(tl_, shape_str, **kw):
            return V(tl_, tl_.t[:, 0:512].bitcast(BF16).rearrange(shape_str, **kw))

        def xtile(c, PA0, PA1, PB, pr, pT, on):
            cs = slice(c * 128, (c + 1) * 128)
            mx = tmp([128, 4], F32, "xmx%d" % c); sm = tmp([128, 4], F32, "xsm%d" % c)
            for h in range(4):
                pt = PA0 if h < 2 else PA1
                o = (h % 2) * 256
                yield K.mm(pt[:, o:o + 256], qT_x[:, 2 * h, cs], MKT[l][:, 2 * h, :], start=True, stop=False)
                yield K.mm(pt[:, o:o + 256], qT_x[:, 2 * h + 1, cs], MKT[l][:, 2 * h + 1, :], start=False, stop=True)
            for hp in range(2):
                pt = PA0 if hp == 0 else PA1
                yield K.reduce(dve, mx[:, 2 * hp:2 * hp + 2], pt[:, :].re("p (h m) -> p h m", h=2), MAX)
            yield K.ts(dve, mx[:, :], mx[:, :], -1.0 / 16.0, MUL)
            for h in range(4):
                pt = PA0 if h < 2 else PA1
                o = (h % 2) * 256
                yield K.activation(pr[:, h, :], pt[:, o:o + 256], AF.Exp, bias=mx[:, h:h + 1], scale=1.0 / 16.0, accum=sm[:, h:h + 1])
            yield K.recip(sm[:, :], sm[:, :])
            for h in range(4):
                for mt in range(2):
                    yield K.tr(PB[:, (h * 2 + mt) * 128:(h * 2 + mt + 1) * 128], pr[:, h, mt * 128:(mt + 1) * 128], identb[:, :])
            yield K.copy(act, pT[:, :, :], PB[:, :].re("p (a c) -> p a c", a=8))
            for h in range(4):
                pt = PA0 if h < 2 else PA1
                o = (h % 2) * 256
                for mt in range(2):
                    yield K.mm(pt[:, o:o + 256], pT[:, h * 2 + mt, :], MVB[l][:, mt, h * 256:(h + 1) * 256], start=(mt == 0), stop=(mt == 1))
            for hp in range(2):
                pt = PA0 if hp == 0 else PA1
                yield K.tt(dve, on[:, 2 * hp:2 * hp + 2, :], pt[:, :].re("p (h m) -> p h m", h=2),
                           sm[:, 2 * hp:2 * hp + 2].un(2).bct([128, 2, 256]), MUL)
            for k8 in range(8):
                yield K.tr(PB[:, k8 * 128:(k8 + 1) * 128], on[:, :, :].re("p h m -> p (h m)")[:, k8 * 128:(k8 + 1) * 128], identb[:, :])
            yield K.copy(act, oT_x[:, :, cs], PB[:, :].re("p (a c) -> p a c", a=8))

        t0 = xtile(0, P[6], P[7], P[5][:, :], tmp([128, 4, 256], BF16, "xpr"), tmp([128, 8, 128], BF16, "xpT"), tmp([128, 4, 256], BF16, "xon"))
        if NTT == 2 and int(_os.environ.get('KXILV', 1)):
            t1 = xtile(1, P[3], P[4], V(P[2], P[2].t[:, :].bitcast(BF16)),
                       bfview(b4_[0], "p (h m) -> p h m", h=4), bfview(b4_[1], "p (a c) -> p a c", a=8),
                       bfview(b8_, "p (h m) -> p h m", h=4))
            alive = [t0, t1]
            while alive:
                for g_ in list(alive):
                    try:
                        next(g_)
                    except StopIteration:
                        alive.remove(g_)
        else:
            for _ in t0:
                pass
            for c in range(1, NTT):
                for _ in xtile(c, P[6], P[7], P[5][:, :], tmp([128, 4, 256], BF16, "xpr"), tmp([128, 8, 128], BF16, "xpT"), tmp([128, 4, 256], BF16, "xon")):
                    pass
        resid_proj(l, "mo", oT_x, 8, N)

    def ffn(l, n):
        rmsnorm_fm(xT, gffn[l], xn, n)
        res_g, dst_g, _, _ = WS[l]["gate"]; res_u, dst_u, _, _ = WS[l]["up"]
        for u0 in range(0, 22, 2):
            nct = min(2, 22 - u0)
            wg_ = wload(res_g, dst_g[:, :, u0 * 128:(u0 + nct) * 128], 8, nct * 128)
            wu_ = wload(res_u, dst_u[:, :, u0 * 128:(u0 + nct) * 128], 8, nct * 128)
            for c in range(nct):
                pg = pjw()
                for kt in range(8):
                    K.mm(pg[:, 0:n], wg_[:, kt, c * 128:(c + 1) * 128], xn[:, kt, 0:n], start=(kt == 0), stop=(kt == 7))
                pu = pjw()
                for kt in range(8):
                    K.mm(pu[:, 0:n], wu_[:, kt, c * 128:(c + 1) * 128], xn[:, kt, 0:n], start=(kt == 0), stop=(kt == 7))
                sg = tmp([128, N], F32, "ffsg")
                K.activation(sg[:, 0:n], pg[:, 0:n], AF.Silu)
                K.tt(dve, actT[:, u0 + c, 0:n], sg[:, 0:n], pu[:, 0:n], MUL)
        resid_proj_k(l, "down", actT, 22, n)

    def resid_proj_k(l, name, src, nk, n):
        res, dst, nk_, tot = WS[l][name]
        for ot in range(8):
            hk = nk // 2
            wa = wload(res, dst[:, 0:hk, ot * 128:(ot + 1) * 128], hk, 128)
            wb = wload(res, dst[:, hk:nk, ot * 128:(ot + 1) * 128], nk - hk, 128)
            ps = pjw()
            for kt in range(nk):
                wv_ = wa[:, kt, :] if kt < hk else wb[:, kt - hk, :]
                K.mm(ps[:, 0:n], wv_, src[:, kt, 0:n], start=(kt == 0), stop=(kt == nk - 1))
            K.tt(dve, xT[ot][:, 0:n], xT[ot][:, 0:n], ps[:, 0:n], ADD)

    def final_out(n, dst_rows):
        rmsnorm_fm(xT, gfin, xn, n)

    def final_norm_store(n, dst):
        ps = pj()
        for kt in range(8):
            sq = tmp([128, N], BF16, "sq%d" % (kt % 2))
            K.activation(sq[:, 0:n], xT[kt][:, 0:n], AF.Square)
            K.mm(ps[:, 0:n], onesb[:, :], sq[:, 0:n], start=(kt == 0), stop=(kt == 7))
        rstd = tmp([128, N], F32, "rstd")
        K.activation(rstd[:, 0:n], ps[:, 0:n], AF.Sqrt, scale=1.0 / D, bias=EPS)
        K.recip(rstd[:, 0:n], rstd[:, 0:n])
        yf = [tmp([128, N], F32, ("gcv%d" % kt) if kt < 6 else ("s5gr" if kt == 6 else "s5gi")) for kt in range(8)]
        for kt in range(8):
            K.stt(dve, yf[kt][:, 0:n], xT[kt][:, 0:n], gfin[:, kt:kt + 1], rstd[:, 0:n], MUL, MUL)
        for c in range((n + 127) // 128):
            w = min(128, n - c * 128)
            stg = tmp([128, 1024], F32, "big4k%d" % (c % 2))
            for half in range(2):
                pt = P[3 + half]
                for k4 in range(4):
                    K.tr(pt[0:w, k4 * 128:(k4 + 1) * 128], yf[half * 4 + k4][:, c * 128:c * 128 + w], ident[:, :])
                K.copy(act, stg[0:w, half * 512:(half + 1) * 512], pt[0:w, :])
            store(dst[c * 128:c * 128 + w, :], stg[0:w, :])

    def load_x(src, n):
        for c in range((n + 127) // 128):
            w = min(128, n - c * 128)
            xin = tmp([128, 1024], F32, "big4k%d" % (c % 2))
            ld(xin[0:w, :], src[c * 128:c * 128 + w, :])
            for half in range(2):
                pt = P[3 + half]
                for k4 in range(4):
                    kt = half * 4 + k4
                    K.tr(pt[:, k4 * 128:k4 * 128 + w], xin[0:w, kt * 128:(kt + 1) * 128], ident[0:w, 0:w])
                for k4 in range(4):
                    K.copy(act, xT[half * 4 + k4][:, c * 128:c * 128 + w], pt[:, k4 * 128:k4 * 128 + w])

    import os
    NRUN = int(os.environ.get('KRUN', NB)); LRUN = int(os.environ.get('KLRUN', LYR)); PH = int(os.environ.get('KPH', 9))
    for bi in range(NRUN):
        load_x(I["xp"][bi * N:(bi + 1) * N, :], N)
        for l in range(LRUN):
            if bi == 0 and l == 1:
                cast_group(1, 0); s5_setup(1); cast_group(1, 1); cast_group(1, 2)
            rmsnorm_fm(xT, gmix[l], xn, N)
            inproj(l, N, True)
            def chainA(l=l, bi=bi):
                if PH >= 1:
                    for _ in mlstm_prompt(l, bi):
                        pass
                MFLAG["m"] = True

            def chainA2(l=l, bi=bi):
                if PH >= 3:
                    for _ in gdn_prompt(l, bi):
                        pass

            def chainB(l=l, bi=bi):
                if PH >= 2:
                    for _ in s5_prompt(l, bi):
                        pass
            if int(_os.environ.get('KILV', 1)):
                MFLAG["m"] = False
                Coop(K, [chainA, chainA2, chainB]).run()
            else:
                chainA(); chainA2(); chainB()
            if PH >= 4: resid_proj(l, "out", mixT, 8, N)
            if PH >= 5: xattn_prompt(l)
            if PH >= 6: ffn(l, N)
        final_norm_store(N, O["y_p"][bi * N:(bi + 1) * N, :])
    for l in range(LYR if NRUN > 0 else 0):
        st = ST[l]
        for h in range(4):
            p = h // 2; rs = slice((h % 2) * 64, (h % 2) * 64 + 64)
            store(O["c_p"][l, h], st["CA"][p][rs, 0:64])
            store(O["n_p"][l, h].rearrange("(d o) -> d o", o=1), st["CA"][p][rs, 64:65])
            store(O["g_p"][l, h], st["GS"][p][rs, :])
        m4 = tmp([128, 1], F32, "m4")
        K.tt(dve, m4[0:4, :], st["car"][0:4, 1:2], st["car"][0:4, 0:1], SUB)
        store(O["m_p"][l].rearrange("(h o) -> h o", o=1), m4[0:4, :])


    ZSD = dscr([NS, 4, 3, 64], F32); GSD = dscr([NS, 4, 4], F32); GQD = dscr([NS, 4, 3, 64], F32)
    HSD = dscr([2, NS, 4, 64], F32); LAMD = dscr([2, 2048], F32); QSD = dscr([NS, 1024], F32); OSD = dscr([NS, 1024], F32)
    zres, gres, gqres, hres, lres, qres, ores = (newres("sres") for _ in range(7))
    actf = V(actT, actT.t[:, :, :].rearrange("p a b -> p (a b)").bitcast(F32))
    n = NS
    BH = 64

    def sample_layer(l):
        st = ST[l]; bc_ = bcol[l]; s5 = S5[l]
        gcvt = [tmp([128, N], F32, "gcv%d" % i_) for i_ in range(6)]

        def q64(ti, qi):
            return V(gcvt[ti], gcvt[ti].t[0:BH, qi * 64:(qi + 1) * 64])
        rmsnorm_fm(xT, gmix[l], xn, n)
        inproj(l, n, False)
        ztok = V(big4k[0], big4k[0].t[0:NS, 0:768])
        res, dst, nk, tot = WS[l]["in_tm"]
        for (c0, w) in ((0, 256), (256, 256), (512, 256)):
            wv = wload(res, dst[:, :, c0:c0 + w], 8, w)
            ps = pj()
            for kt in range(8):
                K.mm(ps[0:NS, 0:w], xn[:, kt, 0:NS], wv[:, kt, :], start=(kt == 0), stop=(kt == 7))
            K.copy(act, ztok[:, c0:c0 + w], ps[0:NS, 0:w])
        for j in range(3):
            K.dma_tile(sp, V(zres, ZSD[:, :, j, :]), ztok[:, j * 256:(j + 1) * 256].re("b (h d) -> b h d", h=4))
        ic, l1, be, la = gsc[0], gsc[1], gsc[2], gsc[3]
        r4 = slice(0, 4); g4 = slice(32, 36); cn = slice(0, n)
        K.activation(ic[r4, cn], G[0][r4, cn], AF.Identity, bias=bc_[r4, 0:1])
        K.activation(l1[r4, cn], G[1][r4, cn], AF.Exp, bias=bc_[r4, 1:2], scale=-1.0)
        K.activation(l1[r4, cn], l1[r4, cn], AF.Ln, bias=1.0)
        K.ts(dve, l1[r4, cn], l1[r4, cn], -1.0, MUL)
        K.activation(be[g4, cn], G[0][g4, cn], AF.Sigmoid)
        K.activation(la[g4, cn], G[1][g4, cn], AF.Exp, bias=bc_[g4, 2:3])
        K.activation(la[g4, cn], la[g4, cn], AF.Ln, bias=1.0)
        K.ts(dve, la[g4, cn], la[g4, cn], bc_[g4, 3:4], MUL)
        for j, (t_, rows) in enumerate(((ic, r4), (l1, r4), (be, g4), (la, g4))):
            K.dma_tile(sp, V(gres, GSD[:, :, j].rearrange("b h -> h b")), t_[rows, cn], allow_slow_non_contiguous=True)
        gts = tmp([BH, 4], F32, "s_gts"); qkv = V(gcvt[0], gcvt[0].t[0:BH, 0:192].rearrange("p (j d) -> p j d", j=3))
        ld(gts[:, :], V(gres, GSD.rearrange("b h j -> (b h) j")))
        ld(qkv[:, :, :], V(zres, ZSD.rearrange("b h j d -> (b h) j d")))
        m0 = tmp([BH, 1], F32, "s_m0"); ld(m0[:, :], I["st_m"][l].rearrange("b (h o) -> (b h) o", o=1))
        def chainA():
            t1 = tmp([BH, 4], F32, "s_t1")
            K.tt(dve, t1[:, 0:1], gts[:, 1:2], m0[:, :], ADD)
            K.tt(dve, t1[:, 1:2], t1[:, 0:1], gts[:, 0:1], MAX)
            K.tt(dve, t1[:, 2:3], gts[:, 0:1], t1[:, 1:2], SUB); K.activation(t1[:, 2:3], t1[:, 2:3], AF.Exp)
            K.tt(dve, t1[:, 3:4], t1[:, 0:1], t1[:, 1:2], SUB); K.activation(t1[:, 3:4], t1[:, 3:4], AF.Exp)
            store(O["m_s"][l].rearrange("b (h o) -> (b h) o", o=1), t1[:, 1:2])
            kt_ = q64(2, 0); qt_ = q64(2, 1); va = tmp([BH, 65], F32, "s_va")
            K.ts(dve, kt_[:, :], qkv[:, 1, :], t1[:, 2:3], MUL)
            K.ts(dve, qt_[:, :], qkv[:, 0, :], 0.125, MUL)
            K.memset(dve, va[:, 64:65], 1.0); K.copy(act, va[:, 0:64], qkv[:, 2, :])
            nacc = tmp([BH, 65], F32, "s_nacc"); red = tmp([BH, 65], F32, "s_red")
            cq = actf[0:BH, 0:1040].re("p (d e) -> p d e", d=16); wq = actf[0:BH, 1040:2080].re("p (d e) -> p d e", d=16)
            cview = I["st_c"][l].rearrange("b h d e -> (b h) d e"); nview = I["st_n"][l].rearrange("b h d -> (b h) d")
            coutv = O["c_s"][l].rearrange("b h d e -> (b h) d e"); noutv = O["n_s"][l].rearrange("b h d -> (b h) d")
            for dq in range(4):
                ds = slice(dq * 16, dq * 16 + 16)
                ld(cq[:, :, 0:64], cview[:, ds, :]); ld(cq[:, :, 64], nview[:, ds], allow_slow_non_contiguous=True)
                K.tt(dve, wq, kt_[:, ds].un(2).bct([BH, 16, 65]), va[:, :].un(1).bct([BH, 16, 65]), MUL)
                K.stt(dve, cq, cq, t1[:, 3:4], wq, MUL, ADD)
                store(coutv[:, ds, :], cq[:, :, 0:64])
                if int(_os.environ.get('KDBG', 0)) != 2: store(noutv[:, ds], cq[:, :, 64], allow_slow_non_contiguous=True)
                K.tt(dve, wq, cq, qt_[:, ds].un(2).bct([BH, 16, 65]), MUL)
                K.reduce(dve, red[:, :] if dq else nacc[:, :], wq.re("p d e -> p e d"), ADD)
                if dq:
                    K.tt(dve, nacc[:, :], nacc[:, :], red[:, :], ADD)
            dn = tmp([BH, 2], F32, "s_dn")
            K.activation(dn[:, 0:1], nacc[:, 64:65], AF.Abs)
            K.activation(dn[:, 1:2], t1[:, 1:2], AF.Exp, scale=-1.0)
            K.tt(dve, dn[:, 0:1], dn[:, 0:1], dn[:, 1:2], MAX); K.recip(dn[:, 0:1], dn[:, 0:1])
            hv = q64(3, 3); hsq = q64(4, 0); hss = tmp([BH, 1], F32, "s_hss")
            K.ts(dve, hv[:, :], nacc[:, 0:64], dn[:, 0:1], MUL)
            K.activation(hsq[:, :], hv[:, :], AF.Square, accum=hss[:, :])
            K.activation(hss[:, :], hss[:, :], AF.Sqrt, scale=1.0 / 64, bias=EPS); K.recip(hss[:, :], hss[:, :])
            K.ts(dve, hv[:, :], hv[:, :], hss[:, 0:1], MUL)
            K.dma_tile(sp, V(hres, HSD[0].rearrange("b h d -> (b h) d")), hv[:, :])
            XP = st["XP"]
            cvq = tmp([128, 6, NS], F32, "s_cvq"); xst = tmp([128, 6, NS], F32, "s_xst")
            cvs = [cvq[:, i_, :] for i_ in range(6)]
            gcv_in = I["st_gc"][l]
            for i in range(6):
                buf = tmp([128, 3, NS], F32, "s_buf%d" % (i % 2))
                for j in range(3):
                    ld(buf[:, j, :], gcv_in[:, j, i * 128:(i + 1) * 128].rearrange("b p -> p b"), allow_slow_non_contiguous=True)
                K.ts(dve, cvs[i], buf[:, 0, :], CW[l][:, i, 0:1], MUL)
                for j in range(1, 3):
                    K.stt(dve, cvs[i], buf[:, j, :], CW[l][:, i, j:j + 1], cvs[i], MUL, ADD)
                K.stt(dve, cvs[i], XP[i][:, 3:3 + n], CW[l][:, i, 3:4], cvs[i], MUL, ADD)
                if int(_os.environ.get('KDBG', 0)):
                    store(O["gc_s"][l][:, 0, i * 128:(i + 1) * 128].rearrange("b p -> p b"), cvs[i], allow_slow_non_contiguous=True)
                    store(O["gc_s"][l][:, 1, i * 128:(i + 1) * 128].rearrange("b p -> p b"), buf[:, 1, :], allow_slow_non_contiguous=True)
                K.activation(cvs[i], cvs[i], AF.Silu)
                K.copy(act, xst[:, i, :], XP[i][:, 3:3 + n])
                store(O["gc_s"][l][:, 2, i * 128:(i + 1) * 128].rearrange("b p -> p b"), xst[:, i, :], allow_slow_non_contiguous=True)
            if not int(_os.environ.get('KDBG', 0)):
                gcr = newres("gccp")
                K.dma_tile(pool, V(gcr, O["gc_s"][l][:, 0:2, :]), I["st_gc"][l][:, 1:3, :])
                outs_tiles.append(gcr)
            for i in range(4):
                sq = tmp([128, N], BF16, "gsq")
                K.activation(sq[:, cn], cvs[i], AF.Square)
                K.mm(P[7][:, 0:n], blk64[:, :], sq[:, cn])
                rs_ = tmp([128, N], F32, "grs")
                K.activation(rs_[:, cn], P[7][:, 0:n], AF.Sqrt, bias=EPS)
                K.recip(rs_[:, cn], rs_[:, cn])
                K.stt(dve, cvs[i], cvs[i], 0.125 if i < 2 else 1.0, rs_[:, cn], MUL, MUL)
            for j in range(3):
                for p in range(2):
                    for hh in range(2):
                        K.dma_tile(sp, V(gqres, GQD[:, 2 * p + hh, j, :].rearrange("b d -> d b")), cvs[2 * j + p][hh * 64:(hh + 1) * 64, :],
                                   allow_slow_non_contiguous=True)
            gq = V(gcvt[1], gcvt[1].t[0:BH, 0:192].rearrange("p (j d) -> p j d", j=3))
            ld(gq[:, :, :], V(gqres, GQD.rearrange("b h j d -> (b h) j d")))
            if int(_os.environ.get('KDBG', 0)) == 2:
                store(O["n_s"][l].rearrange("b h d -> (b h) d"), gq[:, 1, :])
            eg = tmp([BH, 2], F32, "s_eg")
            K.activation(eg[:, 0:1], gts[:, 3:4], AF.Exp); K.ts(dve, eg[:, 1:2], eg[:, 0:1], -1.0, MUL)
            kb = q64(2, 2); K.ts(dve, kb[:, :], gq[:, 1, :], gts[:, 2:3], MUL)
            sview = I["st_g"][l].rearrange("b h d e -> (b h) d e"); soutv = O["g_s"][l].rearrange("b h d e -> (b h) d e")
            sq_ = actf[0:BH, 0:1024].re("p (d e) -> p d e", d=16); wq2 = actf[0:BH, 1040:2064].re("p (d e) -> p d e", d=16)
            ks = q64(2, 3); red2 = q64(3, 0)
            for dq in range(4):
                ds = slice(dq * 16, dq * 16 + 16)
                ld(sq_, sview[:, ds, :])
                K.tt(dve, wq2, sq_, gq[:, 1, ds].un(2).bct([BH, 16, 64]), MUL)
                K.reduce(dve, red2[:, :] if dq else ks[:, :], wq2.re("p d e -> p e d"), ADD)
                if dq:
                    K.tt(dve, ks[:, :], ks[:, :], red2[:, :], ADD)
            U = q64(3, 1)
            K.stt(dve, U[:, :], ks[:, :], eg[:, 1:2], gq[:, 2, :], MUL, ADD)
            ov = q64(3, 2)
            for dq in range(4):
                ds = slice(dq * 16, dq * 16 + 16)
                ld(sq_, sview[:, ds, :])
                K.tt(dve, wq2, kb[:, ds].un(2).bct([BH, 16, 64]), U[:, :].un(1).bct([BH, 16, 64]), MUL)
                K.stt(dve, sq_, sq_, eg[:, 0:1], wq2, MUL, ADD)
                store(soutv[:, ds, :], sq_)
                K.tt(dve, wq2, sq_, gq[:, 0, ds].un(2).bct([BH, 16, 64]), MUL)
                K.reduce(dve, red2[:, :] if dq else ov[:, :], wq2.re("p d e -> p e d"), ADD)
                if dq:
                    K.tt(dve, ov[:, :], ov[:, :], red2[:, :], ADD)
            K.activation(hsq[:, :], ov[:, :], AF.Square, accum=hss[:, :])
            K.activation(hss[:, :], hss[:, :], AF.Sqrt, scale=1.0 / 64, bias=EPS); K.recip(hss[:, :], hss[:, :])
            K.ts(dve, ov[:, :], ov[:, :], hss[:, 0:1], MUL)
            K.tt(dve, ov[:, :], ov[:, :], GDNN[l][0:BH, :], MUL)
            K.dma_tile(sp, V(hres, HSD[1].rearrange("b h d -> (b h) d")), ov[:, :])
            gnc = tmp([128, 2], F32, "s_gnc")
            ld(gnc[:, :], I["mlstm_norm"][l].rearrange("(k p) -> p k", p=128), allow_slow_non_contiguous=True)
            for p in range(2):
                hf = tmp([128, NS], F32, "s_hf")
                ld(hf[:, :], V(hres, HSD[0][:, 2 * p:2 * p + 2, :].rearrange("b h d -> (h d) b")), allow_slow_non_contiguous=True)
                K.stt(dve, mixT[:, p, cn], hf[:, :], gnc[:, p:p + 1], aoT[p][:, cn], MUL, MUL)
                hf2 = tmp([128, NS], F32, "s_hf2")
                ld(hf2[:, :], V(hres, HSD[1][:, 2 * p:2 * p + 2, :].rearrange("b h d -> (h d) b")), allow_slow_non_contiguous=True)
                K.tt(dve, mixT[:, 6 + p, cn], hf2[:, :], cgT[p][:, cn], MUL)

        def chainB():
            BTr = wload(s5["BT"][0], s5["BT"][1][:, 0], 16, 128); BTi = wload(s5["BT"][0], s5["BT"][1][:, 1], 16, 128)
            CTr = wload(s5["CT"][0], s5["CT"][1][:, 0], 16, 128); CTi = wload(s5["CT"][0], s5["CT"][1][:, 1], 16, 128)
            K.dma_tile(sp, V(lres, LAMD[0].rearrange("(i p) -> p i", p=128)), s5["lre"][:, :], allow_slow_non_contiguous=True)
            K.dma_tile(sp, V(lres, LAMD[1].rearrange("(i p) -> p i", p=128)), s5["lim"][:, :], allow_slow_non_contiguous=True)
            hTb = [tmp([128, 16, NS], BF16, "s_hT%d" % r) for r in range(2)]
            for ot in range(4):
                fs = slice(ot * 512, (ot + 1) * 512)
                lr = V(big8k, big8k.t[0:NS, 0:512]); li = V(big8k, big8k.t[0:NS, 512:1024])
                ld(lr, V(lres, LAMD[0][fs].partition_broadcast(NS))); ld(li, V(lres, LAMD[1][fs].partition_broadcast(NS)))
                h0r = V(big8k, big8k.t[0:NS, 1024:1536]); h0i = V(big8k, big8k.t[0:NS, 1536:2048])
                ld(h0r, I["st_s5r"][l].rearrange("b g n -> b (g n)")[:, fs]); ld(h0i, I["st_s5i"][l].rearrange("b g n -> b (g n)")[:, fs])
                K.mm(P[3][0:NS, :], uT[ot][:, cn], BTr[:, 4 * ot:4 * ot + 4, :].re("p a c -> p (a c)"))
                K.mm(P[4][0:NS, :], uT[ot][:, cn], BTi[:, 4 * ot:4 * ot + 4, :].re("p a c -> p (a c)"))
                hr = V(big4k[1], big4k[1].t[0:NS, 0:512]); hi = V(big4k[1], big4k[1].t[0:NS, 512:1024]); tt_ = V(big4k[0], big4k[0].t[0:NS, 0:512])
                K.tt(dve, hr[:, :], lr, h0r, MUL); K.tt(dve, tt_, li, h0i, MUL)
                K.tt(dve, hr[:, :], hr[:, :], tt_, SUB); K.tt(dve, hr[:, :], hr[:, :], P[3][0:NS, :], ADD)
                K.tt(dve, hi[:, :], lr, h0i, MUL); K.tt(dve, tt_, li, h0r, MUL)
                K.tt(dve, hi[:, :], hi[:, :], tt_, ADD); K.tt(dve, hi[:, :], hi[:, :], P[4][0:NS, :], ADD)
                store(O["s5r_s"][l].rearrange("b g n -> b (g n)")[:, fs], hr[:, :])
                store(O["s5i_s"][l].rearrange("b g n -> b (g n)")[:, fs], hi[:, :])
                for r_, src in enumerate((hr, hi)):
                    for ii in range(4):
                        K.tr(P[6][:, (r_ * 4 + ii) * NS:(r_ * 4 + ii + 1) * NS], src[0:NS, ii * 128:(ii + 1) * 128], ident[0:NS, 0:NS])
                for r_ in range(2):
                    K.copy(act, hTb[r_][:, 4 * ot:4 * ot + 4, :], P[6][:, r_ * 4 * NS:(r_ + 1) * 4 * NS].re("p (a c) -> p a c", a=4))
            gyT = [tmp([128, N], BF16, "gyT%d" % i) for i in range(4)]
            for ot in range(4):
                for ii in range(4):
                    i = ot * 4 + ii
                    K.mm(P[4][:, 0:n], CTr[:, i, :], hTb[0][:, i, :], start=(ii == 0), stop=False)
                    K.mm(P[4][:, 0:n], CTi[:, i, :], hTb[1][:, i, :], start=False, stop=(ii == 3))
                yv = tmp([128, N], F32, "s5yv")
                K.stt(dve, yv[:, cn], uT[ot][:, cn], Dcol[l][:, ot:ot + 1], P[4][:, 0:n], MUL, ADD)
                K.activation(gyT[ot][:, cn], yv[:, cn], AF.Gelu)
            glu(l, gyT, n)

        if int(_os.environ.get('KSILV', 1)):
            Coop(K, [chainA, chainB]).run()
        else:
            chainA(); chainB()
        resid_proj(l, "out", mixT, 8, n)
        rmsnorm_fm(xT, gxat[l], xn, n)
        qtok = V(big4k[1], big4k[1].t[0:NS, :])
        res, dst, nk, tot = WS[l]["mq"]
        for ch in range(4):
            wv = wload(res, dst[:, :, ch * 256:(ch + 1) * 256], 8, 256)
            ps = pj()
            for kt in range(8):
                K.mm(ps[0:NS, 0:256], xn[:, kt, 0:NS], wv[:, kt, :], start=(kt == 0), stop=(kt == 7))
            K.copy(act, qtok[:, ch * 256:(ch + 1) * 256], ps[0:NS, 0:256])
        K.dma_tile(sp, V(qres, QSD), qtok)
        rf = [V(ring[i], ring[i].t[:, :].bitcast(F32)) for i in range(2)]
        kvb = [big4k[0], big4k[1], V(big8k, big8k.t[:, 0:1024]), V(big8k, big8k.t[:, 1024:2048])]
        for b in range(NS if int(_os.environ.get('KSATT', 1)) else 0):
            qb = rf[0][:, 0:1024]; prod = rf[1][:, 0:1024]
            ld(qb, V(qres, QSD[b].partition_broadcast(128)))
            sc = tmp([128, 2, 4], F32, "s_sc")
            for mt in range(2):
                ld(kvb[mt][:, :], I["ck"][l, b, mt * 128:(mt + 1) * 128, :])
            for mt in range(2):
                ld(kvb[2 + mt][:, :], I["cv"][l, b, mt * 128:(mt + 1) * 128, :])
            for mt in range(2):
                kt_b = kvb[mt]
                K.tt(dve, prod, kt_b[:, :], qb, MUL)
                K.reduce(dve, sc[:, mt, :], prod.re("p (h d) -> p h d", h=4), ADD)
            for mt in range(2):
                K.tr(P[6][0:4, mt * 128:(mt + 1) * 128], sc[:, mt, :], ident[:, :])
            mx = tmp([4, 2], F32, "s_mx"); pe_ = actf[0:4, 1024:1280]
            K.reduce(dve, mx[:, 0:1], P[6][0:4, 0:256], MAX)
            K.ts(dve, mx[:, 0:1], mx[:, 0:1], -1.0 / 16.0, MUL)
            K.activation(pe_[:, :], P[6][0:4, 0:256], AF.Exp, bias=mx[:, 0:1], scale=1.0 / 16.0, accum=mx[:, 1:2])
            K.recip(mx[:, 1:2], mx[:, 1:2])
            K.ts(dve, pe_[:, :], pe_[:, :], mx[:, 1:2], MUL)
            pT = tmp([128, 2, 4], F32, "s_pT")
            for mt in range(2):
                K.tr(P[7][:, mt * 4:(mt + 1) * 4], pe_[:, mt * 128:(mt + 1) * 128], ident[0:4, 0:4])
            K.copy(act, pT[:, :, :], P[7][:, 0:8].re("p (a c) -> p a c", a=2))
            for h in range(4):
                pt_ = P[3] if h < 2 else P[4]
                for mt in range(2):
                    K.mm(pt_[0:1, (h % 2) * 256:(h % 2 + 1) * 256], pT[:, mt, h:h + 1], kvb[2 + mt][:, h * 256:(h + 1) * 256],
                         start=(mt == 0), stop=(mt == 1))
            orow = actf[0:1, 0:1024]
            K.copy(act, orow[:, 0:512], P[3][0:1, :]); K.copy(act, orow[:, 512:1024], P[4][0:1, :])
            K.dma_tile(sp, V(ores, OSD[b:b + 1, :]), orow)
        otok = V(big4k[0], big4k[0].t[0:NS, :])
        ld(otok, V(ores, OSD))
        for half in range(2):
            for k4 in range(4):
                K.tr(P[6][:, k4 * NS:(k4 + 1) * NS], otok[:, (half * 4 + k4) * 128:(half * 4 + k4 + 1) * 128], ident[0:NS, 0:NS])
            K.copy(act, oT_x[:, half * 4:half * 4 + 4, cn], P[6][:, 0:4 * NS].re("p (a c) -> p a c", a=4))
        resid_proj(l, "mo", oT_x, 8, n)
        ffn(l, n)

    if len(S5) < 2:
        cast_group(1, 0); s5_setup(1); cast_group(1, 1); cast_group(1, 2)
    if int(_os.environ.get('KSAMP', 1)):
        big4k = [tmp([128, 1024], F32, "big4k%d" % i) for i in range(2)]
        big8k = tmp([128, 2048], F32, "big8k")
        load_x(I["xs"], NS)
        for l in range(LYR):
            sample_layer(l)
        final_norm_store(NS, O["y_s"])

    for tl in outs_tiles:
        pool.prog.append(('w', tl.dsem, K.sem_cnt[id(tl.dsem)] * 16))
    K.replay()
    return nc


_NC = [None]


def kernel(**inp):
    if _NC[0] is None:
        _NC[0] = build()
    nc = _NC[0]
    consts = make_consts()
    f = lambda a: np.ascontiguousarray(np.asarray(a, dtype=np.float32))
    wts = {k: f(inp[k]) for k in WSHAPES}
    in_maps = []
    for c in range(8):
        sl = slice(c * NS, (c + 1) * NS)
        m = dict(wts); m.update(consts)
        m["xp"] = f(inp["x_prompt"][c]); m["xs"] = f(inp["x_sample"][sl, 0]); m["mem"] = f(inp["mem_prompt"][c])
        m["ck"] = f(np.asarray(inp["cache_mem_k"])[:, sl].reshape(2, NS, 256, 1024))
        m["cv"] = f(np.asarray(inp["cache_mem_v"])[:, sl].reshape(2, NS, 256, 1024))
        m["st_c"] = f(np.asarray(inp["state_mlstm_c"])[:, sl]); m["st_n"] = f(np.asarray(inp["state_mlstm_n"])[:, sl])
        m["st_m"] = f(np.asarray(inp["state_mlstm_m"])[:, sl]); m["st_s5r"] = f(np.asarray(inp["state_s5_re"])[:, sl])
        m["st_s5i"] = f(np.asarray(inp["state_s5_im"])[:, sl]); m["st_g"] = f(np.asarray(inp["state_gdn"])[:, sl])
        m["st_gc"] = f(np.asarray(inp["state_gdn_conv"])[:, sl])
        in_maps.append(m)
    res = run_bass_kernel_spmd(nc, in_maps, core_ids=list(range(8)))
    R = res.results
    cat = lambda k, ax: np.concatenate([np.asarray(r[k], dtype=np.float32) for r in R], axis=ax)
    stk = lambda k: np.stack([np.asarray(r[k], dtype=np.float32) for r in R], axis=1)
    y_p = np.stack([np.asarray(r["y_p"], dtype=np.float32) for r in R], axis=0)
    y_s = cat("y_s", 0).reshape(128, 1, 1024)
    outs = [y_p, y_s, stk("mk_p").reshape(2, 8, 256, 4, 256), stk("mv_p").reshape(2, 8, 256, 4, 256),
            stk("c_p"), stk("n_p"), stk("m_p"), stk("s5r_p"), stk("s5i_p"), stk("g_p"), stk("gc_p"),
            cat("c_s", 1), cat("n_s", 1), cat("m_s", 1), cat("s5r_s", 1), cat("s5i_s", 1), cat("g_s", 1), cat("gc_s", 1)]
    return tuple(outs)
```

```python
import contextlib
import numpy as np
import concourse.bass as bass
import concourse.mybir as mybir
from concourse.bass_utils import run_bass_kernel_spmd
import math

F32 = mybir.dt.float32
BF16 = mybir.dt.bfloat16
I32 = mybir.dt.int32
ALU = mybir.AluOpType
AF = mybir.ActivationFunctionType
AX = mybir.AxisListType


class Tl:
    def __init__(self, K, t, name):
        self.K = K
        self.t = t
        self.name = name
        self.w = None
        self.r = []
        self.dsem = None
        self.dcnt = 0

    def __getitem__(self, k):
        return V(self, self.t[k])

    def ap(self):
        return self.t.ap() if hasattr(self.t, "ap") else self.t[:]


class V:
    def __init__(self, tl, ap):
        self.tl = tl
        self.ap = ap

    def __getitem__(self, k):
        return V(self.tl, self.ap[k])

    def re(self, s, **kw):
        return V(self.tl, self.ap.rearrange(s, **kw))

    def bc(self, shape):
        return V(self.tl, self.ap.to_broadcast(shape))

    def bct(self, shape):
        return V(self.tl, self.ap.broadcast_to(shape))

    def un(self, axis):
        return V(self.tl, self.ap.unsqueeze(axis))

    def bitcast(self, dt):
        return V(self.tl, self.ap.bitcast(dt))


class Eng:
    def __init__(self, K, name, e):
        self.K = K
        self.name = name
        self.e = e
        self.sem = K.es.enter_context(K.nc.semaphore("s_" + name))
        self.n = 0
        self.known = {}
        self.prog = []


class KB:
    def __init__(self, nc):
        self.nc = nc
        self.es = contextlib.ExitStack()
        self.uid = 0
        self.nwaits = 0
        self.sem_cnt = {}
        self.lazy_sems = set()

    def start(self):
        self.pe = Eng(self, "pe", self.nc.tensor)
        self.dve = Eng(self, "dve", self.nc.vector)
        self.act = Eng(self, "act", self.nc.scalar)
        self.pool = Eng(self, "pool", self.nc.gpsimd)
        self.sp = Eng(self, "sp", self.nc.sync)
        self.engs = [self.pe, self.dve, self.act, self.pool, self.sp]

    def sb(self, shape, dt=F32, name=None):
        self.uid += 1
        name = (name or "t") + "_%d" % self.uid
        t = self.es.enter_context(self.nc.sbuf_tensor(name, list(shape), dt))
        return Tl(self, t, name)

    def ps(self, shape, dt=F32, name=None):
        self.uid += 1
        name = (name or "p") + "_%d" % self.uid
        t = self.es.enter_context(self.nc.psum_tensor(name, list(shape), dt))
        tl = Tl(self, t, name)
        tl.is_psum = True
        return tl

    def _wait(self, eng, tok):
        sem, val, src = tok
        k = id(sem)
        if val == -1:
            if eng.known.get(k, -1) == 'all':
                return
            eng.prog.append(('w', sem, None))
            self.nwaits += 1
            eng.known[k] = 'all'
            return
        if eng.known.get(k, -1) == 'all' or eng.known.get(k, -1) >= val:
            return
        if src is eng.name:
            if eng.name == "pe" or val < eng.n - 2:
                return
        eng.prog.append(('w', sem, val))
        self.nwaits += 1
        eng.known[k] = val

    def _tok_of(self, tl, tok):
        sem, val, src = tok
        if src == "dma":
            if id(sem) in self.lazy_sems:
                return (sem, -1, src)
            return (sem, self.sem_cnt[id(sem)] * 16, src)
        return tok

    def deps(self, eng, reads, writes, raw_only_same=True):
        for v in reads:
            tl = v.tl if isinstance(v, V) else v
            if tl.w is not None:
                self._wait(eng, self._tok_of(tl, tl.w))
            if getattr(tl, "is_psum", False):
                for tok in tl.r:
                    if not (tok[2] is eng.name):
                        self._wait(eng, tok)
        for v in writes:
            tl = v.tl if isinstance(v, V) else v
            if tl.w is not None:
                tok = self._tok_of(tl, tl.w)
                if not (tok[2] is eng.name):
                    self._wait(eng, tok)
            for tok in tl.r:
                tok = self._tok_of(tl, tok)
                if not (tok[2] is eng.name):
                    self._wait(eng, tok)

    def done(self, tok, reads, writes):
        for v in reads:
            tl = v.tl if isinstance(v, V) else v
            tl.r.append(tok)
            if len(tl.r) > 24:
                best = {}
                for t_ in tl.r:
                    kk = (id(t_[0]))
                    if kk not in best or best[kk][1] < t_[1]:
                        best[kk] = t_
                tl.r = list(best.values())
        for v in writes:
            tl = v.tl if isinstance(v, V) else v
            tl.w = tok
            tl.r = []

    def op(self, eng, fn, reads, writes):
        reads = [r for r in reads if isinstance(r, (V, Tl))]
        self.deps(eng, reads, writes)
        eng.n += 1
        eng.prog.append(('i', fn, eng.sem, 1))
        tok = (eng.sem, eng.n, eng.name)
        self.done(tok, reads, writes)
        if getattr(self, "switch_hook", None) is not None:
            self.switch_hook()
        return tok

    def dma(self, eng, out, in_, **kw):
        reads = [in_] if isinstance(in_, V) else []
        writes = [out] if isinstance(out, V) else []
        self.deps(eng, reads, writes)
        tl = (writes + reads)[0].tl if (writes + reads) else None
        o = out.ap if isinstance(out, V) else out
        i = in_.ap if isinstance(in_, V) else in_
        fn = (lambda e: e.dma_start(out=o, in_=i, **kw))
        return fn, reads, writes

    def dma_tile(self, eng, out, in_, **kw):
        fn, reads, writes = self.dma(eng, out, in_, **kw)
        tl = (writes + reads)[0].tl
        if tl.dsem is None:
            if getattr(self, "cur_group", None) is not None:
                tl.dsem = self.cur_group
            else:
                self.nsem = getattr(self, 'nsem', 0) + 1
                try:
                    tl.dsem = self.es.enter_context(self.nc.semaphore("d_" + tl.name))
                except KeyError:
                    print('OUT OF SEMAPHORES at', self.nsem, tl.name); raise
        eng.prog.append(('i', fn, tl.dsem, 16))
        c_ = self.sem_cnt.get(id(tl.dsem), 0) + 1
        self.sem_cnt[id(tl.dsem)] = c_
        tl.dcnt = c_
        tok = (tl.dsem, c_ * 16, "dma")
        self.done(tok, reads, writes)
        return tok

    @staticmethod
    def _a(x):
        return x.ap if isinstance(x, V) else x

    def mm(self, out, lhsT, rhs, start=True, stop=True):
        a = self._a
        lb = a(lhsT).base_partition(); ob = a(out).base_partition()
        kw = {}
        if lb != 0 or ob != 0:
            kw["tile_position"] = (lb, ob)
        return self.op(self.pe, lambda e: e.matmul(a(out), a(lhsT), a(rhs), start=start, stop=stop, **kw),
                       [lhsT, rhs] + ([] if start else [out]), [out])

    def tr(self, out, in_, ident):
        a = self._a
        return self.op(self.pe, lambda e: e.transpose(a(out), a(in_), a(ident)),
                       [in_, ident], [out])

    def activation(self, out, in_, func, bias=None, scale=None, accum=None, eng=None):
        a = self._a
        eng = eng or self.act
        kw = {}
        if bias is not None:
            kw["bias"] = a(bias)
        if scale is not None:
            kw["scale"] = a(scale)
        if accum is not None:
            kw["accum_out"] = a(accum)
        w = [out] + ([accum] if accum is not None else [])
        return self.op(eng, lambda e: e.activation(a(out), a(in_), func, **kw),
                       [in_, bias, scale], w)

    def tt(self, eng, out, in0, in1, op):
        a = self._a
        return self.op(eng, lambda e: e.tensor_tensor(a(out), a(in0), a(in1), op), [in0, in1], [out])

    def ts(self, eng, out, in0, s1, op0, s2=None, op1=None, accum=None):
        a = self._a
        kw = {}
        if op1 is not None:
            kw["op1"] = op1
        if accum is not None:
            kw["accum_out"] = a(accum)
        w = [out] + ([accum] if accum is not None else [])
        return self.op(eng, lambda e: e.tensor_scalar(a(out), a(in0), a(s1), a(s2) if s2 is not None else None, op0, **kw),
                       [in0, s1, s2], w)

    def stt(self, eng, out, in0, scalar, in1, op0, op1):
        a = self._a
        return self.op(eng, lambda e: e.scalar_tensor_tensor(a(out), a(in0), a(scalar), a(in1), op0, op1),
                       [in0, scalar, in1], [out])

    def copy(self, eng, out, in_):
        a = self._a
        if eng is self.act:
            return self.op(eng, lambda e: e.copy(a(out), a(in_)), [in_], [out])
        return self.op(eng, lambda e: e.tensor_copy(a(out), a(in_)), [in_], [out])

    def scan(self, out, d0, d1, init, op0, op1, eng=None):
        a = self._a
        eng = eng or self.dve
        return self.op(eng, lambda e: e.tensor_tensor_scan(a(out), a(d0), a(d1), a(init), op0, op1),
                       [d0, d1, init], [out])

    def reduce(self, eng, out, in_, op, axis=None):
        a = self._a
        axis = axis if axis is not None else AX.X
        return self.op(eng, lambda e: e.tensor_reduce(a(out), a(in_), axis, op), [in_], [out])

    def memset(self, eng, out, val):
        a = self._a
        return self.op(eng, lambda e: e.memset(a(out), val), [], [out])

    def recip(self, out, in_):
        a = self._a
        return self.op(self.dve, lambda e: e.reciprocal(a(out), a(in_)), [in_], [out])

    def finish(self, tiles):
        for tl in tiles:
            if tl.dsem is not None:
                self.sp.prog.append(('w', tl.dsem, tl.dcnt * 16))

    def replay(self):
        def run(eng):
            def f(e):
                for a in eng.prog:
                    if a[0] == 'w':
                        e.wait_ge(a[1], a[2] if a[2] is not None else self.sem_cnt[id(a[1])] * 16)
                    else:
                        ins = a[1](e)
                        ins.then_inc(a[2], a[3])
            return f
        with self.nc.Block() as block:
            block.tensor(run(self.pe))
            block.vector(run(self.dve))
            block.scalar(run(self.act))
            block.gpsimd(run(self.pool))
            block.sync(run(self.sp))


import threading


class Coop:
    def __init__(self, K, fns):
        self.K = K; self.fns = fns; self.n = len(fns)
        self.sems = [threading.Semaphore(0) for _ in fns]
        self.finished = [False] * self.n
        self.err = None
        self.tl = threading.local()

    def _next(self, i):
        for d in range(1, self.n + 1):
            j = (i + d) % self.n
            if not self.finished[j]:
                return j
        return None

    def switch(self):
        i = self.tl.idx
        j = self._next(i)
        if j is None or j == i:
            return
        self.sems[j].release()
        self.sems[i].acquire()

    def _wrap(self, i):
        self.tl.idx = i
        self.sems[i].acquire()
        try:
            r = self.fns[i]()
            if r is not None:
                for _ in r:
                    pass
        except BaseException as e:
            self.err = e
        finally:
            self.finished[i] = True
            j = self._next(i)
            if j is not None:
                self.sems[j].release()
            else:
                self.main.release()

    def run(self):
        self.main = threading.Semaphore(0)
        ths = [threading.Thread(target=self._wrap, args=(i,)) for i in range(self.n)]
        for t in ths:
            t.start()
        self.K.switch_hook = self.switch
        self.sems[0].release()
        self.main.acquire()
        self.K.switch_hook = None
        for t in ths:
            t.join()
        if self.err is not None:
            raise self.err

D = 1024; T = 2048; N = 256; NB = T // N; NTT = N // 128; LYR = 2; DFF = 2816; NS = 16
EPS = 1e-6
MUL = ALU.mult; ADD = ALU.add; SUB = ALU.subtract; MAX = ALU.max

WSHAPES = dict(norm_mix=(2, 1024), w_in=(2, 1024, 2576), w_out=(2, 1024, 1024), mlstm_b_i=(2, 4), mlstm_b_f=(2, 4),
               mlstm_norm=(2, 256), s5_a_re=(2, 32, 64), s5_a_im=(2, 32, 64), s5_log_dt=(2, 32),
               s5_b_re=(2, 32, 64, 16), s5_b_im=(2, 32, 64, 16), s5_c_re=(2, 32, 16, 64), s5_c_im=(2, 32, 16, 64),
               s5_d=(2, 32, 16), s5_w_glu=(2, 512, 512), gdn_conv_w=(2, 4, 768), gdn_a_log=(2, 4), gdn_dt_bias=(2, 4),
               gdn_norm=(2, 64), norm_xattn=(2, 1024), norm_mem=(2, 1024), w_mq=(2, 1024, 1024), w_mk=(2, 1024, 1024),
               w_mv=(2, 1024, 1024), w_mo=(2, 1024, 1024), norm_ffn=(2, 1024), w_gate=(2, 1024, 2816),
               w_up=(2, 1024, 2816), w_down=(2, 2816, 1024), norm_final=(1024,))
INSHAPES = dict(xp=(T, D), xs=(NS, D), mem=(256, D), ck=(2, NS, 256, 1024), cv=(2, NS, 256, 1024),
                st_c=(2, NS, 4, 64, 64), st_n=(2, NS, 4, 64), st_m=(2, NS, 4), st_s5r=(2, NS, 32, 64),
                st_s5i=(2, NS, 32, 64), st_g=(2, NS, 4, 64, 64), st_gc=(2, NS, 3, 768))
OUTSHAPES = dict(y_p=(T, D), y_s=(NS, D), mk_p=(2, 256, 1024), mv_p=(2, 256, 1024),
                 c_p=(2, 4, 64, 64), n_p=(2, 4, 64), m_p=(2, 4), s5r_p=(2, 32, 64), s5i_p=(2, 32, 64),
                 g_p=(2, 4, 64, 64), gc_p=(2, 3, 768),
                 c_s=(2, NS, 4, 64, 64), n_s=(2, NS, 4, 64), m_s=(2, NS, 4), s5r_s=(2, NS, 32, 64),
                 s5i_s=(2, NS, 32, 64), g_s=(2, NS, 4, 64, 64), gc_s=(2, NS, 3, 768))


def make_consts():
    c = {}
    c["c_ident"] = np.eye(128, dtype=np.float32)
    j = np.arange(128)[:, None]; t = np.arange(128)[None, :]
    m = np.zeros((128, 5, 128), np.float32)
    m[:, 0] = (j <= t); m[:, 1] = np.where(j <= t, 0.0, -30000.0); m[:, 2] = (j < t)
    m[:, 3] = (j // 64 == t // 64); m[:, 4] = (j >= 64) & (t < 64)
    c["c_mask"] = m
    c["c_blk64"] = (j // 64 == t // 64).astype(np.float32)
    sel = np.zeros((128, 12, 128), np.float32)
    k = np.arange(128)[:, None]; mm_ = np.arange(128)[None, :]
    for p in range(2):
        sel[:, p] = 8.0 * (k == 2 * p + mm_ // 64)
        sel[:, 2 + p] = 1.0 * (k == 32 + 2 * p + mm_ // 64)
    for h in range(4):
        sel[:, 4 + h] = -1.0 * (k == 32 + h)
        sel[:, 8 + h] = 1.0 * (k == 32 + h)
    c["c_sel"] = sel
    c["c_tau"] = np.tile(np.arange(N, dtype=np.float32)[None, :], (128, 1))
    return c


CONSTSHAPES = dict(c_ident=(128, 128), c_mask=(128, 5, 128), c_blk64=(128, 128), c_sel=(128, 12, 128), c_tau=(128, N))


def build():
    nc = bass.Bass("TRN2", target_bir_lowering=False)
    I = {k: nc.dram_tensor(k, list(s), F32, kind="ExternalInput").ap()
         for k, s in {**INSHAPES, **WSHAPES, **CONSTSHAPES}.items()}
    O = {k: nc.dram_tensor(k, list(s), F32, kind="ExternalOutput").ap() for k, s in OUTSHAPES.items()}
    K = KB(nc); K.start()
    dve, act, pool, sp, pe = K.dve, K.act, K.pool, K.sp, K.pe
    cnt = [0]

    def dscr(shape, dt=BF16):
        cnt[0] += 1
        return nc.dram_tensor("scr%d" % cnt[0], list(shape), dt, kind="Internal").ap()

    tcache = {}

    def tmp(shape, dt, name):
        k = (name, tuple(shape), str(dt))
        if k not in tcache:
            tcache[k] = K.sb(shape, dt, name)
        return tcache[k]

    def newres(nm="res"):
        cnt[0] += 1
        return Tl(K, None, "%s%d" % (nm, cnt[0]))

    outs_tiles = []

    def store(dst_ap, src_v, eng=None, **kw):
        K.dma_tile(eng or pool, dst_ap, src_v, **kw)
        if src_v.tl not in outs_tiles:
            outs_tiles.append(src_v.tl)

    def ld(dst_v, src_ap, **kw):
        K.dma_tile(sp, dst_v, src_ap, **kw)

    ident = K.sb([128, 128], F32, "ident"); ld(ident[:, :], I["c_ident"])
    identb = K.sb([128, 128], BF16, "identb"); K.copy(dve, identb[:, :], ident[:, :])
    mask = K.sb([128, 5, 128], F32, "mask"); ld(mask[:, :, :], I["c_mask"])
    blkf = K.sb([128, 128], F32, "blkf"); ld(blkf[:, :], I["c_blk64"])
    blk64 = K.sb([128, 128], BF16, "blk64"); K.copy(dve, blk64[:, :], blkf[:, :])
    onesb = K.sb([128, 128], BF16, "onesb"); K.memset(dve, onesb[:, :], 1.0)
    sel = K.sb([128, 12, 128], F32, "sel"); ld(sel[:, :, :], I["c_sel"])
    tau = K.sb([128, N], F32, "tau"); ld(tau[:, :], I["c_tau"])
    zrow = K.sb([128, N], F32, "zrow"); K.memset(dve, zrow[:, :], 0.0)
    onerow = K.sb([128, N], F32, "onerow"); K.memset(dve, onerow[:, :], 1.0)

    P = [K.ps([128, 512], F32, "P%d" % i) if i != 5 else K.ps([128, 1024], BF16, "P5b") for i in range(8)]
    pjc = [0]

    def pj():
        pjc[0] += 1
        return P[pjc[0] % 2]

    import os as _os2
    WS = [dict() for _ in range(LYR)]

    def castw(l, name, groups, nk):
        tot = sum(g.shape[2] for g in groups)
        dst = dscr([128, nk, tot]); res = newres("w")
        if l >= 1 and int(_os2.environ.get('KL1', 1)) and name not in ('mk', 'mv'):
            if getattr(K, "l1sem", None) is None:
                K.l1sem = K.es.enter_context(nc.semaphore("w_l1"))
                if int(_os2.environ.get('KL1', 1)) == 1: K.lazy_sems.add(id(K.l1sem))
            res.dsem = K.l1sem
        o = 0
        for gi_, g in enumerate(groups):
            w = g.shape[2]
            if gi_ > 0 and id(res.dsem) in K.lazy_sems:
                r2 = newres("w"); r2.dsem = res.dsem; res = r2
            K.dma_tile(pool, V(res, dst[:, :, o:o + w]), g)
            o += w
        WS[l][name] = (res, dst, nk, tot)

    def cast_group(l, which):
        win = I["w_in"][l].rearrange("(k p) c -> p k c", p=128)
        if which == 0:
            castw(l, "in_tm", [win[:, :, 0:768]], 8)
            castw(l, "in_fm", [win[:, :, 768:1024], win[:, :, 1032:1544], win[:, :, 1544:2312], win[:, :, 2312:2568]], 8)
        elif which == 3:
            castw(l, "mk", [I["w_mk"][l].rearrange("(k p) c -> p k c", p=128)], 8)
            castw(l, "mv", [I["w_mv"][l].rearrange("(k p) c -> p k c", p=128)], 8)
        elif which == 1:
            castw(l, "glu", [I["s5_w_glu"][l].rearrange("(k p) c -> p k c", p=128)], 4)
            for nm, src in (("out", "w_out"), ("mq", "w_mq"), ("mo", "w_mo")):
                castw(l, nm, [I[src][l].rearrange("(k p) c -> p k c", p=128)], 8)
        else:
            for nm, src in (("gate", "w_gate"), ("up", "w_up")):
                castw(l, nm, [I[src][l].rearrange("(k p) c -> p k c", p=128)], 8)
            castw(l, "down", [I["w_down"][l].rearrange("(k p) c -> p k c", p=128)], 22)


    ring = [K.sb([128, 2048], BF16, "wr%d" % i) for i in range(4)]
    rpos = [0]

    def wload(res, ap3, nk, w):
        s = ring[rpos[0] % 4]; rpos[0] += 1
        assert nk * w <= 2048, (nk, w)
        v = s[:, 0:nk * w].re("p (k w) -> p k w", k=nk)
        K.dma_tile(sp, v, V(res, ap3))
        return v

    def proj_fm(l, name, c0, nct_tot, rhs_fn, evac, n):
        res, dst, nk, tot = WS[l][name]
        per = max(1, (2048 // nk) // 128)
        for u0 in range(0, nct_tot, per):
            nct = min(per, nct_tot - u0)
            wv = wload(res, dst[:, :, c0 + u0 * 128: c0 + (u0 + nct) * 128], nk, nct * 128)
            for c in range(nct):
                ps = pj()
                for kt in range(nk):
                    K.mm(ps[:, 0:n], wv[:, kt, c * 128:(c + 1) * 128], rhs_fn(kt), start=(kt == 0), stop=(kt == nk - 1))
                evac(u0 + c, ps[:, 0:n])

    PAR = K.sb([128, 160], F32, "PAR"); parpos = [0]

    def paralloc(w):
        v = PAR[:, parpos[0]:parpos[0] + w]; parpos[0] += w
        return v

    def colload(src1d, name):
        t = paralloc(8)
        ld(t, src1d.rearrange("(k p) -> p k", p=128), allow_slow_non_contiguous=True)
        return t

    gmix = [colload(I["norm_mix"][l], "gmix") for l in range(LYR)]
    gxat = [colload(I["norm_xattn"][l], "gxat") for l in range(LYR)]
    gffn = [colload(I["norm_ffn"][l], "gffn") for l in range(LYR)]
    gmem = [colload(I["norm_mem"][l], "gmem") for l in range(LYR)]
    gfin = colload(I["norm_final"], "gfin")
    WG, bcol, GN, GDNN, CW, Dcol = [], [], [], [], [], []
    for l in range(LYR):
        win = I["w_in"][l].rearrange("(k p) c -> p k c", p=128)
        gwf = K.sb([128, 8, 16], F32, "gwf")
        ld(gwf[:, :, 0:8], win[:, :, 1024:1032]); ld(gwf[:, :, 8:16], win[:, :, 2568:2576])
        wg = K.sb([128, 8, 2, 128], BF16, "WG"); K.memset(dve, wg[:, :, :, :], 0.0)
        K.copy(dve, wg[:, :, 0, 0:4], gwf[:, :, 0:4]); K.copy(dve, wg[:, :, 1, 0:4], gwf[:, :, 4:8])
        K.copy(dve, wg[:, :, 0, 32:36], gwf[:, :, 8:12]); K.copy(dve, wg[:, :, 1, 32:36], gwf[:, :, 12:16])
        WG.append(wg)
        bc_ = K.sb([128, 4], F32, "bcol"); K.memset(dve, bc_[:, :], 0.0)
        ld(bc_[0:4, 0:1], I["mlstm_b_i"][l].rearrange("(h o) -> h o", o=1))
        ld(bc_[0:4, 1:2], I["mlstm_b_f"][l].rearrange("(h o) -> h o", o=1))
        ld(bc_[32:36, 2:3], I["gdn_dt_bias"][l].rearrange("(h o) -> h o", o=1))
        ld(bc_[32:36, 3:4], I["gdn_a_log"][l].rearrange("(h o) -> h o", o=1))
        K.ts(dve, bc_[0:4, 1:2], bc_[0:4, 1:2], -1.0, MUL)
        K.activation(bc_[32:36, 3:4], bc_[32:36, 3:4], AF.Exp)
        K.ts(dve, bc_[32:36, 3:4], bc_[32:36, 3:4], -1.0, MUL)
        bcol.append(bc_)
        gn = K.sb([128, 256], F32, "GN"); ld(gn[:, :], I["mlstm_norm"][l].partition_broadcast(128)); GN.append(gn)
        g2 = K.sb([128, 64], F32, "GDNN"); ld(g2[:, :], I["gdn_norm"][l].partition_broadcast(128)); GDNN.append(g2)
        cw = paralloc(24).re("p (i j) -> p i j", i=6)
        for i6 in range(6):
            ld(cw[:, i6, :], I["gdn_conv_w"][l][:, i6 * 128:(i6 + 1) * 128].rearrange("j p -> p j"), allow_slow_non_contiguous=True)
        CW.append(cw)
        dc = paralloc(4)
        ld(dc, I["s5_d"][l].rearrange("g p -> (g p)").rearrange("(o q) -> q o", q=128), allow_slow_non_contiguous=True)
        Dcol.append(dc)

    K.cur_group = None
    TWO_PI = 2.0 * math.pi

    def sincos(sin_o, cos_o, ang, shp, nm):
        for out, add in ((sin_o, 0.0), (cos_o, math.pi / 2)):
            r = tmp(shp, F32, nm + "r"); ki = tmp(shp, I32, nm + "ki"); kf = tmp(shp, F32, nm + "kf")
            K.ts(dve, r[:, :], ang, add, ADD, 1.0 / TWO_PI, MUL)
            K.copy(dve, ki[:, :], r[:, :]); K.copy(dve, kf[:, :], ki[:, :])
            K.tt(dve, r[:, :], r[:, :], kf[:, :], SUB)
            K.stt(dve, kf[:, :], r[:, :], 0.5, r[:, :], ALU.is_gt, SUB)
            K.activation(out, kf[:, :], AF.Sin, scale=-TWO_PI)

    tabring_pre = K.sb([128, 2, N], F32, "tabr0")
    import os as _os
    S5 = []

    def s5_setup(l):
        s = {}
        are = tmp([128, 16], F32, "are"); aim = tmp([128, 16], F32, "aim"); ldt = tmp([128, 16], F32, "ldt")
        ld(are[:, :], I["s5_a_re"][l].rearrange("(i g) n -> (g n) i", g=2), allow_slow_non_contiguous=True)
        ld(aim[:, :], I["s5_a_im"][l].rearrange("(i g) n -> (g n) i", g=2), allow_slow_non_contiguous=True)
        lv = I["s5_log_dt"][l].rearrange("(i g) -> g i", g=2)
        for gg in range(2):
            ld(ldt[gg * 64:(gg + 1) * 64, :], lv[gg].partition_broadcast(64), allow_slow_non_contiguous=True)
        dt_ = tmp([128, 16], F32, "dt"); K.activation(dt_[:, :], ldt[:, :], AF.Exp)
        mag = K.sb([128, 16], F32, "mag"); K.tt(dve, mag[:, :], are[:, :], dt_[:, :], MUL)
        K.activation(mag[:, :], mag[:, :], AF.Exp)
        th = K.sb([128, 16], F32, "th"); K.tt(dve, th[:, :], aim[:, :], dt_[:, :], MUL)
        ur = K.sb([128, 16], F32, "ur"); ui = K.sb([128, 16], F32, "ui")
        sincos(ui[:, :], ur[:, :], th[:, :], [128, 16], "sc0")
        lre = K.sb([128, 16], F32, "lre"); lim = K.sb([128, 16], F32, "lim")
        K.tt(dve, lre[:, :], mag[:, :], ur[:, :], MUL); K.tt(dve, lim[:, :], mag[:, :], ui[:, :], MUL)
        nr = tmp([128, 16], F32, "nr"); K.ts(dve, nr[:, :], lre[:, :], -1.0, ADD)
        den = tmp([128, 16], F32, "den"); t0 = tmp([128, 16], F32, "t0"); t1 = tmp([128, 16], F32, "t1")
        K.tt(dve, den[:, :], are[:, :], are[:, :], MUL); K.tt(dve, t0[:, :], aim[:, :], aim[:, :], MUL)
        K.tt(dve, den[:, :], den[:, :], t0[:, :], ADD); K.recip(den[:, :], den[:, :])
        cre = tmp([128, 16], F32, "cre"); cim = tmp([128, 16], F32, "cim")
        K.tt(dve, t0[:, :], nr[:, :], are[:, :], MUL); K.tt(dve, t1[:, :], lim[:, :], aim[:, :], MUL)
        K.tt(dve, t0[:, :], t0[:, :], t1[:, :], ADD); K.tt(dve, cre[:, :], t0[:, :], den[:, :], MUL)
        K.tt(dve, t0[:, :], lim[:, :], are[:, :], MUL); K.tt(dve, t1[:, :], nr[:, :], aim[:, :], MUL)
        K.tt(dve, t0[:, :], t0[:, :], t1[:, :], SUB); K.tt(dve, cim[:, :], t0[:, :], den[:, :], MUL)
        bre = V(tmp([128, N], F32, "s5t2"), tmp([128, N], F32, "s5t2").t[:, :].rearrange("p (a b) -> p a b", a=16)); bim = V(tmp([128, N], F32, "s5t3"), tmp([128, N], F32, "s5t3").t[:, :].rearrange("p (a b) -> p a b", a=16))
        ld(bre[:, :, :], I["s5_b_re"][l].rearrange("(i g) n p -> (g n) i p", g=2))
        ld(bim[:, :, :], I["s5_b_im"][l].rearrange("(i g) n p -> (g n) i p", g=2))
        bb = [V(tmp([128, N], F32, "s5t4"), tmp([128, N], F32, "s5t4").t[:, :].rearrange("p (a b) -> p a b", a=16)), V(tmp([128, N], F32, "s5gr"), tmp([128, N], F32, "s5gr").t[:, :].rearrange("p (a b) -> p a b", a=16))]
        tb = V(tmp([128, N], F32, "s5gi"), tmp([128, N], F32, "s5gi").t[:, :].rearrange("p (a b) -> p a b", a=16))
        crb = cre[:, :].un(2).bct([128, 16, 16]); cib = cim[:, :].un(2).bct([128, 16, 16])
        K.tt(dve, bb[0][:, :, :], bre[:, :, :], crb, MUL); K.tt(dve, tb[:, :, :], bim[:, :, :], cib, MUL)
        K.tt(dve, bb[0][:, :, :], bb[0][:, :, :], tb[:, :, :], SUB)
        K.tt(dve, bb[1][:, :, :], bim[:, :, :], crb, MUL); K.tt(dve, tb[:, :, :], bre[:, :, :], cib, MUL)
        K.tt(dve, bb[1][:, :, :], bb[1][:, :, :], tb[:, :, :], ADD)
        BTd = dscr([128, 2, 16, 128]); BTres = newres("bt")
        for ri in range(2):
            bbz = V(tmp([128, 2048], F32, "big8k"), tmp([128, 2048], F32, "big8k").t[:, :].rearrange("p (i c) -> p i c", i=16)); K.memset(dve, bbz[:, :, :], 0.0)
            for gg in range(2):
                for q in range(4):
                    dst = bbz[gg * 64:(gg + 1) * 64, :, :].re("p (a q) c -> p q a c", q=4)[:, q, :, (2 * q + gg) * 16:(2 * q + gg) * 16 + 16]
                    src = bb[ri][gg * 64:(gg + 1) * 64, :, :].re("p (a q) c -> p q a c", q=4)[:, q, :, :]
                    K.copy(dve, dst, src)
            bts = tmp([128, 16, 128], BF16, "big4kb")
            for i4 in range(4):
                for ii in range(4):
                    K.tr(P[3][:, ii * 128:(ii + 1) * 128], bbz[:, i4 * 4 + ii, :], ident[:, :])
                K.copy(act, bts[:, i4 * 4:(i4 + 1) * 4, :], P[3][:, :].re("p (a c) -> p a c", a=4))
            K.dma_tile(sp, V(BTres, BTd[:, ri]), bts[:, :, :])
        s["BT"] = (BTres, BTd)
        CTd = dscr([128, 2, 16, 128]); CTres = newres("ct")
        for ri, nm in enumerate(("s5_c_re", "s5_c_im")):
            cz = V(tmp([128, 2048], F32, "big8k"), tmp([128, 2048], F32, "big8k").t[0:32, :].rearrange("p (i c) -> p i c", i=16)); K.memset(dve, cz[:, :, :], 0.0)
            cv_ = I[nm][l].rearrange("(i g) p n -> g p i n", g=2)
            ld(cz[0:16, :, 0:64], cv_[0]); ld(cz[16:32, :, 64:128], cv_[1])
            for i in range(16):
                K.tr(P[4][:, i * 32:(i + 1) * 32], cz[:, i, :], ident[0:32, 0:32])
            cts = tmp([128, 16, 128], BF16, "big4kb"); K.memset(dve, cts[:, :, :], 0.0)
            for q in range(4):
                dst = cts[:, :, :].re("p (a q) c -> p q a c", q=4)[:, q, :, q * 32:(q + 1) * 32]
                src = P[4][:, :].re("p (a q c) -> p q a c", q=4, c=32)[:, q, :, :]
                K.ts(dve, dst, src, 1.0 if ri == 0 else -1.0, MUL)
            K.dma_tile(sp, V(CTres, CTd[:, ri]), cts[:, :, :])
        s["CT"] = (CTres, CTd)
        TABd = dscr([16, 128, 2, N], F32); TABres = newres("tab")
        CL = K.sb([128, 16], F32, "CL"); SL = K.sb([128, 16], F32, "SL")
        for i in range(16):
            ang = tmp([128, N], F32, "s5t1"); K.ts(dve, ang[:, :], tau[:, :], th[:, i:i + 1], MUL)
            tabs = tabring_pre
            sincos(tabs[:, 1, :], tabs[:, 0, :], ang[:, :], [128, N], "sct")
            K.copy(act, CL[:, i:i + 1], tabs[:, 0, N - 1:N]); K.copy(act, SL[:, i:i + 1], tabs[:, 1, N - 1:N])
            K.dma_tile(sp, V(TABres, TABd[i]), tabs[:, :, :])
        s.update(TAB=(TABres, TABd), CL=CL, SL=SL, ur=ur, ui=ui, mag=mag, lre=lre, lim=lim)
        s["Hre"] = K.sb([128, 16], F32, "Hre"); s["Him"] = K.sb([128, 16], F32, "Him")
        K.memset(dve, s["Hre"][:, :], 0.0); K.memset(dve, s["Him"][:, :], 0.0)
        S5.append(s)

    SS = {"ready": False, "pend": None}

    def ss_after(ot, n):
        sq = tmp([128, N], BF16, "sq%d" % (ot % 2))
        K.activation(sq[:, 0:n], xT[ot][:, 0:n], AF.Square)
        if SS["pend"] is not None:
            o_, q_ = SS["pend"]
            K.mm(P[2][:, 0:n], onesb[:, :], q_[:, 0:n], start=(o_ == 0), stop=False)
        SS["pend"] = (ot, sq)
        if ot == 7:
            K.mm(P[2][:, 0:n], onesb[:, :], sq[:, 0:n], start=False, stop=True)
            SS["pend"] = None
            SS["ready"] = True

    def rmsnorm_fm(xT, gcol, xn, n):
        if SS["ready"]:
            ps = P[2]; SS["ready"] = False
        else:
            ps = pj()
            for kt in range(8):
                sq = tmp([128, N], BF16, "sq%d" % (kt % 2))
                K.activation(sq[:, 0:n], xT[kt][:, 0:n], AF.Square)
                K.mm(ps[:, 0:n], onesb[:, :], sq[:, 0:n], start=(kt == 0), stop=(kt == 7))
        rstd = tmp([128, N], F32, "rstd")
        K.activation(rstd[:, 0:n], ps[:, 0:n], AF.Sqrt, scale=1.0 / D, bias=EPS)
        K.recip(rstd[:, 0:n], rstd[:, 0:n])
        for kt in range(8):
            K.stt(dve, xn[:, kt, 0:n], xT[kt][:, 0:n], gcol[:, kt:kt + 1], rstd[:, 0:n], MUL, MUL)

    cast_group(0, 0); cast_group(0, 3); cast_group(1, 3)
    s5_setup(0)
    MKT, MVB = [], []
    memx = [tmp([128, 1024], F32, "big4k%d" % i) for i in range(2)]
    xhT = V(tmp([128, 2048], F32, "big8k"), tmp([128, 2048], F32, "big8k").t[:, :].rearrange("p (k m) -> p k m", k=8))
    for mt in range(2):
        ld(memx[mt][:, :], I["mem"][mt * 128:(mt + 1) * 128, :])
        ss = tmp([128, 1], F32, "mss"); junk = tmp([128, 1024], BF16, "mjunk")
        K.activation(junk[:, :], memx[mt][:, :], AF.Square, accum=ss[:, :])
        K.activation(ss[:, :], ss[:, :], AF.Sqrt, scale=1.0 / D, bias=EPS)
        K.recip(ss[:, :], ss[:, :])
        K.ts(dve, memx[mt][:, :], memx[mt][:, :], ss[:, 0:1], MUL)
        for kt in range(8):
            K.tr(P[3][:, (kt % 4) * 128:(kt % 4 + 1) * 128], memx[mt][:, kt * 128:(kt + 1) * 128], ident[:, :])
            if kt % 4 == 3:
                K.copy(act, xhT[:, kt - 3:kt + 1, mt * 128:(mt + 1) * 128], P[3][:, :].re("p (a c) -> p a c", a=4))
    def memkv(l):
        mnT = tmp([128, 8, 256], BF16, "mnT")
        for kt in range(8):
            K.ts(dve, mnT[:, kt, :], xhT[:, kt, :], gmem[l][:, kt:kt + 1], MUL)
        mkT = K.sb([128, 8, 256], BF16, "mkT"); mvb = K.sb([128, 2, 1024], BF16, "mvb")
        proj_fm(l, "mk", 0, 8, lambda kt: mnT[:, kt, :], lambda ot, ps: K.copy(act, mkT[:, ot, :], ps), 256)
        for nm, oname in (("mk", "mk_p"), ("mv", "mv_p")):
            res, dst, nk, tot = WS[l][nm]
            for ch in range(4):
                wv = wload(res, dst[:, :, ch * 256:(ch + 1) * 256], 8, 256)
                for mt in range(2):
                    ps = pj()
                    for kt in range(8):
                        K.mm(ps[:, 0:256], mnT[:, kt, mt * 128:(mt + 1) * 128], wv[:, kt, :], start=(kt == 0), stop=(kt == 7))
                    st_ = tmp([128, 512], F32, "kvst%d" % ((ch * 2 + mt) % 2))
                    K.copy(act, st_[:, 0:256], ps[:, 0:256])
                    if nm == "mv":
                        K.copy(dve, mvb[:, mt, ch * 256:(ch + 1) * 256], ps[:, 0:256])
                    store(O[oname][l, mt * 128:(mt + 1) * 128, ch * 256:(ch + 1) * 256], st_[:, 0:256])
        MKT.append(mkT); MVB.append(mvb)

    memkv(0); memkv(1)
    cast_group(0, 1); cast_group(0, 2)

    ST = []
    for l in range(LYR):
        s = {}
        s["CA"] = [K.sb([128, 65], F32, "CA") for _ in range(2)]
        s["CAb"] = [K.sb([128, 65], BF16, "CAb") for _ in range(2)]
        s["GS"] = [K.sb([128, 64], F32, "GS") for _ in range(2)]
        s["GSb"] = [K.sb([128, 64], BF16, "GSb") for _ in range(2)]
        for p in range(2):
            K.memset(dve, s["CA"][p][:, :], 0.0); K.memset(dve, s["CAb"][p][:, :], 0.0)
            K.memset(dve, s["GS"][p][:, :], 0.0); K.memset(dve, s["GSb"][p][:, :], 0.0)
        s["car"] = K.sb([128, 2], F32, "car"); K.memset(dve, s["car"][:, :], 0.0)
        s["XP"] = [K.sb([128, 3 + N], F32, "XP") for _ in range(6)]
        for i in range(6):
            K.memset(dve, s["XP"][i][:, 0:3], 0.0)
        ST.append(s)

    xT = [K.sb([128, N], F32, "xT%d" % i) for i in range(8)]
    xn = K.sb([128, 8, N], BF16, "xn")
    mixT = K.sb([128, 8, N], BF16, "mixT")
    ztm = [K.sb([128, 768], BF16, "ztm%d" % i) for i in range(NTT)]
    G = [K.sb([128, N], F32, "G%d" % i) for i in range(2)]
    for g_ in G:
        K.memset(dve, g_[:, :], 0.0)
    uT = [K.sb([128, N], BF16, "uT%d" % i) for i in range(4)]
    aoT = [K.sb([128, N], BF16, "aoT%d" % i) for i in range(2)]
    cgT = [K.sb([128, N], BF16, "cgT%d" % i) for i in range(2)]
    gsc = [K.sb([128, N], F32, "gsc%d" % i) for i in range(6)]
    for g_ in gsc:
        K.memset(dve, g_[:, :], 0.0)

    def inproj(l, n, prompt):
        st = ST[l]
        if prompt:
            res, dst, nk, tot = WS[l]["in_tm"]
            for (c0, w) in ((0, 256), (256, 256), (512, 256)):
                wv = wload(res, dst[:, :, c0:c0 + w], 8, w)
                for ts_ in range(NTT):
                    ps = pj()
                    for kt in range(8):
                        K.mm(ps[:, 0:w], xn[:, kt, ts_ * 128:(ts_ + 1) * 128], wv[:, kt, :], start=(kt == 0), stop=(kt == 7))
                    K.copy(act, ztm[ts_][:, c0:c0 + w], ps[:, 0:w])
        for gi in range(2):
            ps = pj()
            for kt in range(8):
                K.mm(ps[:, 0:n], WG[l][:, kt, gi, :], xn[:, kt, 0:n], start=(kt == 0), stop=(kt == 7))
            K.copy(act, G[gi][0:36, 0:n], ps[0:36, 0:n])

        def evac(ot, ps):
            if ot < 2:
                K.activation(aoT[ot][:, 0:n], ps, AF.Sigmoid)
            elif ot < 6:
                K.copy(act, uT[ot - 2][:, 0:n], ps)
            elif ot < 12:
                K.copy(act, st["XP"][ot - 6][:, 3:3 + n], ps)
            else:
                K.activation(cgT[ot - 12][:, 0:n], ps, AF.Silu)
        proj_fm(l, "in_fm", 0, 14, lambda kt: xn[:, kt, 0:n], evac, n)

    def mlstm_prompt(l, bi):
        st = ST[l]; car = st["car"]; bc_ = bcol[l]
        ic, l1, Fn, M_, Mp = gsc[0], gsc[1], gsc[2], gsc[3], gsc[4]
        r4 = slice(0, 4)
        K.activation(ic[r4, :], G[0][r4, :], AF.Identity, bias=bc_[r4, 0:1])
        K.activation(l1[r4, :], G[1][r4, :], AF.Exp, bias=bc_[r4, 1:2], scale=-1.0)
        K.activation(l1[r4, :], l1[r4, :], AF.Ln, bias=1.0)
        K.scan(Fn[r4, :], onerow[r4, :], l1[r4, :], car[r4, 0:1], MUL, ADD)
        K.tt(dve, ic[r4, :], ic[r4, :], Fn[r4, :], ADD)
        K.scan(M_[r4, :], ic[r4, :], zrow[r4, :], car[r4, 1:2], MAX, ADD)
        K.tt(dve, l1[r4, :], Fn[r4, :], M_[r4, :], SUB)
        K.activation(l1[r4, :], l1[r4, :], AF.Exp)
        for c in range(NTT):
            src = car[r4, 1:2] if c == 0 else M_[r4, c * 128 - 1:c * 128]
            K.ts(dve, Mp[r4, c * 128:(c + 1) * 128], zrow[r4, 0:128], src, ADD)
        K.tt(dve, ic[r4, :], ic[r4, :], Mp[r4, :], SUB)
        K.activation(ic[r4, :], ic[r4, :], AF.Exp)
        K.tt(dve, Mp[r4, :], Mp[r4, :], M_[r4, :], SUB)
        K.activation(Mp[r4, :], Mp[r4, :], AF.Exp, bias=-math.log(8.0))
        SUBS = float(_os.environ.get('KSUB', 9))
        if SUBS < 1: return
        dec = tmp([128, 2, NTT], F32, "mdec")
        for p in range(2):
            K.mm(P[3][:, p * 8:p * 8 + NTT], sel[0:4, p, :], Mp[r4, 127::128])
        K.copy(act, dec[:, :, :], P[3][:, 0:16].re("p (a c) -> p a c", a=2)[:, :, 0:NTT])
        K.copy(act, car[r4, 0:1], Fn[r4, N - 1:N]); K.copy(act, car[r4, 1:2], M_[r4, N - 1:N])
        if SUBS < 2: return
        for c in range(NTT):
            cs = slice(c * 128, (c + 1) * 128)
            for qi, srct in enumerate((ic, Mp, l1)):
                K.tr(P[4][:, qi * 4:qi * 4 + 4], srct[r4, cs], ident[0:4, 0:4])
            TT = tmp([128, 12], F32, "mTT"); K.copy(act, TT[:, :], P[4][:, 0:12])
            ka = tmp([128, 4, 64], BF16, "mka"); qa = tmp([128, 4, 64], BF16, "mqa")
            va = tmp([128, 4, 65], BF16, "mva%d" % c)
            if bi == 0 and l == 0:
                K.memset(dve, va[:, :, 64:65], 1.0)
            z = ztm[c]
            K.tt(dve, qa[:, :, :], z[:, 0:256].re("p (h d) -> p h d", h=4), TT[:, 4:8].un(2).bct([128, 4, 64]), MUL)
            K.tt(dve, ka[:, :, :], z[:, 256:512].re("p (h d) -> p h d", h=4), TT[:, 0:4].un(2).bct([128, 4, 64]), MUL)
            K.copy(act, va[:, :, 0:64], z[:, 512:768].re("p (h d) -> p h d", h=4))
            if SUBS < 3: continue
            kT = tmp([128, 2, 128], BF16, "mkT"); qT = tmp([128, 2, 128], BF16, "mqT")
            PB = P[5][:, :]
            for p in range(2):
                K.tr(PB[:, p * 128:(p + 1) * 128], ka[:, 2 * p:2 * p + 2, :].re("p h d -> p (h d)"), identb[:, :])
                K.tr(PB[:, 256 + p * 128:256 + (p + 1) * 128], qa[:, 2 * p:2 * p + 2, :].re("p h d -> p (h d)"), identb[:, :])
            K.copy(act, kT[:, :, :], PB[:, 0:256].re("p (a c) -> p a c", a=2))
            K.copy(act, qT[:, :, :], PB[:, 256:512].re("p (a c) -> p a c", a=2))
            if SUBS < 3.5: continue
            for h in range(4):
                rs = slice((h % 2) * 64, (h % 2) * 64 + 64)
                K.mm((P[6] if h % 2 == 0 else P[4])[:, (h // 2) * 128:(h // 2 + 1) * 128], kT[rs, h // 2, :], qT[rs, h // 2, :])
            if SUBS < 4: continue
            sT = tmp([128, 4, 128], BF16, "msT")
            for par, pt_ in ((0, P[6]), (1, P[4])):
                K.tt(dve, sT[:, :, :].re("p (a b) t -> p b a t", b=2)[:, par], pt_[:, 0:256].re("p (h t) -> p h t", h=2), mask[:, 0:1, :].bct([128, 2, 128]), MUL)
            for h in range(4):
                rs = slice((h % 2) * 64, (h % 2) * 64 + 64)
                K.mm(P[7][:, h * 128:h * 128 + 65], sT[:, h, :], va[:, h, :], start=True, stop=False)
                K.mm(P[7][:, h * 128:h * 128 + 65], qT[rs, h // 2, :], st["CAb"][h // 2][rs, :], start=False, stop=True)
            if SUBS < 5: continue
            na = P[7][:, :].re("p (h e) -> p h e", h=4)
            dn = tmp([128, 4], F32, "mdn")
            K.activation(dn[:, :], na[:, :, 64], AF.Abs)
            K.tt(dve, dn[:, :], dn[:, :], TT[:, 8:12], MAX)
            K.recip(dn[:, :], dn[:, :])
            hh = tmp([128, 4, 64], F32, "mhh"); sq = tmp([128, 4, 64], F32, "mhsq"); ss = tmp([128, 4], F32, "mss4")
            K.tt(dve, hh[:, :, :], na[:, :, 0:64], dn[:, :].un(2).bct([128, 4, 64]), MUL)
            K.tt(dve, sq[:, :, :], hh[:, :, :], hh[:, :, :], MUL)
            K.reduce(dve, ss[:, :], sq[:, :, :], ADD)
            K.activation(ss[:, :], ss[:, :], AF.Sqrt, scale=1.0 / 64, bias=EPS)
            K.recip(ss[:, :], ss[:, :])
            K.tt(dve, hh[:, :, :], hh[:, :, :], ss[:, :].un(2).bct([128, 4, 64]), MUL)
            if SUBS < 6: continue
            hb = tmp([128, 256], BF16, "mhb")
            K.tt(dve, hb[:, :], hh[:, :, :].re("p h d -> p (h d)"), GN[l][:, :], MUL)
            for p in range(2):
                K.tr(PB[:, 512 + p * 128:512 + (p + 1) * 128], hb[:, p * 128:(p + 1) * 128], identb[:, :])
            for p in range(2):
                K.tt(dve, mixT[:, p, cs], PB[:, 512 + p * 128:512 + (p + 1) * 128], aoT[p][:, cs], MUL)
            if SUBS < 7: continue
            for p in range(2):
                for hh_ in range(2):
                    h = 2 * p + hh_
                    K.mm(P[3][:, 64 + h * 65:64 + (h + 1) * 65], ka[:, 2 * p:2 * p + 2, :].re("p h d -> p (h d)"), va[:, h, :])
                for hh_ in range(2):
                    h = 2 * p + hh_; rs = slice(hh_ * 64, hh_ * 64 + 64)
                    K.tt(dve, st["CA"][p][rs, :], st["CA"][p][rs, :], P[3][rs, 64 + h * 65:64 + (h + 1) * 65], ADD)
                K.ts(dve, st["CA"][p][:, :], st["CA"][p][:, :], dec[:, p, c:c + 1], MUL)
                K.copy(act, st["CAb"][p][:, :], st["CA"][p][:, :])
            yield

    MFLAG = {"m": True, "g": False}

    def gdn_prompt(l, bi):
        st = ST[l]; bc_ = bcol[l]; XP = st["XP"]
        r4 = slice(32, 36)
        bet, la, gg, eg, wr = gsc[0], gsc[1], gsc[2], gsc[3], gsc[4]
        cvs = [tmp([128, N], F32, "gcv%d" % i) for i in range(6)]
        for i in range(6):
            K.ts(dve, cvs[i][:, :], XP[i][:, 0:N], CW[l][:, i, 0:1], MUL)
            for j in range(1, 4):
                K.stt(dve, cvs[i][:, :], XP[i][:, j:j + N], CW[l][:, i, j:j + 1], cvs[i][:, :], MUL, ADD)
            K.activation(cvs[i][:, :], cvs[i][:, :], AF.Silu)
            if bi == NB - 1:
                gcst = tmp([128, 6, 3], F32, "gcst")
                K.copy(act, gcst[:, i, :], XP[i][:, N:N + 3])
                store(O["gc_p"][l].rearrange("j (i p) -> p i j", p=128)[:, i, :], gcst[:, i, :], allow_slow_non_contiguous=True)
            else:
                K.copy(act, XP[i][:, 0:3], XP[i][:, N:N + 3])
        qkT = [tmp([128, N], BF16, "gqk%d" % i) for i in range(4)]
        for i in range(4):
            sq = tmp([128, N], BF16, "gsq")
            K.activation(sq[:, :], cvs[i][:, :], AF.Square)
            K.mm(P[2][:, 0:N], blk64[:, :], sq[:, :])
            rs_ = tmp([128, N], F32, "grs")
            K.activation(rs_[:, :], P[2][:, 0:N], AF.Sqrt, bias=EPS)
            K.recip(rs_[:, :], rs_[:, :])
            K.stt(dve, qkT[i][:, :], cvs[i][:, :], 0.125 if i < 2 else 1.0, rs_[:, :], MUL, MUL)
        K.activation(bet[r4, :], G[0][r4, :], AF.Sigmoid)
        K.activation(la[r4, :], G[1][r4, :], AF.Exp, bias=bc_[r4, 2:3])
        K.activation(la[r4, :], la[r4, :], AF.Ln, bias=1.0)
        K.ts(dve, la[r4, :], la[r4, :], bc_[r4, 3:4], MUL)
        for c in range(NTT):
            cs = slice(c * 128, (c + 1) * 128)
            K.scan(gg[r4, cs], onerow[r4, 0:128], la[r4, cs], 0.0, MUL, ADD)
            K.ts(dve, wr[r4, cs], gg[r4, cs], -1.0, MUL, gg[r4, c * 128 + 127:c * 128 + 128], ADD)
        K.activation(eg[r4, :], gg[r4, :], AF.Exp)
        K.activation(wr[r4, :], wr[r4, :], AF.Exp)
        K.tt(dve, wr[r4, :], wr[r4, :], bet[r4, :], MUL)
        egl = tmp([128, 2, NTT], F32, "gegl")
        for p in range(2):
            K.mm(P[2][:, 256 + p * 8:256 + p * 8 + NTT], sel[:, 2 + p, :], eg[:, 127::128])
        K.copy(act, egl[:, :, :], P[2][:, 256:272].re("p (a c) -> p a c", a=2)[:, :, 0:NTT])
        MFLAG["g"] = True
        while not MFLAG["m"]:
            if getattr(K, "switch_hook", None) is None:
                break
            K.switch_hook()
        PB = P[5][:, :]
        for c in range(NTT):
            cs = slice(c * 128, (c + 1) * 128)
            for qi, srct in enumerate((bet, eg, wr)):
                K.tr(P[4][:, 32 + qi * 4:32 + qi * 4 + 4], srct[r4, cs], ident[32:36, 32:36])
            TT = tmp([128, 16], F32, "gTT"); K.copy(act, TT[:, 0:12], P[4][:, 32:44])
            K.ts(dve, TT[:, 12:16], TT[:, 4:8], -1.0, MUL)
            ktm = tmp([128, 256], BF16, "gktm"); vtm = tmp([128, 256], F32, "gvtm")
            for p in range(2):
                K.tr(PB[:, p * 128:(p + 1) * 128], qkT[2 + p][:, cs], identb[:, :])
                K.tr(P[6][:, p * 128:(p + 1) * 128], cvs[4 + p][:, cs], ident[:, :])
            K.copy(act, ktm[:, :], PB[:, 0:256]); K.copy(act, vtm[:, :], P[6][:, 0:256])
            Tb = tmp([128, 4, 128], BF16, "gTb"); QKb = tmp([128, 4, 128], BF16, "gQKb")
            def par(h, T_, PP):
                rs = slice((h % 2) * 64, (h % 2) * 64 + 64)
                kT_h = qkT[2 + h // 2][rs, cs]; qT_h = qkT[h // 2][rs, cs]
                yield K.mm(PP[:, 0:128], gg[:, cs], sel[:, 4 + h, :], start=True, stop=False)
                yield K.mm(PP[:, 0:128], sel[:, 8 + h, :], gg[:, cs], start=False, stop=False)
                yield K.mm(PP[:, 0:128], ident[:, :], mask[:, 1, :], start=False, stop=True)
                ET = T_("gET"); ETs = T_("gETs")
                yield K.activation(ET, PP[:, 0:128], AF.Exp)
                yield K.tt(dve, ETs, ET, mask[:, 2, :], MUL)
                yield K.mm(PP[:, 128:256], kT_h, kT_h)
                yield K.mm(PP[:, 256:384], kT_h, qT_h)
                At = T_("gAt")
                yield K.stt(dve, At, PP[:, 128:256], TT[:, h:h + 1], ETs, MUL, MUL)
                yield K.stt(dve, QKb[:, h, :], PP[:, 256:384], TT[:, h:h + 1], ET, MUL, MUL)
                yield K.tr(PP[:, 384:512], At, ident[:, :])
                Cm = [T_("gC0"), T_("gC1")]; Bm = [T_("gB0"), T_("gB1")]
                Aoff = T_("gAoff")
                yield K.tt(dve, Cm[0], PP[:, 384:512], mask[:, 3, :], MUL)
                yield K.tt(dve, Aoff, PP[:, 384:512], mask[:, 4, :], MUL)
                yield K.tt(dve, Bm[0], At, mask[:, 3, :], MUL)
                Pt = T_("gPt")
                yield K.tt(dve, Pt, ident[:, :], Bm[0], SUB)
                cur = 0
                for k_ in range(5):
                    nxt = 1 - cur
                    yield K.mm(PP[:, 0:128], Bm[cur], Cm[cur])
                    if k_ < 4:
                        yield K.mm(PP[:, 128:256], Cm[cur], Bm[cur])
                    yield K.copy(act, Cm[nxt], PP[:, 0:128])
                    if k_ < 4:
                        yield K.copy(act, Bm[nxt], PP[:, 128:256])
                    yield K.mm(PP[:, 256:384], Cm[nxt], Pt)
                    yield K.tt(dve, Pt, Pt, PP[:, 256:384], ADD)
                    cur = nxt
                X = T_("gET"); Pn = T_("gETs")
                yield K.mm(PP[:, 0:128], Aoff, Pt)
                yield K.copy(act, X, PP[:, 0:128])
                yield K.tr(PP[:, 128:256], Pt, ident[:, :])
                yield K.copy(act, Pn, PP[:, 128:256])
                yield K.mm(PP[:, 256:384], Pn, X)
                yield K.tt(dve, Tb[:, h, :], Pt, PP[:, 256:384], SUB)

            b8 = tmp([128, 2048], F32, "big8k")
            b4 = [tmp([128, 1024], F32, "big4k%d" % i_) for i_ in range(2)]
            names1 = ["gET", "gETs", "gAt", "gC0", "gC1", "gB0", "gB1", "gAoff", "gPt"]
            names8 = ["gET", "gETs", "gC0", "gC1", "gB0", "gB1", "gAoff", "gPt"]

            def set0(nm):
                return tmp([128, 128], F32, nm)[:, :]

            def set1(nm):
                k_ = names1.index(nm)
                return V(b8, b8.t[:, k_ * 128:(k_ + 1) * 128])

            def mkset(tl_):
                def f(nm):
                    k_ = names8.index("gC1" if nm == "gAt" else nm)
                    return V(tl_, tl_.t[:, k_ * 128:(k_ + 1) * 128])
                return f
            sets = [set0, set1, mkset(b4[0]), mkset(b4[1])]
            PPs = [P[7], P[2], P[6], P[3]]

            def rr(gens_):
                alive = list(gens_)
                while alive:
                    for g_ in list(alive):
                        try:
                            next(g_)
                        except StopIteration:
                            alive.remove(g_)
            rr([par(h_, sets[h_], PPs[h_]) for h_ in range(4)])
            O4 = tmp([128, 4, 64], F32, "gO4")

            def seq(h, T_):
                p = h // 2; rs = slice((h % 2) * 64, (h % 2) * 64 + 64)
                kT_h = qkT[2 + p][rs, cs]; qT_h = qkT[p][rs, cs]
                Sb = st["GSb"][p]; Sf = st["GS"][p]
                PQ = P[6] if h % 2 == 0 else P[3]
                o_ = (h // 2) * 256
                QSe = T_("gET")[:, 0:64]
                bfv = T_("gETs").bitcast(BF16)
                R = bfv[:, 0:64]; Ub = bfv[:, 64:128]; Uw = bfv[:, 128:192]
                yield K.mm(PQ[:, o_:o_ + 64], kT_h, Sb[rs, :])
                yield K.mm(PQ[:, o_ + 64:o_ + 128], qT_h, Sb[rs, :])
                yield K.stt(dve, R, PQ[:, o_:o_ + 64], TT[:, 12 + h:13 + h], vtm[:, h * 64:(h + 1) * 64], MUL, ADD)
                yield K.ts(dve, QSe, PQ[:, o_ + 64:o_ + 128], TT[:, 4 + h:5 + h], MUL)
                yield K.mm(PQ[:, o_ + 128:o_ + 192], Tb[:, h, :], R)
                yield K.copy(act, Ub, PQ[:, o_ + 128:o_ + 192])
                yield K.ts(dve, Uw, PQ[:, o_ + 128:o_ + 192], TT[:, 8 + h:9 + h], MUL)
                yield K.mm(PQ[:, o_ + 192:o_ + 256], QKb[:, h, :], Ub)
                yield K.tt(dve, O4[:, h, :], PQ[:, o_ + 192:o_ + 256], QSe, ADD)
                yield K.mm(P[4][:, 64 + h * 64:64 + (h + 1) * 64], ktm[:, p * 128:(p + 1) * 128], Uw)
                yield K.stt(dve, Sf[rs, :], Sf[rs, :], egl[rs, p, c:c + 1], P[4][rs, 64 + h * 64:64 + (h + 1) * 64], MUL, ADD)
                yield K.copy(act, Sb[rs, :], Sf[rs, :])
            rr([seq(h_, sets[h_]) for h_ in range(4)])
            yield
            sq = tmp([128, 4, 64], F32, "gosq"); ss = tmp([128, 4], F32, "goss")
            K.tt(dve, sq[:, :, :], O4[:, :, :], O4[:, :, :], MUL)
            K.reduce(dve, ss[:, :], sq[:, :, :], ADD)
            K.activation(ss[:, :], ss[:, :], AF.Sqrt, scale=1.0 / 64, bias=EPS)
            K.recip(ss[:, :], ss[:, :])
            K.tt(dve, O4[:, :, :], O4[:, :, :], ss[:, :].un(2).bct([128, 4, 64]), MUL)
            ob = tmp([128, 4, 64], BF16, "gob")
            K.tt(dve, ob[:, :, :], O4[:, :, :], GDNN[l][:, :].un(1).bct([128, 4, 64]), MUL)
            for p in range(2):
                K.tr(PB[:, 512 + p * 128:512 + (p + 1) * 128], ob[:, 2 * p:2 * p + 2, :].re("p h d -> p (h d)"), identb[:, :])
            for p in range(2):
                K.tt(dve, mixT[:, 6 + p, cs], PB[:, 512 + p * 128:512 + (p + 1) * 128], cgT[p][:, cs], MUL)

    tabring = [tabring_pre, K.sb([128, 2, N], F32, "tabr1")]
    tabc = [0]

    def s5_prompt(l, bi):
        s = S5[l]
        BTr = wload(s["BT"][0], s["BT"][1][:, 0], 16, 128); BTi = wload(s["BT"][0], s["BT"][1][:, 1], 16, 128)
        CTr = wload(s["CT"][0], s["CT"][1][:, 0], 16, 128); CTi = wload(s["CT"][0], s["CT"][1][:, 1], 16, 128)
        g0r = tmp([128, 16], F32, "g0r"); g0i = tmp([128, 16], F32, "g0i"); t_ = tmp([128, 16], F32, "g0t")
        K.tt(dve, g0r[:, :], s["ur"][:, :], s["Hre"][:, :], MUL); K.tt(dve, t_[:, :], s["ui"][:, :], s["Him"][:, :], MUL)
        K.tt(dve, g0r[:, :], g0r[:, :], t_[:, :], SUB)
        K.tt(dve, g0i[:, :], s["ui"][:, :], s["Hre"][:, :], MUL); K.tt(dve, t_[:, :], s["ur"][:, :], s["Him"][:, :], MUL)
        K.tt(dve, g0i[:, :], g0i[:, :], t_[:, :], ADD)
        ger = tmp([128, 16], F32, "ger"); gei = tmp([128, 16], F32, "gei")
        gyT = [tmp([128, N], BF16, "gyT%d" % i) for i in range(4)]
        for ot in range(4):
            for ii in range(4):
                i = ot * 4 + ii
                tab = tabring[tabc[0] % 2]; tabc[0] += 1
                K.dma_tile(sp, tab[:, :, :], V(s["TAB"][0], s["TAB"][1][i]))
                co = tab[:, 0, :]; si = tab[:, 1, :]
                K.mm(P[1][:, 0:N], BTr[:, i, :], uT[ot][:, :])
                K.mm(P[1][:, N:2 * N], BTi[:, i, :], uT[ot][:, :])
                a_ = P[1][:, 0:N]; b_ = P[1][:, N:2 * N]
                t1 = tmp([128, N], F32, "s5t1"); t2 = tmp([128, N], F32, "s5t2")
                t3 = tmp([128, N], F32, "s5t3"); t4 = tmp([128, N], F32, "s5t4")
                K.tt(dve, t1[:, :], a_, co, MUL); K.tt(dve, t2[:, :], b_, si, MUL)
                K.tt(pool, t1[:, :], t1[:, :], t2[:, :], ADD)
                K.tt(dve, t3[:, :], b_, co, MUL); K.tt(dve, t4[:, :], a_, si, MUL)
                K.tt(pool, t3[:, :], t3[:, :], t4[:, :], SUB)
                gr = tmp([128, N], F32, "s5gr"); gi_ = tmp([128, N], F32, "s5gi")
                K.scan(gr[:, :], s["mag"][:, i:i + 1].bc([128, N]), t1[:, :], g0r[:, i:i + 1], MUL, ADD)
                K.scan(gi_[:, :], s["mag"][:, i:i + 1].bc([128, N]), t3[:, :], g0i[:, i:i + 1], MUL, ADD)
                K.copy(act, ger[:, i:i + 1], gr[:, N - 1:N]); K.copy(act, gei[:, i:i + 1], gi_[:, N - 1:N])
                hr = tmp([128, N], BF16, "s5hr"); hi = tmp([128, N], BF16, "s5hi")
                K.tt(dve, t1[:, :], gr[:, :], co, MUL); K.tt(dve, t2[:, :], gi_[:, :], si, MUL)
                K.tt(dve, hr[:, :], t1[:, :], t2[:, :], SUB)
                K.tt(pool, t3[:, :], gr[:, :], si, MUL); K.tt(pool, t4[:, :], gi_[:, :], co, MUL)
                K.tt(pool, hi[:, :], t3[:, :], t4[:, :], ADD)
                K.mm(P[0][:, 0:N], CTr[:, i, :], hr[:, :], start=(ii == 0), stop=False)
                K.mm(P[0][:, 0:N], CTi[:, i, :], hi[:, :], start=False, stop=(ii == 3))
                yield
            yv = tmp([128, N], F32, "s5yv")
            K.stt(dve, yv[:, :], uT[ot][:, :], Dcol[l][:, ot:ot + 1], P[0][:, 0:N], MUL, ADD)
            K.activation(gyT[ot][:, :], yv[:, :], AF.Gelu)
        t_ = tmp([128, 16], F32, "g0t")
        K.tt(dve, s["Hre"][:, :], ger[:, :], s["CL"][:, :], MUL); K.tt(dve, t_[:, :], gei[:, :], s["SL"][:, :], MUL)
        K.tt(dve, s["Hre"][:, :], s["Hre"][:, :], t_[:, :], SUB)
        K.tt(dve, s["Him"][:, :], ger[:, :], s["SL"][:, :], MUL); K.tt(dve, t_[:, :], gei[:, :], s["CL"][:, :], MUL)
        K.tt(dve, s["Him"][:, :], s["Him"][:, :], t_[:, :], ADD)
        if bi == NRUN - 1:
            store(O["s5r_p"][l].rearrange("(i g) n -> (g n) i", g=2), s["Hre"][:, :], allow_slow_non_contiguous=True)
            store(O["s5i_p"][l].rearrange("(i g) n -> (g n) i", g=2), s["Him"][:, :], allow_slow_non_contiguous=True)
        glu(l, gyT, N)

    def glu(l, gyT, n):
        def evac(ot, ps):
            sg = tmp([128, N], F32, "glusg")
            K.activation(sg[:, 0:n], ps, AF.Sigmoid)
            K.tt(dve, mixT[:, 2 + ot, 0:n], gyT[ot][:, 0:n], sg[:, 0:n], MUL)
        proj_fm(l, "glu", 0, 4, lambda kt: gyT[kt][:, 0:n], evac, n)

    def resid_proj(l, name, src, nk, n):
        def evac(ot, ps):
            K.tt(dve, xT[ot][:, 0:n], xT[ot][:, 0:n], ps, ADD)
            ss_after(ot, n)
        proj_fm(l, name, 0, 8, lambda kt: src[:, kt, 0:n], evac, n)

    actT = K.sb([128, 22, N], BF16, "actT")
    qT_x = actT[:, 0:8, :]; oT_x = actT[:, 8:16, :]

    def xattn_prompt(l):
        rmsnorm_fm(xT, gxat[l], xn, N)
        proj_fm(l, "mq", 0, 8, lambda kt: xn[:, kt, :], lambda ot, ps: K.copy(act, qT_x[:, ot, :], ps), N)
        b4_ = [tmp([128, 1024], F32, "big4k%d" % i_) for i_ in range(2)]
        b8_ = tmp([128, 2048], F32, "big8k")

        def bfview(tl_, shape_str, **kw):
            return V(tl_, tl_.t[:, 0:512].bitcast(BF16).rearrange(shape_str, **kw))

        def xtile(c, PA0, PA1, PB, pr, pT, on):
            cs = slice(c * 128, (c + 1) * 128)
            mx = tmp([128, 4], F32, "xmx%d" % c); sm = tmp([128, 4], F32, "xsm%d" % c)
            for h in range(4):
                pt = PA0 if h < 2 else PA1
                o = (h % 2) * 256
                yield K.mm(pt[:, o:o + 256], qT_x[:, 2 * h, cs], MKT[l][:, 2 * h, :], start=True, stop=False)
                yield K.mm(pt[:, o:o + 256], qT_x[:, 2 * h + 1, cs], MKT[l][:, 2 * h + 1, :], start=False, stop=True)
            for hp in range(2):
                pt = PA0 if hp == 0 else PA1
                yield K.reduce(dve, mx[:, 2 * hp:2 * hp + 2], pt[:, :].re("p (h m) -> p h m", h=2), MAX)
            yield K.ts(dve, mx[:, :], mx[:, :], -1.0 / 16.0, MUL)
            for h in range(4):
                pt = PA0 if h < 2 else PA1
                o = (h % 2) * 256
                yield K.activation(pr[:, h, :], pt[:, o:o + 256], AF.Exp, bias=mx[:, h:h + 1], scale=1.0 / 16.0, accum=sm[:, h:h + 1])
            yield K.recip(sm[:, :], sm[:, :])
            for h in range(4):
                for mt in range(2):
                    yield K.tr(PB[:, (h * 2 + mt) * 128:(h * 2 + mt + 1) * 128], pr[:, h, mt * 128:(mt + 1) * 128], identb[:, :])
            yield K.copy(act, pT[:, :, :], PB[:, :].re("p (a c) -> p a c", a=8))
            for h in range(4):
                pt = PA0 if h < 2 else PA1
                o = (h % 2) * 256
                for mt in range(2):
                    yield K.mm(pt[:, o:o + 256], pT[:, h * 2 + mt, :], MVB[l][:, mt, h * 256:(h + 1) * 256], start=(mt == 0), stop=(mt == 1))
            for hp in range(2):
                pt = PA0 if hp == 0 else PA1
                yield K.tt(dve, on[:, 2 * hp:2 * hp + 2, :], pt[:, :].re("p (h m) -> p h m", h=2),
                           sm[:, 2 * hp:2 * hp + 2].un(2).bct([128, 2, 256]), MUL)
            for k8 in range(8):
                yield K.tr(PB[:, k8 * 128:(k8 + 1) * 128], on[:, :, :].re("p h m -> p (h m)")[:, k8 * 128:(k8 + 1) * 128], identb[:, :])
            yield K.copy(act, oT_x[:, :, cs], PB[:, :].re("p (a c) -> p a c", a=8))

        t0 = xtile(0, P[6], P[7], P[5][:, :], tmp([128, 4, 256], BF16, "xpr"), tmp([128, 8, 128], BF16, "xpT"), tmp([128, 4, 256], BF16, "xon"))
        if NTT == 2 and int(_os.environ.get('KXILV', 1)):
            t1 = xtile(1, P[3], P[4], V(P[2], P[2].t[:, :].bitcast(BF16)),
                       bfview(b4_[0], "p (h m) -> p h m", h=4), bfview(b4_[1], "p (a c) -> p a c", a=8),
                       bfview(b8_, "p (h m) -> p h m", h=4))
            alive = [t0, t1]
            while alive:
                for g_ in list(alive):
                    try:
                        next(g_)
                    except StopIteration:
                        alive.remove(g_)
        else:
            for _ in t0:
                pass
            for c in range(1, NTT):
                for _ in xtile(c, P[6], P[7], P[5][:, :], tmp([128, 4, 256], BF16, "xpr"), tmp([128, 8, 128], BF16, "xpT"), tmp([128, 4, 256], BF16, "xon")):
                    pass
        resid_proj(l, "mo", oT_x, 8, N)

    def ffn(l, n):
        rmsnorm_fm(xT, gffn[l], xn, n)
        res_g, dst_g, _, _ = WS[l]["gate"]; res_u, dst_u, _, _ = WS[l]["up"]
        for u0 in range(0, 22, 2):
            nct = min(2, 22 - u0)
            wg_ = wload(res_g, dst_g[:, :, u0 * 128:(u0 + nct) * 128], 8, nct * 128)
            wu_ = wload(res_u, dst_u[:, :, u0 * 128:(u0 + nct) * 128], 8, nct * 128)
            for c in range(nct):
                pg = pj()
                for kt in range(8):
                    K.mm(pg[:, 0:n], wg_[:, kt, c * 128:(c + 1) * 128], xn[:, kt, 0:n], start=(kt == 0), stop=(kt == 7))
                pu = pj()
                for kt in range(8):
                    K.mm(pu[:, 0:n], wu_[:, kt, c * 128:(c + 1) * 128], xn[:, kt, 0:n], start=(kt == 0), stop=(kt == 7))
                sg = tmp([128, N], F32, "ffsg")
                K.activation(sg[:, 0:n], pg[:, 0:n], AF.Silu)
                K.tt(dve, actT[:, u0 + c, 0:n], sg[:, 0:n], pu[:, 0:n], MUL)
        resid_proj_k(l, "down", actT, 22, n)

    def resid_proj_k(l, name, src, nk, n):
        res, dst, nk_, tot = WS[l][name]
        for ot in range(8):
            hk = nk // 2
            wa = wload(res, dst[:, 0:hk, ot * 128:(ot + 1) * 128], hk, 128)
            wb = wload(res, dst[:, hk:nk, ot * 128:(ot + 1) * 128], nk - hk, 128)
            ps = pj()
            for kt in range(nk):
                wv_ = wa[:, kt, :] if kt < hk else wb[:, kt - hk, :]
                K.mm(ps[:, 0:n], wv_, src[:, kt, 0:n], start=(kt == 0), stop=(kt == nk - 1))
            K.tt(dve, xT[ot][:, 0:n], xT[ot][:, 0:n], ps[:, 0:n], ADD)
            ss_after(ot, n)

    def final_out(n, dst_rows):
        rmsnorm_fm(xT, gfin, xn, n)

    def final_norm_store(n, dst):
        if SS["ready"]:
            ps = P[2]; SS["ready"] = False
        else:
            ps = pj()
            for kt in range(8):
                sq = tmp([128, N], BF16, "sq%d" % (kt % 2))
                K.activation(sq[:, 0:n], xT[kt][:, 0:n], AF.Square)
                K.mm(ps[:, 0:n], onesb[:, :], sq[:, 0:n], start=(kt == 0), stop=(kt == 7))
        rstd = tmp([128, N], F32, "rstd")
        K.activation(rstd[:, 0:n], ps[:, 0:n], AF.Sqrt, scale=1.0 / D, bias=EPS)
        K.recip(rstd[:, 0:n], rstd[:, 0:n])
        yf = [tmp([128, N], F32, ("gcv%d" % kt) if kt < 6 else ("s5gr" if kt == 6 else "s5gi")) for kt in range(8)]
        for kt in range(8):
            K.stt(dve, yf[kt][:, 0:n], xT[kt][:, 0:n], gfin[:, kt:kt + 1], rstd[:, 0:n], MUL, MUL)
        for c in range((n + 127) // 128):
            w = min(128, n - c * 128)
            stg = tmp([128, 1024], F32, "big4k%d" % (c % 2))
            for half in range(2):
                pt = P[3 + half]
                for k4 in range(4):
                    K.tr(pt[0:w, k4 * 128:(k4 + 1) * 128], yf[half * 4 + k4][:, c * 128:c * 128 + w], ident[:, :])
                K.copy(act, stg[0:w, half * 512:(half + 1) * 512], pt[0:w, :])
            store(dst[c * 128:c * 128 + w, :], stg[0:w, :])

    def load_x(src, n):
        for c in range((n + 127) // 128):
            w = min(128, n - c * 128)
            xin = tmp([128, 1024], F32, "big4k%d" % (c % 2))
            ld(xin[0:w, :], src[c * 128:c * 128 + w, :])
            for half in range(2):
                pt = P[3 + half]
                for k4 in range(4):
                    kt = half * 4 + k4
                    K.tr(pt[:, k4 * 128:k4 * 128 + w], xin[0:w, kt * 128:(kt + 1) * 128], ident[0:w, 0:w])
                for k4 in range(4):
                    K.copy(act, xT[half * 4 + k4][:, c * 128:c * 128 + w], pt[:, k4 * 128:k4 * 128 + w])

    import os
    NRUN = int(os.environ.get('KRUN', NB)); LRUN = int(os.environ.get('KLRUN', LYR)); PH = int(os.environ.get('KPH', 9))
    for bi in range(NRUN):
        load_x(I["xp"][bi * N:(bi + 1) * N, :], N)
        for l in range(LRUN):
            if bi == 0 and l == 1:
                cast_group(1, 0); s5_setup(1); cast_group(1, 1); cast_group(1, 2)
            rmsnorm_fm(xT, gmix[l], xn, N)
            inproj(l, N, True)
            def chainA(l=l, bi=bi):
                if PH >= 1:
                    for _ in mlstm_prompt(l, bi):
                        pass
                MFLAG["m"] = True

            def chainA2(l=l, bi=bi):
                if PH >= 3:
                    for _ in gdn_prompt(l, bi):
                        pass

            def chainB(l=l, bi=bi):
                if PH >= 2:
                    for _ in s5_prompt(l, bi):
                        pass
            if int(_os.environ.get('KILV', 1)):
                MFLAG["m"] = False
                Coop(K, [chainA, chainA2, chainB]).run()
            else:
                chainA(); chainA2(); chainB()
            if PH >= 4: resid_proj(l, "out", mixT, 8, N)
            if PH >= 5: xattn_prompt(l)
            if PH >= 6: ffn(l, N)
        final_norm_store(N, O["y_p"][bi * N:(bi + 1) * N, :])
    for l in range(LYR if NRUN > 0 else 0):
        st = ST[l]
        for h in range(4):
            p = h // 2; rs = slice((h % 2) * 64, (h % 2) * 64 + 64)
            store(O["c_p"][l, h], st["CA"][p][rs, 0:64])
            store(O["n_p"][l, h].rearrange("(d o) -> d o", o=1), st["CA"][p][rs, 64:65])
            store(O["g_p"][l, h], st["GS"][p][rs, :])
        m4 = tmp([128, 1], F32, "m4")
        K.tt(dve, m4[0:4, :], st["car"][0:4, 1:2], st["car"][0:4, 0:1], SUB)
        store(O["m_p"][l].rearrange("(h o) -> h o", o=1), m4[0:4, :])


    ZSD = dscr([NS, 4, 3, 64], F32); GSD = dscr([NS, 4, 4], F32); GQD = dscr([NS, 4, 3, 64], F32)
    HSD = dscr([2, NS, 4, 64], F32); LAMD = dscr([2, 2048], F32); QSD = dscr([NS, 1024], F32); OSD = dscr([NS, 1024], F32)
    zres, gres, gqres, hres, lres, qres, ores = (newres("sres") for _ in range(7))
    actf = V(actT, actT.t[:, :, :].rearrange("p a b -> p (a b)").bitcast(F32))
    n = NS
    BH = 64

    def sample_layer(l):
        st = ST[l]; bc_ = bcol[l]; s5 = S5[l]
        gcvt = [tmp([128, N], F32, "gcv%d" % i_) for i_ in range(6)]

        def q64(ti, qi):
            return V(gcvt[ti], gcvt[ti].t[0:BH, qi * 64:(qi + 1) * 64])
        rmsnorm_fm(xT, gmix[l], xn, n)
        inproj(l, n, False)
        ztok = V(big4k[0], big4k[0].t[0:NS, 0:768])
        res, dst, nk, tot = WS[l]["in_tm"]
        for (c0, w) in ((0, 256), (256, 256), (512, 256)):
            wv = wload(res, dst[:, :, c0:c0 + w], 8, w)
            ps = pj()
            for kt in range(8):
                K.mm(ps[0:NS, 0:w], xn[:, kt, 0:NS], wv[:, kt, :], start=(kt == 0), stop=(kt == 7))
            K.copy(act, ztok[:, c0:c0 + w], ps[0:NS, 0:w])
        for j in range(3):
            K.dma_tile(sp, V(zres, ZSD[:, :, j, :]), ztok[:, j * 256:(j + 1) * 256].re("b (h d) -> b h d", h=4))
        ic, l1, be, la = gsc[0], gsc[1], gsc[2], gsc[3]
        r4 = slice(0, 4); g4 = slice(32, 36); cn = slice(0, n)
        K.activation(ic[r4, cn], G[0][r4, cn], AF.Identity, bias=bc_[r4, 0:1])
        K.activation(l1[r4, cn], G[1][r4, cn], AF.Exp, bias=bc_[r4, 1:2], scale=-1.0)
        K.activation(l1[r4, cn], l1[r4, cn], AF.Ln, bias=1.0)
        K.ts(dve, l1[r4, cn], l1[r4, cn], -1.0, MUL)
        K.activation(be[g4, cn], G[0][g4, cn], AF.Sigmoid)
        K.activation(la[g4, cn], G[1][g4, cn], AF.Exp, bias=bc_[g4, 2:3])
        K.activation(la[g4, cn], la[g4, cn], AF.Ln, bias=1.0)
        K.ts(dve, la[g4, cn], la[g4, cn], bc_[g4, 3:4], MUL)
        for j, (t_, rows) in enumerate(((ic, r4), (l1, r4), (be, g4), (la, g4))):
            K.dma_tile(sp, V(gres, GSD[:, :, j].rearrange("b h -> h b")), t_[rows, cn], allow_slow_non_contiguous=True)
        gts = tmp([BH, 4], F32, "s_gts"); qkv = V(gcvt[0], gcvt[0].t[0:BH, 0:192].rearrange("p (j d) -> p j d", j=3))
        ld(gts[:, :], V(gres, GSD.rearrange("b h j -> (b h) j")))
        ld(qkv[:, :, :], V(zres, ZSD.rearrange("b h j d -> (b h) j d")))
        m0 = tmp([BH, 1], F32, "s_m0"); ld(m0[:, :], I["st_m"][l].rearrange("b (h o) -> (b h) o", o=1))
        def chainA():
            t1 = tmp([BH, 4], F32, "s_t1")
            K.tt(dve, t1[:, 0:1], gts[:, 1:2], m0[:, :], ADD)
            K.tt(dve, t1[:, 1:2], t1[:, 0:1], gts[:, 0:1], MAX)
            K.tt(dve, t1[:, 2:3], gts[:, 0:1], t1[:, 1:2], SUB); K.activation(t1[:, 2:3], t1[:, 2:3], AF.Exp)
            K.tt(dve, t1[:, 3:4], t1[:, 0:1], t1[:, 1:2], SUB); K.activation(t1[:, 3:4], t1[:, 3:4], AF.Exp)
            store(O["m_s"][l].rearrange("b (h o) -> (b h) o", o=1), t1[:, 1:2])
            kt_ = q64(2, 0); qt_ = q64(2, 1); va = tmp([BH, 65], F32, "s_va")
            K.ts(dve, kt_[:, :], qkv[:, 1, :], t1[:, 2:3], MUL)
            K.ts(dve, qt_[:, :], qkv[:, 0, :], 0.125, MUL)
            K.memset(dve, va[:, 64:65], 1.0); K.copy(act, va[:, 0:64], qkv[:, 2, :])
            nacc = tmp([BH, 65], F32, "s_nacc"); red = tmp([BH, 65], F32, "s_red")
            cq = actf[0:BH, 0:1040].re("p (d e) -> p d e", d=16); wq = actf[0:BH, 1040:2080].re("p (d e) -> p d e", d=16)
            cview = I["st_c"][l].rearrange("b h d e -> (b h) d e"); nview = I["st_n"][l].rearrange("b h d -> (b h) d")
            coutv = O["c_s"][l].rearrange("b h d e -> (b h) d e"); noutv = O["n_s"][l].rearrange("b h d -> (b h) d")
            for dq in range(4):
                ds = slice(dq * 16, dq * 16 + 16)
                ld(cq[:, :, 0:64], cview[:, ds, :]); ld(cq[:, :, 64], nview[:, ds], allow_slow_non_contiguous=True)
                K.tt(dve, wq, kt_[:, ds].un(2).bct([BH, 16, 65]), va[:, :].un(1).bct([BH, 16, 65]), MUL)
                K.stt(dve, cq, cq, t1[:, 3:4], wq, MUL, ADD)
                store(coutv[:, ds, :], cq[:, :, 0:64])
                if int(_os.environ.get('KDBG', 0)) != 2: store(noutv[:, ds], cq[:, :, 64], allow_slow_non_contiguous=True)
                K.tt(dve, wq, cq, qt_[:, ds].un(2).bct([BH, 16, 65]), MUL)
                K.reduce(dve, red[:, :] if dq else nacc[:, :], wq.re("p d e -> p e d"), ADD)
                if dq:
                    K.tt(dve, nacc[:, :], nacc[:, :], red[:, :], ADD)
            dn = tmp([BH, 2], F32, "s_dn")
            K.activation(dn[:, 0:1], nacc[:, 64:65], AF.Abs)
            K.activation(dn[:, 1:2], t1[:, 1:2], AF.Exp, scale=-1.0)
            K.tt(dve, dn[:, 0:1], dn[:, 0:1], dn[:, 1:2], MAX); K.recip(dn[:, 0:1], dn[:, 0:1])
            hv = q64(3, 3); hsq = q64(4, 0); hss = tmp([BH, 1], F32, "s_hss")
            K.ts(dve, hv[:, :], nacc[:, 0:64], dn[:, 0:1], MUL)
            K.activation(hsq[:, :], hv[:, :], AF.Square, accum=hss[:, :])
            K.activation(hss[:, :], hss[:, :], AF.Sqrt, scale=1.0 / 64, bias=EPS); K.recip(hss[:, :], hss[:, :])
            K.ts(dve, hv[:, :], hv[:, :], hss[:, 0:1], MUL)
            K.dma_tile(sp, V(hres, HSD[0].rearrange("b h d -> (b h) d")), hv[:, :])
            XP = st["XP"]
            cvq = tmp([128, 6, NS], F32, "s_cvq"); xst = tmp([128, 6, NS], F32, "s_xst")
            cvs = [cvq[:, i_, :] for i_ in range(6)]
            gcv_in = I["st_gc"][l]
            for i in range(6):
                buf = tmp([128, 3, NS], F32, "s_buf%d" % (i % 2))
                for j in range(3):
                    ld(buf[:, j, :], gcv_in[:, j, i * 128:(i + 1) * 128].rearrange("b p -> p b"), allow_slow_non_contiguous=True)
                K.ts(dve, cvs[i], buf[:, 0, :], CW[l][:, i, 0:1], MUL)
                for j in range(1, 3):
                    K.stt(dve, cvs[i], buf[:, j, :], CW[l][:, i, j:j + 1], cvs[i], MUL, ADD)
                K.stt(dve, cvs[i], XP[i][:, 3:3 + n], CW[l][:, i, 3:4], cvs[i], MUL, ADD)
                if int(_os.environ.get('KDBG', 0)):
                    store(O["gc_s"][l][:, 0, i * 128:(i + 1) * 128].rearrange("b p -> p b"), cvs[i], allow_slow_non_contiguous=True)
                    store(O["gc_s"][l][:, 1, i * 128:(i + 1) * 128].rearrange("b p -> p b"), buf[:, 1, :], allow_slow_non_contiguous=True)
                K.activation(cvs[i], cvs[i], AF.Silu)
                K.copy(act, xst[:, i, :], XP[i][:, 3:3 + n])
                store(O["gc_s"][l][:, 2, i * 128:(i + 1) * 128].rearrange("b p -> p b"), xst[:, i, :], allow_slow_non_contiguous=True)
            if not int(_os.environ.get('KDBG', 0)):
                gcr = newres("gccp")
                K.dma_tile(pool, V(gcr, O["gc_s"][l][:, 0:2, :]), I["st_gc"][l][:, 1:3, :])
                outs_tiles.append(gcr)
            for i in range(4):
                sq = tmp([128, N], BF16, "gsq")
                K.activation(sq[:, cn], cvs[i], AF.Square)
                K.mm(P[7][:, 0:n], blk64[:, :], sq[:, cn])
                rs_ = tmp([128, N], F32, "grs")
                K.activation(rs_[:, cn], P[7][:, 0:n], AF.Sqrt, bias=EPS)
                K.recip(rs_[:, cn], rs_[:, cn])
                K.stt(dve, cvs[i], cvs[i], 0.125 if i < 2 else 1.0, rs_[:, cn], MUL, MUL)
            for j in range(3):
                for p in range(2):
                    for hh in range(2):
                        K.dma_tile(sp, V(gqres, GQD[:, 2 * p + hh, j, :].rearrange("b d -> d b")), cvs[2 * j + p][hh * 64:(hh + 1) * 64, :],
                                   allow_slow_non_contiguous=True)
            gq = V(gcvt[1], gcvt[1].t[0:BH, 0:192].rearrange("p (j d) -> p j d", j=3))
            ld(gq[:, :, :], V(gqres, GQD.rearrange("b h j d -> (b h) j d")))
            if int(_os.environ.get('KDBG', 0)) == 2:
                store(O["n_s"][l].rearrange("b h d -> (b h) d"), gq[:, 1, :])
            eg = tmp([BH, 2], F32, "s_eg")
            K.activation(eg[:, 0:1], gts[:, 3:4], AF.Exp); K.ts(dve, eg[:, 1:2], eg[:, 0:1], -1.0, MUL)
            kb = q64(2, 2); K.ts(dve, kb[:, :], gq[:, 1, :], gts[:, 2:3], MUL)
            sview = I["st_g"][l].rearrange("b h d e -> (b h) d e"); soutv = O["g_s"][l].rearrange("b h d e -> (b h) d e")
            sq_ = actf[0:BH, 0:1024].re("p (d e) -> p d e", d=16); wq2 = actf[0:BH, 1040:2064].re("p (d e) -> p d e", d=16)
            ks = q64(2, 3); red2 = q64(3, 0)
            for dq in range(4):
                ds = slice(dq * 16, dq * 16 + 16)
                ld(sq_, sview[:, ds, :])
                K.tt(dve, wq2, sq_, gq[:, 1, ds].un(2).bct([BH, 16, 64]), MUL)
                K.reduce(dve, red2[:, :] if dq else ks[:, :], wq2.re("p d e -> p e d"), ADD)
                if dq:
                    K.tt(dve, ks[:, :], ks[:, :], red2[:, :], ADD)
            U = q64(3, 1)
            K.stt(dve, U[:, :], ks[:, :], eg[:, 1:2], gq[:, 2, :], MUL, ADD)
            ov = q64(3, 2)
            for dq in range(4):
                ds = slice(dq * 16, dq * 16 + 16)
                ld(sq_, sview[:, ds, :])
                K.tt(dve, wq2, kb[:, ds].un(2).bct([BH, 16, 64]), U[:, :].un(1).bct([BH, 16, 64]), MUL)
                K.stt(dve, sq_, sq_, eg[:, 0:1], wq2, MUL, ADD)
                store(soutv[:, ds, :], sq_)
                K.tt(dve, wq2, sq_, gq[:, 0, ds].un(2).bct([BH, 16, 64]), MUL)
                K.reduce(dve, red2[:, :] if dq else ov[:, :], wq2.re("p d e -> p e d"), ADD)
                if dq:
                    K.tt(dve, ov[:, :], ov[:, :], red2[:, :], ADD)
            K.activation(hsq[:, :], ov[:, :], AF.Square, accum=hss[:, :])
            K.activation(hss[:, :], hss[:, :], AF.Sqrt, scale=1.0 / 64, bias=EPS); K.recip(hss[:, :], hss[:, :])
            K.ts(dve, ov[:, :], ov[:, :], hss[:, 0:1], MUL)
            K.tt(dve, ov[:, :], ov[:, :], GDNN[l][0:BH, :], MUL)
            K.dma_tile(sp, V(hres, HSD[1].rearrange("b h d -> (b h) d")), ov[:, :])
            gnc = tmp([128, 2], F32, "s_gnc")
            ld(gnc[:, :], I["mlstm_norm"][l].rearrange("(k p) -> p k", p=128), allow_slow_non_contiguous=True)
            for p in range(2):
                hf = tmp([128, NS], F32, "s_hf")
                ld(hf[:, :], V(hres, HSD[0][:, 2 * p:2 * p + 2, :].rearrange("b h d -> (h d) b")), allow_slow_non_contiguous=True)
                K.stt(dve, mixT[:, p, cn], hf[:, :], gnc[:, p:p + 1], aoT[p][:, cn], MUL, MUL)
                hf2 = tmp([128, NS], F32, "s_hf2")
                ld(hf2[:, :], V(hres, HSD[1][:, 2 * p:2 * p + 2, :].rearrange("b h d -> (h d) b")), allow_slow_non_contiguous=True)
                K.tt(dve, mixT[:, 6 + p, cn], hf2[:, :], cgT[p][:, cn], MUL)

        def chainB():
            BTr = wload(s5["BT"][0], s5["BT"][1][:, 0], 16, 128); BTi = wload(s5["BT"][0], s5["BT"][1][:, 1], 16, 128)
            CTr = wload(s5["CT"][0], s5["CT"][1][:, 0], 16, 128); CTi = wload(s5["CT"][0], s5["CT"][1][:, 1], 16, 128)
            K.dma_tile(sp, V(lres, LAMD[0].rearrange("(i p) -> p i", p=128)), s5["lre"][:, :], allow_slow_non_contiguous=True)
            K.dma_tile(sp, V(lres, LAMD[1].rearrange("(i p) -> p i", p=128)), s5["lim"][:, :], allow_slow_non_contiguous=True)
            hTb = [tmp([128, 16, NS], BF16, "s_hT%d" % r) for r in range(2)]
            for ot in range(4):
                fs = slice(ot * 512, (ot + 1) * 512)
                lr = V(big8k, big8k.t[0:NS, 0:512]); li = V(big8k, big8k.t[0:NS, 512:1024])
                ld(lr, V(lres, LAMD[0][fs].partition_broadcast(NS))); ld(li, V(lres, LAMD[1][fs].partition_broadcast(NS)))
                h0r = V(big8k, big8k.t[0:NS, 1024:1536]); h0i = V(big8k, big8k.t[0:NS, 1536:2048])
                ld(h0r, I["st_s5r"][l].rearrange("b g n -> b (g n)")[:, fs]); ld(h0i, I["st_s5i"][l].rearrange("b g n -> b (g n)")[:, fs])
                K.mm(P[3][0:NS, :], uT[ot][:, cn], BTr[:, 4 * ot:4 * ot + 4, :].re("p a c -> p (a c)"))
                K.mm(P[4][0:NS, :], uT[ot][:, cn], BTi[:, 4 * ot:4 * ot + 4, :].re("p a c -> p (a c)"))
                hr = V(big4k[1], big4k[1].t[0:NS, 0:512]); hi = V(big4k[1], big4k[1].t[0:NS, 512:1024]); tt_ = V(big4k[0], big4k[0].t[0:NS, 0:512])
                K.tt(dve, hr[:, :], lr, h0r, MUL); K.tt(dve, tt_, li, h0i, MUL)
                K.tt(dve, hr[:, :], hr[:, :], tt_, SUB); K.tt(dve, hr[:, :], hr[:, :], P[3][0:NS, :], ADD)
                K.tt(dve, hi[:, :], lr, h0i, MUL); K.tt(dve, tt_, li, h0r, MUL)
                K.tt(dve, hi[:, :], hi[:, :], tt_, ADD); K.tt(dve, hi[:, :], hi[:, :], P[4][0:NS, :], ADD)
                store(O["s5r_s"][l].rearrange("b g n -> b (g n)")[:, fs], hr[:, :])
                store(O["s5i_s"][l].rearrange("b g n -> b (g n)")[:, fs], hi[:, :])
                for r_, src in enumerate((hr, hi)):
                    for ii in range(4):
                        K.tr(P[6][:, (r_ * 4 + ii) * NS:(r_ * 4 + ii + 1) * NS], src[0:NS, ii * 128:(ii + 1) * 128], ident[0:NS, 0:NS])
                for r_ in range(2):
                    K.copy(act, hTb[r_][:, 4 * ot:4 * ot + 4, :], P[6][:, r_ * 4 * NS:(r_ + 1) * 4 * NS].re("p (a c) -> p a c", a=4))
            gyT = [tmp([128, N], BF16, "gyT%d" % i) for i in range(4)]
            for ot in range(4):
                for ii in range(4):
                    i = ot * 4 + ii
                    K.mm(P[4][:, 0:n], CTr[:, i, :], hTb[0][:, i, :], start=(ii == 0), stop=False)
                    K.mm(P[4][:, 0:n], CTi[:, i, :], hTb[1][:, i, :], start=False, stop=(ii == 3))
                yv = tmp([128, N], F32, "s5yv")
                K.stt(dve, yv[:, cn], uT[ot][:, cn], Dcol[l][:, ot:ot + 1], P[4][:, 0:n], MUL, ADD)
                K.activation(gyT[ot][:, cn], yv[:, cn], AF.Gelu)
            glu(l, gyT, n)

        if int(_os.environ.get('KSILV', 1)):
            Coop(K, [chainA, chainB]).run()
        else:
            chainA(); chainB()
        resid_proj(l, "out", mixT, 8, n)
        rmsnorm_fm(xT, gxat[l], xn, n)
        qtok = V(big4k[1], big4k[1].t[0:NS, :])
        res, dst, nk, tot = WS[l]["mq"]
        for ch in range(4):
            wv = wload(res, dst[:, :, ch * 256:(ch + 1) * 256], 8, 256)
            ps = pj()
            for kt in range(8):
                K.mm(ps[0:NS, 0:256], xn[:, kt, 0:NS], wv[:, kt, :], start=(kt == 0), stop=(kt == 7))
            K.copy(act, qtok[:, ch * 256:(ch + 1) * 256], ps[0:NS, 0:256])
        K.dma_tile(sp, V(qres, QSD), qtok)
        rf = [V(ring[i], ring[i].t[:, :].bitcast(F32)) for i in range(2)]
        kvb = [big4k[0], big4k[1], V(big8k, big8k.t[:, 0:1024]), V(big8k, big8k.t[:, 1024:2048])]
        for b in range(NS if int(_os.environ.get('KSATT', 1)) else 0):
            qb = rf[0][:, 0:1024]; prod = rf[1][:, 0:1024]
            ld(qb, V(qres, QSD[b].partition_broadcast(128)))
            sc = tmp([128, 2, 4], F32, "s_sc")
            for mt in range(2):
                ld(kvb[mt][:, :], I["ck"][l, b, mt * 128:(mt + 1) * 128, :])
            for mt in range(2):
                ld(kvb[2 + mt][:, :], I["cv"][l, b, mt * 128:(mt + 1) * 128, :])
            for mt in range(2):
                kt_b = kvb[mt]
                K.tt(dve, prod, kt_b[:, :], qb, MUL)
                K.reduce(dve, sc[:, mt, :], prod.re("p (h d) -> p h d", h=4), ADD)
            for mt in range(2):
                K.tr(P[6][0:4, mt * 128:(mt + 1) * 128], sc[:, mt, :], ident[:, :])
            mx = tmp([4, 2], F32, "s_mx"); pe_ = actf[0:4, 1024:1280]
            K.reduce(dve, mx[:, 0:1], P[6][0:4, 0:256], MAX)
            K.ts(dve, mx[:, 0:1], mx[:, 0:1], -1.0 / 16.0, MUL)
            K.activation(pe_[:, :], P[6][0:4, 0:256], AF.Exp, bias=mx[:, 0:1], scale=1.0 / 16.0, accum=mx[:, 1:2])
            K.recip(mx[:, 1:2], mx[:, 1:2])
            K.ts(dve, pe_[:, :], pe_[:, :], mx[:, 1:2], MUL)
            pT = tmp([128, 2, 4], F32, "s_pT")
            for mt in range(2):
                K.tr(P[7][:, mt * 4:(mt + 1) * 4], pe_[:, mt * 128:(mt + 1) * 128], ident[0:4, 0:4])
            K.copy(act, pT[:, :, :], P[7][:, 0:8].re("p (a c) -> p a c", a=2))
            for h in range(4):
                pt_ = P[3] if h < 2 else P[4]
                for mt in range(2):
                    K.mm(pt_[0:1, (h % 2) * 256:(h % 2 + 1) * 256], pT[:, mt, h:h + 1], kvb[2 + mt][:, h * 256:(h + 1) * 256],
                         start=(mt == 0), stop=(mt == 1))
            orow = actf[0:1, 0:1024]
            K.copy(act, orow[:, 0:512], P[3][0:1, :]); K.copy(act, orow[:, 512:1024], P[4][0:1, :])
            K.dma_tile(sp, V(ores, OSD[b:b + 1, :]), orow)
        otok = V(big4k[0], big4k[0].t[0:NS, :])
        ld(otok, V(ores, OSD))
        for half in range(2):
            for k4 in range(4):
                K.tr(P[6][:, k4 * NS:(k4 + 1) * NS], otok[:, (half * 4 + k4) * 128:(half * 4 + k4 + 1) * 128], ident[0:NS, 0:NS])
            K.copy(act, oT_x[:, half * 4:half * 4 + 4, cn], P[6][:, 0:4 * NS].re("p (a c) -> p a c", a=4))
        resid_proj(l, "mo", oT_x, 8, n)
        ffn(l, n)

    if len(S5) < 2:
        cast_group(1, 0); s5_setup(1); cast_group(1, 1); cast_group(1, 2)
    if int(_os.environ.get('KSAMP', 1)):
        big4k = [tmp([128, 1024], F32, "big4k%d" % i) for i in range(2)]
        big8k = tmp([128, 2048], F32, "big8k")
        load_x(I["xs"], NS)
        for l in range(LYR):
            sample_layer(l)
        final_norm_store(NS, O["y_s"])

    for tl in outs_tiles:
        pool.prog.append(('w', tl.dsem, K.sem_cnt[id(tl.dsem)] * 16))
    K.replay()
    return nc


_NC = [None]


def kernel(**inp):
    if _NC[0] is None:
        _NC[0] = build()
    nc = _NC[0]
    consts = make_consts()
    f = lambda a: np.ascontiguousarray(np.asarray(a, dtype=np.float32))
    wts = {k: f(inp[k]) for k in WSHAPES}
    in_maps = []
    for c in range(8):
        sl = slice(c * NS, (c + 1) * NS)
        m = dict(wts); m.update(consts)
        m["xp"] = f(inp["x_prompt"][c]); m["xs"] = f(inp["x_sample"][sl, 0]); m["mem"] = f(inp["mem_prompt"][c])
        m["ck"] = f(np.asarray(inp["cache_mem_k"])[:, sl].reshape(2, NS, 256, 1024))
        m["cv"] = f(np.asarray(inp["cache_mem_v"])[:, sl].reshape(2, NS, 256, 1024))
        m["st_c"] = f(np.asarray(inp["state_mlstm_c"])[:, sl]); m["st_n"] = f(np.asarray(inp["state_mlstm_n"])[:, sl])
        m["st_m"] = f(np.asarray(inp["state_mlstm_m"])[:, sl]); m["st_s5r"] = f(np.asarray(inp["state_s5_re"])[:, sl])
        m["st_s5i"] = f(np.asarray(inp["state_s5_im"])[:, sl]); m["st_g"] = f(np.asarray(inp["state_gdn"])[:, sl])
        m["st_gc"] = f(np.asarray(inp["state_gdn_conv"])[:, sl])
        in_maps.append(m)
    res = run_bass_kernel_spmd(nc, in_maps, core_ids=list(range(8)))
    R = res.results
    cat = lambda k, ax: np.concatenate([np.asarray(r[k], dtype=np.float32) for r in R], axis=ax)
    stk = lambda k: np.stack([np.asarray(r[k], dtype=np.float32) for r in R], axis=1)
    y_p = np.stack([np.asarray(r["y_p"], dtype=np.float32) for r in R], axis=0)
    y_s = cat("y_s", 0).reshape(128, 1, 1024)
    outs = [y_p, y_s, stk("mk_p").reshape(2, 8, 256, 4, 256), stk("mv_p").reshape(2, 8, 256, 4, 256),
            stk("c_p"), stk("n_p"), stk("m_p"), stk("s5r_p"), stk("s5i_p"), stk("g_p"), stk("gc_p"),
            cat("c_s", 1), cat("n_s", 1), cat("m_s", 1), cat("s5r_s", 1), cat("s5i_s", 1), cat("g_s", 1), cat("gc_s", 1)]
    return tuple(outs)
```
